# Optimizing a Trainium2 kernel written in Bass

```python
import math
import jax, jax.numpy as jnp
from jax import lax
import numpy as np

D_MODEL = 1024
BATCH = 1
SEQ = 16384
DEPTH = 1
DEC_BATCH = 32
DEC_SEQ = 64
PAST_LEN = 4096

CHUNK = 64
QBLOCK = 128
MIX_WIDTH = D_MODEL
FOX_HEADS = 8
FOX_HD = MIX_WIDTH // 2 // FOX_HEADS
FOX_W = FOX_HEADS * FOX_HD
DIFF_HEADS = 4
DIFF_HD = MIX_WIDTH // 2 // (2 * DIFF_HEADS)
DIFF_W = DIFF_HEADS * 2 * DIFF_HD
NUM_BUCKETS = 32
MAX_DISTANCE = 128
EPS = 1e-6
NEG_INF = -1e30
IN_SPLITS = (FOX_W, FOX_W, FOX_W, FOX_HEADS, FOX_W, DIFF_W, DIFF_W, DIFF_W, DIFF_W)
IN_COLS = sum(IN_SPLITS)
SPLIT_POINTS = [int(c) for c in np.cumsum(IN_SPLITS)[:-1]]

kernel_name = "hybrid_fox_diffattn_streaming_step"


def rmsnorm(x, g):
    xf = x.astype(jnp.float32)
    y = xf * lax.rsqrt(jnp.mean(xf * xf, axis=-1, keepdims=True) + EPS)
    return (y * g.astype(jnp.float32)).astype(x.dtype)


def t5_bucket(rel):
    half = NUM_BUCKETS // 2
    max_exact = half // 2
    ret = jnp.where(rel > 0, half, 0)
    n = jnp.abs(rel)
    nf = jnp.maximum(n, 1).astype(jnp.float32)
    large = max_exact + (jnp.log(nf / max_exact) / math.log(MAX_DISTANCE / max_exact)
                         * (half - max_exact)).astype(jnp.int32)
    large = jnp.minimum(large, half - 1)
    return ret + jnp.where(n < max_exact, n, large)


def _to_blocks(a, qb):
    b, sq = a.shape[:2]
    a = a.reshape((b, sq // qb, qb) + a.shape[2:])
    return jnp.moveaxis(a, 1, 0)


def _from_blocks(a):
    a = jnp.moveaxis(a, 0, 1)
    return a.reshape((a.shape[0], a.shape[1] * a.shape[2]) + a.shape[3:])


def fox_attention(q, k, v, cum_q, cum_k, q_pos, k_pos):
    qb = min(QBLOCK, q.shape[1])
    scale = FOX_HD ** -0.5
    cum_kT = jnp.swapaxes(cum_k, 1, 2)

    def block(args):
        qi, ci, pi = args
        s = jnp.einsum('bqhd,bkhd->bhqk', qi, k).astype(jnp.float32) * scale
        s = s + jnp.swapaxes(ci, 1, 2)[..., :, None] - cum_kT[..., None, :]
        s = jnp.where(k_pos[None, :] <= pi[:, None], s, NEG_INF)
        p = jax.nn.softmax(s, axis=-1).astype(v.dtype)
        return jnp.einsum('bhqk,bkhd->bqhd', p, v)

    out = lax.map(block, (_to_blocks(q, qb), _to_blocks(cum_q, qb), q_pos.reshape(-1, qb)))
    return _from_blocks(out)


def diff_attention(q1, q2, k1, k2, v, lam, rel_bias, q_pos, k_pos):
    qb = min(QBLOCK, q1.shape[1])
    scale = DIFF_HD ** -0.5
    k_chunk = k_pos // CHUNK

    def block(args):
        q1i, q2i, pi = args
        bias = jnp.moveaxis(rel_bias[t5_bucket(k_pos[None, :] - pi[:, None])], -1, 0).astype(jnp.float32)
        mask = k_chunk[None, :] <= (pi // CHUNK)[:, None]

        def probs(qi, ki):
            s = jnp.einsum('bqhd,bkhd->bhqk', qi, ki).astype(jnp.float32) * scale + bias
            return jax.nn.softmax(jnp.where(mask, s, NEG_INF), axis=-1)

        p = probs(q1i, k1) - lam * probs(q2i, k2)
        return jnp.einsum('bhqk,bkhd->bqhd', p.astype(v.dtype), v)

    out = lax.map(block, (_to_blocks(q1, qb), _to_blocks(q2, qb), q_pos.reshape(-1, qb)))
    return _from_blocks(out)


def mixer_layer(x, past, g_pre, w_in, b_f, lq1, lk1, lq2, lk2, subln_g, w_out, g_post, rel_bias, layer_idx):
    b, s, _ = x.shape
    h = rmsnorm(x, g_pre)
    proj = jnp.einsum('bsd,dc->bsc', h, w_in)
    fq, fk, fv, ff, fg, dq, dk, dv, dg = jnp.split(proj, SPLIT_POINTS, axis=-1)
    fq = fq.reshape(b, s, FOX_HEADS, FOX_HD)
    fk = fk.reshape(b, s, FOX_HEADS, FOX_HD)
    fv = fv.reshape(b, s, FOX_HEADS, FOX_HD)
    logf = jax.nn.log_sigmoid(ff.astype(jnp.float32) + b_f.astype(jnp.float32))
    dq = dq.reshape(b, s, DIFF_HEADS, 2 * DIFF_HD)
    dk = dk.reshape(b, s, DIFF_HEADS, 2 * DIFF_HD)
    dv = dv.reshape(b, s, DIFF_HEADS, 2 * DIFF_HD)
    new_rows = (fk, fv, logf.astype(x.dtype), dk, dv)

    if past is None:
        p_len = 0
        fk_all, fv_all, logf_all, dk_all, dv_all = fk, fv, logf, dk, dv
    else:
        pk, pv, plogf, pdk, pdv = past
        p_len = pk.shape[1]
        fk_all = jnp.concatenate([pk.astype(fk.dtype), fk], axis=1)
        fv_all = jnp.concatenate([pv.astype(fv.dtype), fv], axis=1)
        logf_all = jnp.concatenate([plogf.astype(jnp.float32), logf], axis=1)
        dk_all = jnp.concatenate([pdk.astype(dk.dtype), dk], axis=1)
        dv_all = jnp.concatenate([pdv.astype(dv.dtype), dv], axis=1)
    k_pos = jnp.arange(p_len + s, dtype=jnp.int32)
    q_pos = k_pos[p_len:]

    cum = jnp.cumsum(logf_all, axis=1)
    fox_o = fox_attention(fq, fk_all, fv_all, cum[:, p_len:], cum, q_pos, k_pos).reshape(b, s, FOX_W)

    lam_init = 0.8 - 0.6 * math.exp(-0.3 * layer_idx)
    lam = (jnp.exp(jnp.sum(lq1.astype(jnp.float32) * lk1.astype(jnp.float32)))
           - jnp.exp(jnp.sum(lq2.astype(jnp.float32) * lk2.astype(jnp.float32))) + lam_init)
    diff_o = diff_attention(dq[..., :DIFF_HD], dq[..., DIFF_HD:], dk_all[..., :DIFF_HD], dk_all[..., DIFF_HD:],
                            dv_all, lam, rel_bias, q_pos, k_pos)
    diff_o = (rmsnorm(diff_o, subln_g) * (1.0 - lam_init)).reshape(b, s, DIFF_W)

    mixed = jnp.concatenate([fox_o * jax.nn.silu(fg), diff_o * jax.nn.silu(dg)], axis=-1)
    out = jnp.einsum('bsc,cd->bsd', mixed, w_out)
    return x + rmsnorm(out, g_post), new_rows


def setup_inputs(seed: int = 0) -> dict:
    key = jax.random.key(seed)
    ks = jax.random.split(key, 20)
    f32 = jnp.float32
    cache_shape_f = (DEPTH, DEC_BATCH, PAST_LEN, FOX_HEADS, FOX_HD)
    cache_shape_d = (DEPTH, DEC_BATCH, PAST_LEN, DIFF_HEADS, 2 * DIFF_HD)
    return {
        "x_prompt": jax.random.normal(ks[0], (BATCH, SEQ, D_MODEL), f32),
        "x_sample": jax.random.normal(ks[1], (DEC_BATCH, DEC_SEQ, D_MODEL), f32),
        "cache_fox_k": jax.random.normal(ks[2], cache_shape_f, f32),
        "cache_fox_v": jax.random.normal(ks[3], cache_shape_f, f32),
        "cache_fox_logf": jax.nn.log_sigmoid(3.0 + jax.random.normal(ks[4], (DEPTH, DEC_BATCH, PAST_LEN, FOX_HEADS), f32)),
        "cache_diff_k": jax.random.normal(ks[5], cache_shape_d, f32),
        "cache_diff_v": jax.random.normal(ks[6], cache_shape_d, f32),
        "norm_pre_g": 1.0 + 0.02 * jax.random.normal(ks[7], (DEPTH, D_MODEL), f32),
        "w_in": jax.random.normal(ks[8], (DEPTH, D_MODEL, IN_COLS), f32) * D_MODEL ** -0.5,
        "forget_bias": 3.0 + 0.1 * jax.random.normal(ks[9], (DEPTH, FOX_HEADS), f32),
        "lambda_q1": 0.1 * jax.random.normal(ks[10], (DEPTH, DIFF_HD), f32),
        "lambda_k1": 0.1 * jax.random.normal(ks[11], (DEPTH, DIFF_HD), f32),
        "lambda_q2": 0.1 * jax.random.normal(ks[12], (DEPTH, DIFF_HD), f32),
        "lambda_k2": 0.1 * jax.random.normal(ks[13], (DEPTH, DIFF_HD), f32),
        "subln_g": 1.0 + 0.02 * jax.random.normal(ks[14], (DEPTH, 2 * DIFF_HD), f32),
        "w_out": jax.random.normal(ks[15], (DEPTH, MIX_WIDTH, D_MODEL), f32) * MIX_WIDTH ** -0.5,
        "norm_post_g": 1.0 + 0.02 * jax.random.normal(ks[16], (DEPTH, D_MODEL), f32),
        "rel_bias": 0.5 * jax.random.normal(ks[17], (NUM_BUCKETS, DIFF_HEADS), f32),
    }


def reference(x_prompt, x_sample, cache_fox_k, cache_fox_v, cache_fox_logf, cache_diff_k, cache_diff_v,
              norm_pre_g, w_in, forget_bias, lambda_q1, lambda_k1, lambda_q2, lambda_k2, subln_g, w_out,
              norm_post_g, rel_bias):
    y_p = x_prompt
    y_s = x_sample
    rows_p = []
    rows_s = []
    for l in range(DEPTH):
        params = (norm_pre_g[l], w_in[l], forget_bias[l], lambda_q1[l], lambda_k1[l], lambda_q2[l],
                  lambda_k2[l], subln_g[l], w_out[l], norm_post_g[l], rel_bias)
        y_p, new_p = mixer_layer(y_p, None, *params, l)
        past = (cache_fox_k[l], cache_fox_v[l], cache_fox_logf[l], cache_diff_k[l], cache_diff_v[l])
        y_s, new_s = mixer_layer(y_s, past, *params, l)
        rows_p.append(new_p)
        rows_s.append(new_s)
    fox_k_p = jnp.stack([r[0] for r in rows_p], axis=0)
    fox_v_p = jnp.stack([r[1] for r in rows_p], axis=0)
    fox_logf_p = jnp.stack([r[2] for r in rows_p], axis=0)
    diff_k_p = jnp.stack([r[3] for r in rows_p], axis=0)
    diff_v_p = jnp.stack([r[4] for r in rows_p], axis=0)
    fox_k_s = jnp.stack([r[0] for r in rows_s], axis=0)
    fox_v_s = jnp.stack([r[1] for r in rows_s], axis=0)
    fox_logf_s = jnp.stack([r[2] for r in rows_s], axis=0)
    diff_k_s = jnp.stack([r[3] for r in rows_s], axis=0)
    diff_v_s = jnp.stack([r[4] for r in rows_s], axis=0)
    return (y_p, y_s, fox_k_p, fox_v_p, fox_logf_p, diff_k_p, diff_v_p,
            fox_k_s, fox_v_s, fox_logf_s, diff_k_s, diff_v_s)
```

```python
import math
from contextlib import ExitStack

import numpy as np
import concourse.bass as bass
import concourse.mybir as mybir
from concourse.bass_utils import run_bass_kernel_spmd

F32 = mybir.dt.float32
BF16 = mybir.dt.bfloat16
AF = mybir.ActivationFunctionType
ALU = mybir.AluOpType

NCORES = 8
D = 1024
TP = 16384
NSEQ = 32
SQ = 64
PAST = 4096
TS = NSEQ * SQ
T = TP + TS
NBLK = T // 512
NPB = TP // 512
NT = T // 128
NPT = TP // 128
EPS = 1e-6
NEG = -30000.0
LAM_INIT = 0.8 - 0.6 * math.exp(-0.3 * 0)
TC = T // NCORES
CB = 384
NCB = TC // CB

C_ID, C_J, C_U, C_U2, C_SL, C_BL, C_ONE = [i * 128 for i in range(7)]
C_MA = 7 * 128
C_MAP = C_MA + 4 * 512
C_MB = C_MAP + 128
C_MBP = C_MB + 5 * 512
NCONST = C_MBP + 128
GLEN = 1152


class Sched:
    def __init__(self, nc, sems):
        self.nc = nc
        self.eng = {"act": nc.scalar, "dve": nc.vector, "pool": nc.gpsimd, "pe": nc.tensor, "sp": nc.sync}
        self.sem = sems
        self.cnt = {k: 0 for k in sems}
        self.waited = {e: {} for e in self.eng}

    def wait(self, e, deps):
        for d in deps:
            if d is None:
                continue
            sname, val = d
            if self.waited[e].get(sname, 0) >= val:
                continue
            self.eng[e].wait_ge(self.sem[sname], val)
            self.waited[e][sname] = val

    def op(self, e, fn, deps=(), inc=True):
        self.wait(e, deps)
        ins = fn(self.eng[e])
        if inc:
            ins.then_inc(self.sem[e], 1)
            self.cnt[e] += 1
            return (e, self.cnt[e])
        return None

    def dma(self, q, stream, out, in_, deps=(), **kw):
        self.wait(q, deps)
        self.eng[q].dma_start(out=out, in_=in_, **kw).then_inc(self.sem[stream], 16)
        self.cnt[stream] += 16
        return (stream, self.cnt[stream])

    def last(self, name):
        return (name, self.cnt[name]) if self.cnt[name] > 0 else None

    def barrier(self, streams):
        toks = [self.last(k) for k in list(self.eng) + list(streams)]
        for e in self.eng:
            self.wait(e, toks)


def _np_bucket(rel):
    rel = rel.astype(np.int32)
    ret = np.where(rel > 0, 16, 0)
    n = np.abs(rel)
    nf = np.maximum(n, 1).astype(np.float32)
    large = 8 + (np.log(nf / np.float32(8)) / np.float32(math.log(16)) * np.float32(8)).astype(np.int32)
    large = np.minimum(large, 15)
    return ret + np.where(n < 8, n, large)


def make_consts():
    p = np.arange(128)[:, None]
    f = np.arange(128)[None, :]
    c = np.zeros((128, NCONST), np.float32)
    c[:, C_ID:C_ID + 128] = (p == f)
    c[:, C_J:C_J + 128] = (p == 127 - f)
    c[:, C_U:C_U + 128] = (p <= f)
    c[:, C_U2:C_U2 + 128] = (p <= f) & (p // 64 == f // 64)
    c[:, C_SL:C_SL + 128] = (p < f)
    c[:, C_BL:C_BL + 128] = (p // 32 == f // 32) & (p % 32 >= f % 32)
    c[:, C_ONE:C_ONE + 128] = 1.0
    t = np.arange(512)[None, :]
    for r in range(4):
        c[:, C_MA + r * 512:C_MA + (r + 1) * 512] = np.where(128 * r + p <= t, 0.0, NEG)
    c[:, C_MAP:C_MAP + 128] = np.where((p <= f) & (p // 64 == f // 64), 0.0, NEG)
    sl = 127 - p
    for ri, r in enumerate(range(-1, 4)):
        valid = ((128 * r + sl) // 64) <= (t // 64)
        c[:, C_MB + ri * 512:C_MB + (ri + 1) * 512] = np.where(valid, 0.0, NEG)
    c[:, C_MBP:C_MBP + 128] = np.where(sl // 64 == f // 64, 0.0, NEG)
    m = np.arange(GLEN)
    bk = _np_bucket(511 - m)
    oh = np.zeros((32, GLEN), np.float32)
    oh[bk, m] += 1.0
    oh[15, :] -= 1.0
    return c, oh


def build_A():
    nc = bass.Bass("TRN2", target_bir_lowering=False)

    def din(name, shape):
        return nc.dram_tensor(name, shape, F32, kind="ExternalInput").ap()

    def dout(name, shape):
        return nc.dram_tensor(name, shape, F32, kind="ExternalOutput").ap()

    xT = din("xT", [D, T])
    wA = din("wA", [D, 577])
    gpre = din("gpre", [128, 8])
    bfc = din("bfc", [128, 1])
    rbh = din("rbh", [32, 1])
    consts = din("consts", [128, NCONST])
    ohT = din("oh", [32, GLEN])
    cfkT = din("cfkT", [NSEQ, 64, PAST])
    cdkT = din("cdkT", [NSEQ, 64, PAST])
    cfv = din("cfv", [NSEQ, PAST, 64])
    cdv = din("cdv", [NSEQ, PAST, 128])
    clogf = din("clogf", [128, NSEQ * 32])
    gscr = nc.dram_tensor("gscr", [1, GLEN], F32, kind="Internal").ap()

    o_fk = dout("o_fk", [T, 64])
    o_dk = dout("o_dk", [T, 64])
    o_fv = dout("o_fv", [T, 64])
    o_dv = dout("o_dv", [T, 128])
    o_lf = dout("o_lf", [NT, 128])
    o_fo = dout("o_fo", [64, T])
    o_dp = dout("o_dp", [128, T])

    es = ExitStack()
    with es:
        def sb(name, shape, dt):
            return es.enter_context(nc.sbuf_tensor(name, shape, dt))

        sem_names = ["act", "dve", "pool", "pe", "sp", "ldx0", "ldx1", "ldc", "ldm", "ldk0", "ldk1", "ldvb0", "ldvb1", "ldva0", "ldva1", "st", "stf"]
        ALLST = [k for k in sem_names if k not in ("act", "dve", "pool", "pe", "sp")]
        sems = {k: es.enter_context(nc.semaphore("s_" + k)) for k in sem_names}
        S = Sched(nc, sems)
        PS = [es.enter_context(nc.psum_tensor(f"ps{i}", [128, 512], F32)) for i in range(8)]
        SA = [PS[0], PS[1]]
        SB_ = [PS[2], PS[3]]
        OA, OB, LB, MISC = PS[4], PS[5], PS[6], PS[7]

        cst = sb("cst", [128, 7 * 128], F32)
        idb = sb("idb", [128, 128], BF16)
        jb = sb("jb", [128, 128], BF16)
        oneb = sb("oneb", [128, 128], BF16)
        maskA = sb("maskA", [128, 4 * 512 + 128], BF16)
        mbhi = sb("mbhi", [128, 5 * 512 + 128], BF16)
        mblo = sb("mblo", [128, 5 * 512 + 128], BF16)
        KTr = sb("KTr", [128, T // 2], F32)
        KT = KTr[:].bitcast(BF16)
        VA = sb("VA", [128, NT, 65], BF16)
        VBr = sb("VBr", [128, NT * 64], F32)
        VB = VBr[:].bitcast(BF16).rearrange("p (t d) -> p t d", d=128)
        RX = [sb(f"RX{i}", [128, 2048], F32) for i in range(3)]
        QT = [sb(f"QT{i}", [128, 512], BF16) for i in range(2)]
        QTS = sb("QTS", [128, TS], BF16)
        wq = sb("wq", [128, 8, 128], BF16)
        wk = sb("wk", [128, 8, 128], BF16)
        wt = sb("wt", [128, 8, 321], BF16)
        epsT = sb("epsT", [128, 1], F32)
        negbf = sb("negbf", [128, 1], F32)
        LF = sb("LF", [128, NT], F32)
        CUM = sb("CUM", [128, NT], F32)
        GT = sb("GT", [128, NPT + 1], F32)
        WN = sb("WN", [128, 16], F32)
        arg = [sb(f"arg{i}", [128, 128], F32) for i in range(2)]
        xb = [RX[i][:].bitcast(BF16).rearrange("p (k t) -> p k t", k=8) for i in range(2)]
        xsq = RX[2][:].bitcast(BF16).rearrange("p (k t) -> p k t", k=8)
        rbc = sb("rbc", [128, 512], F32)
        rtok = sb("rtok", [128, 4], F32)
        stage = sb("stage", [128, 4, 321], F32)
        tmp4 = sb("tmp4", [128, 4], F32)
        PA = [sb(f"PA{i}", [128, 512], BF16) for i in range(2)]
        PB = [sb(f"PB{i}", [128, 512], BF16) for i in range(2)]
        rl = sb("rl", [128, 512], F32)
        oasb = sb("oasb", [128, 512], F32)
        obsb = sb("obsb", [128, 512], F32)
        outA = sb("outA", [128, 512], F32)
        outB = sb("outB", [128, 512], F32)
        VAf = [RX[i][:].rearrange("p (j d) -> p j d", d=64) for i in range(2)]
        clf = RX[2][:, 0:NSEQ * 32]
        Cc = RX[2][:, NSEQ * 32:2 * NSEQ * 32]
        Wc = sb("Wc", [128, NSEQ * 32], F32)
        totT = sb("totT", [128, 128], F32)

        ident_f = cst[:, C_ID:C_ID + 128]
        U_f = cst[:, C_U:C_U + 128]
        U2_f = cst[:, C_U2:C_U2 + 128]
        BL_f = cst[:, C_BL:C_BL + 128]
        one_f = cst[:, C_ONE:C_ONE + 128]

        wst = VBr[:, 0:8 * 577].rearrange("p (k c) -> p k c", c=577)
        NMB = 5 * 512 + 128
        mbf = KTr[:, 0:NMB]
        mbm = KTr[:, NMB:2 * NMB]
        ohs = KTr[0:32, 2 * NMB:2 * NMB + GLEN]
        gv = KTr[0:1, 2 * NMB + GLEN:2 * NMB + 2 * GLEN]
        rbs = sb("rbs", [32, 1], F32)
        gp = sb("gp", [128, 8], F32)
        bft = sb("bft", [128, 1], F32)
        S.dma("sp", "ldc", cst[:], consts[:, 0:7 * 128])
        S.dma("pool", "ldm", maskA[:], consts[:, C_MA:C_MA + 4 * 512 + 128])
        S.dma("sp", "ldc", wst, wA.rearrange("(k p) c -> p k c", p=128))
        S.dma("sp", "ldc", gp[:], gpre[:, :])
        S.dma("sp", "ldc", bft[:], bfc[:, :])
        S.dma("sp", "ldc", ohs, ohT[:, :])
        S.dma("sp", "ldc", rbs[:], rbh[:, :])
        S.dma("sp", "ldc", mbm[:, 0:5 * 512], consts[:, C_MB:C_MB + 5 * 512])
        S.dma("sp", "ldc", mbm[:, 5 * 512:NMB], consts[:, C_MBP:C_MBP + 128])
        S.op("pool", lambda e: e.memset(epsT[:], EPS))
        S.op("pool", lambda e: e.memset(VA[:, :, 64:65], 1.0))
        S.op("pool", lambda e: e.memset(GT[:, 0:1], 0.0))
        S.barrier(["ldc", "ldm"])
        S.op("pool", lambda e: e.tensor_copy(out=idb[:], in_=cst[:, C_ID:C_ID + 128]))
        S.op("pool", lambda e: e.tensor_copy(out=jb[:], in_=cst[:, C_J:C_J + 128]))
        S.op("pool", lambda e: e.tensor_copy(out=oneb[:], in_=cst[:, C_ONE:C_ONE + 128]))
        S.op("dve", lambda e: e.tensor_scalar(out=negbf[:], in0=bft[:], scalar1=-1.0, scalar2=None, op0=ALU.mult))
        for k in range(8):
            S.op("dve", lambda e: e.tensor_scalar(out=wq[:, k, :], in0=wst[:, k, 0:128], scalar1=gp[:, k:k + 1], scalar2=0.125, op0=ALU.mult, op1=ALU.mult))
            S.op("dve", lambda e: e.tensor_scalar(out=wk[:, k, :], in0=wst[:, k, 128:256], scalar1=gp[:, k:k + 1], scalar2=None, op0=ALU.mult))
            S.op("dve", lambda e: e.tensor_scalar(out=wt[:, k, :], in0=wst[:, k, 256:577], scalar1=gp[:, k:k + 1], scalar2=None, op0=ALU.mult))
        tg = None
        for i0 in range(0, GLEN, 512):
            n = min(512, GLEN - i0)
            tmm = S.op("pe", lambda e: e.matmul(MISC[0:1, 0:n], lhsT=rbs[:, 0:1], rhs=ohs[:, i0:i0 + n], start=True, stop=True), [tg])
            tg = S.op("dve", lambda e: e.tensor_copy(out=gv[0:1, i0:i0 + n], in_=MISC[0:1, 0:n]), [tmm])
        t_gs = S.dma("sp", "stf", gscr[:, :], gv, [tg])
        S.wait("sp", [t_gs])
        for ri in range(5):
            off = 512 - 128 * ri
            S.dma("sp", "ldc", mbf[:, ri * 512:(ri + 1) * 512], bass.AP(gscr.tensor, off, [[1, 128], [1, 512]]))
        th = S.dma("sp", "ldc", mbf[:, 5 * 512:NMB], bass.AP(gscr.tensor, 384, [[1, 128], [1, 128]]))
        t1 = S.op("dve", lambda e: e.tensor_tensor(out=mbf, in0=mbf, in1=mbm, op=ALU.add), [th])
        t2 = S.op("dve", lambda e: e.tensor_copy(out=mbhi[:], in_=mbf), [t1])
        t3 = S.op("dve", lambda e: e.tensor_tensor(out=mblo[:], in0=mbf, in1=mbhi[:], op=ALU.subtract), [t2])
        S.barrier(ALLST)

        xT_v = xT.rearrange("(k p) t -> p k t", p=128)
        state = {"xld": {}, "pe_xb": [None, None], "stage_free": [], "qt_free": [None, None]}

        def issue_xload(b):
            bb = b % 2
            state["xld"][b] = S.dma("pool", f"ldx{bb}", xb[bb][:], xT_v[:, :, b * 512:(b + 1) * 512], [state["pe_xb"][bb]])

        def project(b):
            bb = b % 2
            is_s = b >= NPB
            if b + 1 < NBLK:
                issue_xload(b + 1)
            ld = state["xld"].pop(b)
            tq = S.op("pool", lambda e: e.tensor_tensor(out=xsq[:], in0=xb[bb][:], in1=xb[bb][:], op=ALU.mult), [ld, S.last("pe")])
            for k in range(8):
                tss = S.op("pe", lambda e: e.matmul(SA[0][:], lhsT=oneb[:], rhs=xsq[:, k, :], start=(k == 0), stop=(k == 7)),
                           [tq, S.last("act"), S.last("dve")] if k == 0 else (), inc=(k == 7))
            t_ln = S.op("act", lambda e: e.activation(out=rbc[:], in_=SA[0][:], func=AF.Ln, bias=epsT[:, 0:1], scale=1.0 / D), [tss, S.last("dve"), S.last("pe")])
            t_r = S.op("act", lambda e: e.activation(out=rbc[:], in_=rbc[:], func=AF.Exp, scale=-0.5), [t_ln])
            for k in range(8):
                tqm = S.op("pe", lambda e: e.matmul(SA[1][:], lhsT=wq[:, k, :], rhs=xb[bb][:, k, :], start=(k == 0), stop=(k == 7)), [ld] if k == 0 else (), inc=(k == 7))
            for k in range(8):
                tkm = S.op("pe", lambda e: e.matmul(SB_[0][:], lhsT=wk[:, k, :], rhs=xb[bb][:, k, :], start=(k == 0), stop=(k == 7)), inc=(k == 7))
            qdst = QTS[:, (b - NPB) * 512:(b - NPB + 1) * 512] if is_s else QT[bb][:]
            t_q = S.op("dve", lambda e: e.tensor_tensor(out=qdst, in0=SA[1][:], in1=rbc[:], op=ALU.mult), [tqm, t_r, state["qt_free"][bb]])
            t_k = S.op("dve", lambda e: e.tensor_tensor(out=KT[:, b * 512:(b + 1) * 512], in0=SB_[0][:], in1=rbc[:], op=ALU.mult), [tkm])
            for tt in range(4):
                trt = S.op("pe", lambda e: e.matmul(MISC[:, tt:tt + 1], lhsT=rbc[0:1, tt * 128:(tt + 1) * 128], rhs=one_f[0:1, 0:1], start=True, stop=True),
                           [t_r, S.last("dve")] if tt == 0 else (), inc=(tt == 3))
            t_rt = S.op("dve", lambda e: e.tensor_copy(out=rtok[:], in_=MISC[:, 0:4]), [trt])
            t_ev = t_rt
            for tt in range(4):
                for k in range(8):
                    ttm = S.op("pe", lambda e: e.matmul(SB_[1][:, 0:321], lhsT=xb[bb][:, k, tt * 128:(tt + 1) * 128], rhs=wt[:, k, :], start=(k == 0), stop=(k == 7)),
                               [t_ev] if k == 0 else (), inc=(k == 7))
                t_ev = S.op("dve", lambda e: e.tensor_scalar(out=stage[:, tt, :], in0=SB_[1][:, 0:321], scalar1=rtok[:, tt:tt + 1], scalar2=None, op0=ALU.mult),
                            [ttm, t_rt] + (state["stage_free"] if tt == 0 else []))
            state["pe_xb"][bb] = ttm
            t_e = S.op("act", lambda e: e.activation(out=tmp4[:], in_=stage[:, :, 320], func=AF.Exp, bias=negbf[:, 0:1], scale=-1.0), [t_ev, S.last("dve")])
            t_l = S.op("act", lambda e: e.activation(out=tmp4[:], in_=tmp4[:], func=AF.Ln, bias=1.0, scale=1.0), [t_e])
            t_lf = S.op("dve", lambda e: e.tensor_scalar(out=LF[:, 4 * b:4 * b + 4], in0=tmp4[:], scalar1=-1.0, scalar2=None, op0=ALU.mult), [t_l])
            rows = slice(b * 512, (b + 1) * 512)
            so = []
            so.append(S.dma("sp", "st", o_fk[rows, :].rearrange("(tt p) d -> p tt d", p=128), stage[:, :, 0:64], [t_ev]))
            so.append(S.dma("sp", "st", o_dk[rows, :].rearrange("(tt p) d -> p tt d", p=128), stage[:, :, 64:128]))
            so.append(S.dma("sp", "st", o_fv[rows, :].rearrange("(tt p) d -> p tt d", p=128), stage[:, :, 128:192]))
            so.append(S.dma("sp", "st", o_dv[rows, :].rearrange("(tt p) d -> p tt d", p=128), stage[:, :, 192:320]))
            tcm = S.op("pe", lambda e: e.matmul(MISC[:, 8:12], lhsT=(U2_f if is_s else U_f), rhs=LF[:, 4 * b:4 * b + 4], start=True, stop=True), [t_lf, t_rt], inc=is_s)
            if not is_s:
                tcm = S.op("pe", lambda e: e.matmul(MISC[:, 16:20], lhsT=one_f, rhs=LF[:, 4 * b:4 * b + 4], start=True, stop=True))
                tcu = None
                for tt in range(4):
                    kk = 4 * b + tt
                    S.op("dve", lambda e: e.tensor_tensor(out=CUM[:, kk:kk + 1], in0=MISC[:, 8 + tt:9 + tt], in1=GT[:, kk:kk + 1], op=ALU.add), [tcm, tcu])
                    tcu = S.op("dve", lambda e: e.tensor_tensor(out=GT[:, kk + 1:kk + 2], in0=MISC[:, 16 + tt:17 + tt], in1=GT[:, kk:kk + 1], op=ALU.add))
                state["cum"] = tcu
                tv1 = S.op("pool", lambda e: e.tensor_copy(out=VA[:, 4 * b:4 * b + 4, 0:64], in_=stage[:, :, 128:192]), [t_ev])
            else:
                sbi = b - NPB
                twn = S.op("act", lambda e: e.activation(out=WN[:, 4 * sbi:4 * sbi + 4], in_=MISC[:, 8:12], func=AF.Exp, scale=-1.0), [tcm])
                tv1 = S.op("dve", lambda e: e.tensor_tensor(out=VA[:, 4 * b:4 * b + 4, 0:64], in0=stage[:, :, 128:192],
                                                            in1=WN[:, 4 * sbi:4 * sbi + 4].unsqueeze(2).to_broadcast([128, 4, 64]), op=ALU.mult), [twn, t_ev])
                tv1 = S.op("dve", lambda e: e.tensor_copy(out=VA[:, 4 * b:4 * b + 4, 64:65], in_=WN[:, 4 * sbi:4 * sbi + 4].unsqueeze(2)), [tv1])
                state["misc_free"] = twn
            tv2 = S.op("pool", lambda e: e.tensor_copy(out=VB[:, 4 * b:4 * b + 4, :], in_=stage[:, :, 192:320]), [t_ev, tv1])
            state["stage_free"] = so + [tv1, tv2, t_e]
            state["kv"] = [t_k, tv1, tv2]
            return t_q

        def finalize(ncol, col0, t_last_pe):
            t1 = S.op("dve", lambda e: e.reciprocal(out=rl[64:65, 0:ncol], in_=OA[64:65, 0:ncol]), [t_last_pe, S.last("pe"), S.last("st")])
            t2 = S.op("dve", lambda e: e.tensor_copy(out=oasb[0:64, 0:ncol], in_=OA[0:64, 0:ncol]))
            tm = S.op("pe", lambda e: e.matmul(MISC[0:64, 0:ncol], lhsT=one_f[64:65, 0:64], rhs=rl[64:65, 0:ncol], start=True, stop=True), [t1, S.last("dve"), S.last("act")])
            t3 = S.op("dve", lambda e: e.tensor_tensor(out=outA[0:64, 0:ncol], in0=oasb[0:64, 0:ncol], in1=MISC[0:64, 0:ncol], op=ALU.mult), [tm])
            s1 = S.dma("sp", "st", o_fo[:, col0:col0 + ncol], outA[0:64, 0:ncol], [t3])
            t4 = S.op("dve", lambda e: e.reciprocal(out=rl[0:1, 0:ncol], in_=LB[0:1, 0:ncol]))
            t5 = S.op("dve", lambda e: e.tensor_copy(out=obsb[:, 0:ncol], in_=OB[:, 0:ncol]))
            tm2 = S.op("pe", lambda e: e.matmul(MISC[:, 0:ncol], lhsT=one_f[0:1, 0:128], rhs=rl[0:1, 0:ncol], start=True, stop=True), [t4, t3])
            t6 = S.op("dve", lambda e: e.tensor_tensor(out=outB[:, 0:ncol], in0=obsb[:, 0:ncol], in1=MISC[:, 0:ncol], op=ALU.mult), [tm2])
            s2 = S.dma("sp", "st", o_dp[:, col0:col0 + ncol], outB[:, 0:ncol], [t6])
            return t6

        pst = {"sa_free": [None, None], "sb_free": [None, None], "pa_free": [None, None], "pb_free": [None, None], "acc_free": None}

        def attention(i, t_q):
            qb = i % 2
            nj = 4 * i + 4
            ab = i % 2
            t_arg = S.op("dve", lambda e: e.tensor_scalar(out=arg[ab][:, 0:nj], in0=CUM[:, 0:nj], scalar1=-1.0, scalar2=GT[:, 4 * i + 2:4 * i + 3], op0=ALU.mult, op1=ALU.add),
                         [state["cum"], S.last("act")])
            kv = state["kv"] + [t_q]

            def qk(j):
                u = j % 2
                r = j - 4 * i
                last = r < 0
                S.op("pe", lambda e: e.matmul(SA[u][:], lhsT=KT[0:64, j * 128:(j + 1) * 128], rhs=QT[qb][0:64, :], start=True, stop=last),
                     kv + [pst["sa_free"][u]], inc=False)
                if r >= 0:
                    S.op("pe", lambda e: e.matmul(SA[u][:], lhsT=idb[:], rhs=maskA[:, r * 512:(r + 1) * 512], start=False, stop=True), inc=False)
                lastb = r < -1
                S.op("pe", lambda e: e.matmul(SB_[u][:], lhsT=KT[64:128, j * 128:(j + 1) * 128], rhs=QT[qb][64:128, :], start=True, stop=lastb),
                     [pst["sb_free"][u]], inc=lastb)
                if r >= -1:
                    ri = r + 1
                    S.op("pe", lambda e: e.matmul(SB_[u][:], lhsT=jb[:], rhs=mbhi[:, ri * 512:(ri + 1) * 512], start=False, stop=False), inc=False)
                    S.op("pe", lambda e: e.matmul(SB_[u][:], lhsT=jb[:], rhs=mblo[:, ri * 512:(ri + 1) * 512], start=False, stop=True))
                return S.last("pe")

            def ex(j, tqk):
                u = j % 2
                ta = S.op("act", lambda e: e.activation(out=PA[u][:], in_=SA[u][:], func=AF.Exp, bias=arg[ab][:, j:j + 1], scale=1.0), [tqk, t_arg, pst["pa_free"][u]])
                tb = S.op("act", lambda e: e.activation(out=PB[u][:], in_=SB_[u][:], func=AF.Exp), [pst["pb_free"][u]])
                pst["sa_free"][u] = ta
                pst["sb_free"][u] = tb
                return tb

            def pv(j, tex):
                u = j % 2
                first = j == 0
                lastj = j == nj - 1
                S.op("pe", lambda e: e.matmul(OA[0:65, :], lhsT=VA[:, j, :], rhs=PA[u][:], start=first, stop=lastj), [tex, pst["acc_free"]] if first else [tex], inc=False)
                S.op("pe", lambda e: e.matmul(OB[:, :], lhsT=VB[:, j, :], rhs=PB[u][:], start=first, stop=lastj), inc=False)
                t = S.op("pe", lambda e: e.matmul(LB[0:1, :], lhsT=oneb[:, 0:1], rhs=PB[u][:], start=first, stop=lastj))
                pst["pa_free"][u] = t
                pst["pb_free"][u] = t
                return t

            tqk = {0: qk(0)}
            tl = None
            for j in range(nj):
                if j + 1 < nj:
                    tqk[j + 1] = qk(j + 1)
                tex = ex(j, tqk.pop(j))
                tl = pv(j, tex)
            state["qt_free"][qb] = tl
            pst["acc_free"] = finalize(512, i * 512, tl)

        issue_xload(0)
        for b in range(NPB):
            t_q = project(b)
            attention(b, t_q)
        for b in range(NPB, NBLK):
            project(b)
        S.barrier(ALLST)

        t_cl = S.dma("sp", "ldc", clf[:], clogf[:, :])
        for hh in range(2):
            tmm = S.op("pe", lambda e: e.matmul(SA[hh][:], lhsT=U_f, rhs=clf[:, hh * 512:(hh + 1) * 512], start=True, stop=True), [t_cl])
            S.op("dve", lambda e: e.tensor_copy(out=Cc[:, hh * 512:(hh + 1) * 512], in_=SA[hh][:]), [tmm])
        tprev = None
        for q8 in range(8):
            tmm = S.op("pe", lambda e: e.matmul(SB_[0][:, 0:128], lhsT=clf[:, q8 * 128:(q8 + 1) * 128], rhs=one_f, start=True, stop=True), [tprev])
            tcp = S.op("dve", lambda e: e.tensor_copy(out=totT[:], in_=SB_[0][:, 0:128]), [tmm, tprev])
            tmm2 = S.op("pe", lambda e: e.matmul(SB_[1][:, 0:128], lhsT=totT[:], rhs=BL_f, start=True, stop=True), [tcp])
            tprev = S.op("dve", lambda e: e.tensor_tensor(out=Wc[:, q8 * 128:(q8 + 1) * 128], in0=SB_[1][:, 0:128], in1=Cc[:, q8 * 128:(q8 + 1) * 128], op=ALU.subtract), [tmm2])
        t_wc = S.op("act", lambda e: e.activation(out=Wc[:], in_=Wc[:], func=AF.Exp), [tprev])
        S.barrier(ALLST)

        KTc = [KT[:, 0:PAST], KT[:, PAST:2 * PAST]]
        VBc = [VB[:, 0:32, :], VB[:, 32:64, :]]
        VAc = [VA[:, 0:32, :], VA[:, 32:64, :]]
        cst_free = [None, None]
        ldtok = {}

        def load_seq(n):
            cb = n % 2
            d = [cst_free[cb]]
            a = S.dma("pool", f"ldk{cb}", KTc[cb][0:64, :], cfkT[n, :, :], d)
            a = S.dma("pool", f"ldk{cb}", KTc[cb][64:128, :], cdkT[n, :, :])
            b_ = S.dma("pool", f"ldvb{cb}", VBc[cb], cdv[n, :, :].rearrange("(j p) d -> p j d", p=128))
            c_ = S.dma("sp", f"ldva{cb}", VAf[cb][:], cfv[n, :, :].rearrange("(j p) d -> p j d", p=128), d)
            ldtok[n] = [a, b_, c_]

        load_seq(0)
        sa_free = [None, None]
        sb_free = [None, None]
        pa_free = [None, None]
        acc_free = None
        for m in range(16):
            tt = NPT + m
            u = 0
            S.op("pe", lambda e: e.matmul(SA[u][:, 0:128], lhsT=KT[0:64, tt * 128:(tt + 1) * 128], rhs=QTS[0:64, m * 128:(m + 1) * 128], start=True, stop=False), [sa_free[u], sb_free[u]], inc=False)
            S.op("pe", lambda e: e.matmul(SA[u][:, 0:128], lhsT=idb[:], rhs=maskA[:, 4 * 512:4 * 512 + 128], start=False, stop=True), inc=False)
            S.op("pe", lambda e: e.matmul(SB_[u][:, 0:128], lhsT=KT[64:128, tt * 128:(tt + 1) * 128], rhs=QTS[64:128, m * 128:(m + 1) * 128], start=True, stop=False), inc=False)
            S.op("pe", lambda e: e.matmul(SB_[u][:, 0:128], lhsT=jb[:], rhs=mbhi[:, 5 * 512:5 * 512 + 128], start=False, stop=False), inc=False)
            tqk = S.op("pe", lambda e: e.matmul(SB_[u][:, 0:128], lhsT=jb[:], rhs=mblo[:, 5 * 512:5 * 512 + 128], start=False, stop=True))
            ta = S.op("act", lambda e: e.activation(out=PA[u][:, 0:128], in_=SA[u][:, 0:128], func=AF.Exp), [tqk, pa_free[u]])
            tb = S.op("act", lambda e: e.activation(out=PB[u][:, 0:128], in_=SB_[u][:, 0:128], func=AF.Exp))
            sa_free[u] = tb
            sb_free[u] = tb
            S.op("pe", lambda e: e.matmul(OA[0:65, 0:128], lhsT=VA[:, tt, :], rhs=PA[u][:, 0:128], start=True, stop=False), [tb, acc_free], inc=False)
            S.op("pe", lambda e: e.matmul(OB[:, 0:128], lhsT=VB[:, tt, :], rhs=PB[u][:, 0:128], start=True, stop=False), inc=False)
            tpv = S.op("pe", lambda e: e.matmul(LB[0:1, 0:128], lhsT=oneb[:, 0:1], rhs=PB[u][:, 0:128], start=True, stop=False))
            pa_free[u] = tpv
            for hh in range(2):
                n = 2 * m + hh
                cb = n % 2
                if n + 1 < NSEQ:
                    load_seq(n + 1)
                lda, ldb_, ldc_ = ldtok.pop(n)
                tva = S.op("dve", lambda e: e.tensor_tensor(out=VAc[cb][:, :, 0:64], in0=VAf[cb][:], in1=Wc[:, n * 32:(n + 1) * 32].unsqueeze(2).to_broadcast([128, 32, 64]), op=ALU.mult), [ldc_, t_wc])
                tva = S.op("dve", lambda e: e.tensor_copy(out=VAc[cb][:, :, 64:65], in_=Wc[:, n * 32:(n + 1) * 32].unsqueeze(2)))
                qcols = slice(m * 128 + hh * 64, m * 128 + hh * 64 + 64)
                ocols = slice(hh * 64, hh * 64 + 64)
                for g in range(4):
                    u = (g + 1) % 2
                    for jj in range(8):
                        j = 8 * g + jj
                        S.op("pe", lambda e: e.matmul(SA[u][:, jj * 64:(jj + 1) * 64], lhsT=KTc[cb][0:64, j * 128:(j + 1) * 128], rhs=QTS[0:64, qcols], start=True, stop=True),
                             [lda, sa_free[u], sb_free[u]] if jj == 0 else (), inc=False)
                    for jj in range(8):
                        j = 8 * g + jj
                        lastb = j != 31
                        tqk = S.op("pe", lambda e: e.matmul(SB_[u][:, jj * 64:(jj + 1) * 64], lhsT=KTc[cb][64:128, j * 128:(j + 1) * 128], rhs=QTS[64:128, qcols], start=True, stop=lastb), inc=(jj == 7 and lastb))
                        if j == 31:
                            S.op("pe", lambda e: e.matmul(SB_[u][:, jj * 64:(jj + 1) * 64], lhsT=jb[:], rhs=mbhi[:, 0:64], start=False, stop=False), inc=False)
                            tqk = S.op("pe", lambda e: e.matmul(SB_[u][:, jj * 64:(jj + 1) * 64], lhsT=jb[:], rhs=mblo[:, 0:64], start=False, stop=True))
                    ta = S.op("act", lambda e: e.activation(out=PA[u][:], in_=SA[u][:], func=AF.Exp), [tqk, pa_free[u]])
                    tb = S.op("act", lambda e: e.activation(out=PB[u][:], in_=SB_[u][:], func=AF.Exp))
                    sa_free[u] = tb
                    sb_free[u] = tb
                    for jj in range(8):
                        j = 8 * g + jj
                        lastj = (j == 31)
                        S.op("pe", lambda e: e.matmul(OA[0:65, ocols], lhsT=VAc[cb][:, j, :], rhs=PA[u][:, jj * 64:(jj + 1) * 64], start=False, stop=lastj, skip_group_check=True),
                             [tb, tva, ldb_] if jj == 0 else (), inc=False)
                        S.op("pe", lambda e: e.matmul(OB[:, ocols], lhsT=VBc[cb][:, j, :], rhs=PB[u][:, jj * 64:(jj + 1) * 64], start=False, stop=lastj, skip_group_check=True), inc=False)
                        tpv = S.op("pe", lambda e: e.matmul(LB[0:1, ocols], lhsT=oneb[:, 0:1], rhs=PB[u][:, jj * 64:(jj + 1) * 64], start=False, stop=lastj, skip_group_check=True), inc=(jj == 7))
                    pa_free[u] = tpv
                cst_free[cb] = tpv
            acc_free = finalize(128, TP + m * 128, tpv)

        for c0, n in ((0, 128), (128, 16)):
            tmm = S.op("pe", lambda e: e.matmul(MISC[0:n, 0:128], lhsT=LF[:, c0:c0 + n], rhs=ident_f, start=True, stop=True), [S.last("dve"), S.last("act")])
            tcp = S.op("dve", lambda e: e.tensor_copy(out=rl[0:n, 0:128], in_=MISC[0:n, 0:128]), [tmm, S.last("st")])
            S.dma("sp", "st", o_lf[c0:c0 + n, :], rl[0:n, 0:128], [tcp])
        S.barrier(ALLST)
    return nc


def build_C():
    nc = bass.Bass("TRN2", target_bir_lowering=False)

    def din(name, shape):
        return nc.dram_tensor(name, shape, F32, kind="ExternalInput").ap()

    AT = din("AT", [1536, TC])
    xTc = din("xTc", [D, TC])
    xtok = din("xtok", [TC, D])
    wg = din("wg", [D, 1024])
    wo = din("wo", [1024, D])
    gpre = din("gpre", [128, 8])
    gpost = din("gpost", [128, D])
    sgin = din("sg", [128, 1])
    lamv = din("lamv", [128, 256])
    consts = din("consts", [128, NCONST])
    yc = nc.dram_tensor("yc", [TC, D], F32, kind="ExternalOutput").ap()

    es = ExitStack()
    with es:
        def sb(name, shape, dt):
            return es.enter_context(nc.sbuf_tensor(name, shape, dt))

        sem_names = ["act", "dve", "pool", "pe", "sp", "ldc", "ldx", "lda", "ldt", "st"]
        ALLST = ["ldc", "ldx", "lda", "ldt", "st"]
        sems = {k: es.enter_context(nc.semaphore("s_" + k)) for k in sem_names}
        S = Sched(nc, sems)
        PS = [es.enter_context(nc.psum_tensor(f"ps{i}", [128, 512], F32)) for i in range(6)]

        onef = sb("onef", [128, 128], F32)
        oneb = sb("oneb", [128, 128], BF16)
        wgs = sb("wgs", [128, 8, 1024], F32)
        wgb = sb("wgb", [128, 8, 1024], BF16)
        wob = sb("wob", [128, 8, 1024], BF16)
        gp = sb("gp", [128, 8], F32)
        gpo = sb("gpo", [128, D], F32)
        sgl = sb("sgl", [128, 1], F32)
        lv = sb("lv", [128, 256], F32)
        ltmp = sb("ltmp", [128, 128], F32)
        lsum = sb("lsum", [128, 2], F32)
        neglam = sb("neglam", [128, 1], F32)
        epsT = sb("epsT", [128, 1], F32)
        xb = sb("xb", [128, 8, CB], BF16)
        xsq = sb("xsq", [128, 8, CB], BF16)
        rbc = sb("rbc", [128, CB], F32)
        at = sb("at", [128, 12, CB], F32)
        gsb = sb("gsb", [128, CB], F32)
        esb = sb("esb", [128, CB], F32)
        dsb = sb("dsb", [128, CB], F32)
        dsq = sb("dsq", [128, CB], F32)
        rs = sb("rs", [128, CB], F32)
        mixT = sb("mixT", [128, 8, CB], BF16)
        xt = sb("xt", [128, D], F32)
        sq = sb("sq", [128, D], F32)
        ysb = sb("ysb", [128, D], F32)
        ssq = sb("ssq", [128, 1], F32)
        rstd = sb("rstd", [128, 1], F32)

        S.dma("sp", "ldc", onef[:], consts[:, C_ONE:C_ONE + 128])
        S.dma("pool", "ldx", oneb[:], consts[:, C_ONE:C_ONE + 128])
        S.dma("sp", "ldc", wgs[:], wg.rearrange("(k p) c -> p k c", p=128))
        S.dma("pool", "ldx", wob[:], wo.rearrange("(k p) c -> p k c", p=128))
        S.dma("sp", "ldc", gp[:], gpre[:, :])
        S.dma("sp", "ldc", gpo[:], gpost[:, :])
        S.dma("sp", "ldc", sgl[:], sgin[:, :])
        S.dma("sp", "ldc", lv[:], lamv[:, :])
        S.op("pool", lambda e: e.memset(epsT[:], EPS))
        S.barrier(ALLST)
        for k in range(8):
            S.op("dve", lambda e: e.tensor_scalar(out=wgb[:, k, :], in0=wgs[:, k, :], scalar1=gp[:, k:k + 1], scalar2=None, op0=ALU.mult))
        S.op("dve", lambda e: e.tensor_scalar(out=sgl[:], in0=sgl[:], scalar1=(1.0 - LAM_INIT), scalar2=None, op0=ALU.mult), [S.last("dve")])
        t = S.op("dve", lambda e: e.tensor_tensor(out=ltmp[:, 0:64], in0=lv[:, 0:64], in1=lv[:, 64:128], op=ALU.mult))
        t = S.op("dve", lambda e: e.tensor_tensor(out=ltmp[:, 64:128], in0=lv[:, 128:192], in1=lv[:, 192:256], op=ALU.mult))
        t = S.op("dve", lambda e: e.reduce_sum(out=lsum[:, 0:1], in_=ltmp[:, 0:64], axis=mybir.AxisListType.X), [t])
        t = S.op("dve", lambda e: e.reduce_sum(out=lsum[:, 1:2], in_=ltmp[:, 64:128], axis=mybir.AxisListType.X), [t])
        t = S.op("act", lambda e: e.activation(out=lsum[:], in_=lsum[:], func=AF.Exp), [t])
        t = S.op("dve", lambda e: e.tensor_tensor(out=neglam[:], in0=lsum[:, 1:2], in1=lsum[:, 0:1], op=ALU.subtract), [t])
        t = S.op("dve", lambda e: e.tensor_scalar(out=neglam[:], in0=neglam[:], scalar1=-LAM_INIT, scalar2=None, op0=ALU.add), [t])
        S.barrier(ALLST)

        xT_v = xTc.rearrange("(k p) t -> p k t", p=128)
        AT_v = AT.rearrange("(c p) t -> p c t", p=128)
        for blk in range(NCB):
            cs = slice(blk * CB, (blk + 1) * CB)
            tlx = S.dma("pool", "ldx", xb[:], xT_v[:, :, cs], [S.last("pe"), S.last("pool")])
            tla = S.dma("sp", "lda", at[:], AT_v[:, :, cs], [S.last("dve"), S.last("pool")])
            tq = S.op("pool", lambda e: e.tensor_tensor(out=xsq[:], in0=xb[:], in1=xb[:], op=ALU.mult), [tlx, S.last("pe")])
            for k in range(8):
                tss = S.op("pe", lambda e: e.matmul(PS[0][:, 0:CB], lhsT=oneb[:], rhs=xsq[:, k, :], start=(k == 0), stop=(k == 7)), [tq, S.last("act"), S.last("dve")] if k == 0 else (), inc=(k == 7))
            t = S.op("act", lambda e: e.activation(out=rbc[:], in_=PS[0][:, 0:CB], func=AF.Ln, bias=epsT[:, 0:1], scale=1.0 / D), [tss, S.last("dve")])
            t_r = S.op("act", lambda e: e.activation(out=rbc[:], in_=rbc[:], func=AF.Exp, scale=-0.5), [t])
            for c8 in range(8):
                gps = PS[1 + (c8 % 2)]
                for k in range(8):
                    tg = S.op("pe", lambda e: e.matmul(gps[:, 0:CB], lhsT=wgb[:, k, c8 * 128:(c8 + 1) * 128], rhs=xb[:, k, :], start=(k == 0), stop=(k == 7)),
                              [tlx, S.last("dve")] if k == 0 else (), inc=(k == 7))
                t = S.op("dve", lambda e: e.tensor_tensor(out=gsb[:], in0=gps[:, 0:CB], in1=rbc[:], op=ALU.mult), [tg, t_r, S.last("act"), S.last("dve")])
                t = S.op("act", lambda e: e.activation(out=esb[:], in_=gsb[:], func=AF.Exp, scale=-1.0), [t, S.last("dve")])
                t = S.op("dve", lambda e: e.tensor_scalar(out=esb[:], in0=esb[:], scalar1=1.0, scalar2=None, op0=ALU.add), [t])
                t = S.op("dve", lambda e: e.reciprocal(out=esb[:], in_=esb[:]), [t])
                t = S.op("dve", lambda e: e.tensor_tensor(out=gsb[:], in0=gsb[:], in1=esb[:], op=ALU.mult), [t])
                if c8 < 4:
                    t = S.op("dve", lambda e: e.tensor_tensor(out=mixT[:, c8, :], in0=at[:, c8, :], in1=gsb[:], op=ALU.mult), [t, tla, S.last("pe")])
                else:
                    h = c8 - 4
                    t = S.op("dve", lambda e: e.scalar_tensor_tensor(out=dsb[:], in0=at[:, 4 + 2 * h + 1, :], scalar=neglam[:, 0:1], in1=at[:, 4 + 2 * h, :], op0=ALU.mult, op1=ALU.add), [t, tla, S.last("pe")])
                    t = S.op("dve", lambda e: e.tensor_tensor(out=dsq[:], in0=dsb[:], in1=dsb[:], op=ALU.mult), [t])
                    tm = S.op("pe", lambda e: e.matmul(PS[3][:, 0:CB], lhsT=onef[:], rhs=dsq[:], start=True, stop=True), [t, S.last("act")])
                    t = S.op("act", lambda e: e.activation(out=rs[:], in_=PS[3][:, 0:CB], func=AF.Ln, bias=epsT[:, 0:1], scale=1.0 / 128.0), [tm, S.last("dve")])
                    t = S.op("act", lambda e: e.activation(out=rs[:], in_=rs[:], func=AF.Exp, scale=-0.5), [t])
                    t = S.op("dve", lambda e: e.scalar_tensor_tensor(out=dsb[:], in0=dsb[:], scalar=sgl[:, 0:1], in1=rs[:], op0=ALU.mult, op1=ALU.mult), [t])
                    t = S.op("dve", lambda e: e.tensor_tensor(out=mixT[:, c8, :], in0=dsb[:], in1=gsb[:], op=ALU.mult), [t])
            t_mix = t
            for tt in range(CB // 128):
                r0 = blk * CB + tt * 128
                tlt = S.dma("sp", "ldt", xt[:], xtok[r0:r0 + 128, :], [S.last("pool")])
                for half in range(2):
                    for c8 in range(8):
                        to = S.op("pe", lambda e: e.matmul(PS[4 + half][:], lhsT=mixT[:, c8, tt * 128:(tt + 1) * 128], rhs=wob[:, c8, half * 512:(half + 1) * 512], start=(c8 == 0), stop=(c8 == 7)),
                                  [t_mix, S.last("dve")] if (c8 == 0 and half == 0) else (), inc=(c8 == 7))
                for half in range(2):
                    t = S.op("act", lambda e: e.activation(out=sq[:, half * 512:(half + 1) * 512], in_=PS[4 + half][:], func=AF.Square), [to, S.last("dve")])
                t = S.op("dve", lambda e: e.reduce_sum(out=ssq[:], in_=sq[:], axis=mybir.AxisListType.X), [t])
                t = S.op("act", lambda e: e.activation(out=rstd[:], in_=ssq[:], func=AF.Ln, bias=epsT[:, 0:1], scale=1.0 / D), [t])
                t = S.op("act", lambda e: e.activation(out=rstd[:], in_=rstd[:], func=AF.Exp, scale=-0.5), [t])
                for half in range(2):
                    t = S.op("dve", lambda e: e.scalar_tensor_tensor(out=ysb[:, half * 512:(half + 1) * 512], in0=PS[4 + half][:], scalar=rstd[:, 0:1], in1=gpo[:, half * 512:(half + 1) * 512], op0=ALU.mult, op1=ALU.mult),
                             [t, S.last("st")])
                t = S.op("pool", lambda e: e.tensor_tensor(out=ysb[:], in0=ysb[:], in1=xt[:], op=ALU.add), [t, tlt])
                S.dma("sp", "st", yc[r0:r0 + 128, :], ysb[:], [t])
        S.barrier(ALLST)
    return nc


_CACHE = {}
O_FQ, O_FK, O_FV, O_FF, O_FG, O_DQ, O_DK, O_DV, O_DG = 0, 512, 1024, 1536, 1544, 2056, 2568, 3080, 3592


def _prep_A(inp):
    xp = np.asarray(inp["x_prompt"], np.float32)[0]
    xs = np.asarray(inp["x_sample"], np.float32).reshape(TS, D)
    xT = np.ascontiguousarray(np.concatenate([xp, xs], axis=0).T)
    w = np.asarray(inp["w_in"], np.float32)[0]
    consts, oh = make_consts()
    gpre = np.ascontiguousarray(np.asarray(inp["norm_pre_g"], np.float32)[0].reshape(8, 128).T)
    fb = np.asarray(inp["forget_bias"], np.float32)[0]
    rb = np.asarray(inp["rel_bias"], np.float32)
    cfk = np.asarray(inp["cache_fox_k"], np.float32)[0]
    cfv = np.asarray(inp["cache_fox_v"], np.float32)[0]
    clf = np.asarray(inp["cache_fox_logf"], np.float32)[0]
    cdk = np.asarray(inp["cache_diff_k"], np.float32)[0]
    cdv = np.asarray(inp["cache_diff_v"], np.float32)[0]
    maps = []
    for c in range(NCORES):
        h, part = c // 2, c % 2
        fkc = np.arange(O_FK + 64 * c, O_FK + 64 * c + 64)
        dkc = np.arange(O_DK + 128 * h + 64 * part, O_DK + 128 * h + 64 * part + 64)
        cols = np.concatenate([
            np.arange(O_FQ + 64 * c, O_FQ + 64 * c + 64),
            np.arange(O_DQ + 128 * h + 64 * part, O_DQ + 128 * h + 64 * part + 64),
            fkc, dkc, fkc, dkc,
            np.arange(O_FV + 64 * c, O_FV + 64 * c + 64),
            np.arange(O_DV + 128 * h, O_DV + 128 * h + 128),
            np.arange(O_FF + c, O_FF + c + 1)])
        maps.append({
            "xT": xT,
            "wA": np.ascontiguousarray(w[:, cols]),
            "gpre": gpre,
            "bfc": np.full((128, 1), fb[c], np.float32),
            "rbh": np.ascontiguousarray(rb[:, h:h + 1]),
            "consts": consts,
            "oh": oh,
            "cfkT": np.ascontiguousarray(cfk[:, :, c, :].transpose(0, 2, 1)),
            "cdkT": np.ascontiguousarray(cdk[:, :, h, 64 * part:64 * part + 64].transpose(0, 2, 1)),
            "cfv": np.ascontiguousarray(cfv[:, :, c, :]),
            "cdv": np.ascontiguousarray(cdv[:, :, h, :]),
            "clogf": np.ascontiguousarray(clf[:, :, c].reshape(NSEQ, 32, 128).transpose(2, 0, 1).reshape(128, NSEQ * 32)),
        })
    return maps, xT


def run_A(inp):
    if "A" not in _CACHE:
        _CACHE["A"] = build_A()
    maps, xT = _prep_A(inp)
    res = run_bass_kernel_spmd(_CACHE["A"], maps, core_ids=list(range(NCORES)))
    return res.results, xT


def _tok_cols(c):
    return np.concatenate([np.arange(2048 * c, 2048 * c + 2048), np.arange(TP + 256 * c, TP + 256 * c + 256)])


def run_C(inp, resA, xT):
    if "C" not in _CACHE:
        _CACHE["C"] = build_C()
    w = np.asarray(inp["w_in"], np.float32)[0]
    wg = np.ascontiguousarray(np.concatenate([w[:, O_FG:O_FG + 512], w[:, O_DG:O_DG + 512]], axis=1))
    wo = np.ascontiguousarray(np.asarray(inp["w_out"], np.float32)[0])
    consts, _ = make_consts()
    gpre = np.ascontiguousarray(np.asarray(inp["norm_pre_g"], np.float32)[0].reshape(8, 128).T)
    gpost = np.ascontiguousarray(np.broadcast_to(np.asarray(inp["norm_post_g"], np.float32)[0][None, :], (128, D)))
    sg = np.ascontiguousarray(np.asarray(inp["subln_g"], np.float32)[0].reshape(128, 1))
    lamv = np.concatenate([np.asarray(inp[k], np.float32)[0] for k in ("lambda_q1", "lambda_k1", "lambda_q2", "lambda_k2")])
    lamv = np.ascontiguousarray(np.broadcast_to(lamv[None, :], (128, 256)))
    ATfull = np.concatenate([resA[c]["o_fo"] for c in range(NCORES)] + [resA[c]["o_dp"] for c in range(NCORES)], axis=0)
    maps = []
    for c in range(NCORES):
        tc = _tok_cols(c)
        maps.append({
            "AT": np.ascontiguousarray(ATfull[:, tc]),
            "xTc": np.ascontiguousarray(xT[:, tc]),
            "xtok": np.ascontiguousarray(xT[:, tc].T),
            "wg": wg, "wo": wo, "gpre": gpre, "gpost": gpost, "sg": sg, "lamv": lamv, "consts": consts,
        })
    res = run_bass_kernel_spmd(_CACHE["C"], maps, core_ids=list(range(NCORES)))
    return res.results


def kernel(**inp):
    resA, xT = run_A(inp)
    resC = run_C(inp, resA, xT)
    y_p = np.concatenate([resC[c]["yc"][:2048] for c in range(NCORES)], axis=0).reshape(1, TP, D)
    y_s = np.concatenate([resC[c]["yc"][2048:] for c in range(NCORES)], axis=0).reshape(NSEQ, SQ, D)

    def heads(key, cores, width):
        a = np.stack([resA[c][key] for c in cores], axis=1)
        return a

    fk = heads("o_fk", range(8), 64)
    fv = heads("o_fv", range(8), 64)
    lf = np.stack([resA[c]["o_lf"].reshape(T) for c in range(8)], axis=1)
    dk = heads("o_dk", range(8), 64).reshape(T, 4, 128)
    dv = heads("o_dv", range(0, 8, 2), 128)
    f32 = np.float32
    outs = (y_p.astype(f32), y_s.astype(f32),
            fk[:TP].reshape(1, 1, TP, 8, 64), fv[:TP].reshape(1, 1, TP, 8, 64), lf[:TP].reshape(1, 1, TP, 8),
            dk[:TP].reshape(1, 1, TP, 4, 128), dv[:TP].reshape(1, 1, TP, 4, 128),
            fk[TP:].reshape(1, NSEQ, SQ, 8, 64), fv[TP:].reshape(1, NSEQ, SQ, 8, 64), lf[TP:].reshape(1, NSEQ, SQ, 8),
            dk[TP:].reshape(1, NSEQ, SQ, 4, 128), dv[TP:].reshape(1, NSEQ, SQ, 4, 128))
    return tuple(np.ascontiguousarray(o, dtype=f32) for o in outs)
```

```python
import math
from contextlib import ExitStack

import numpy as np
import concourse.bass as bass
import concourse.mybir as mybir
from concourse.bass_utils import run_bass_kernel_spmd

F32 = mybir.dt.float32
BF16 = mybir.dt.bfloat16
AF = mybir.ActivationFunctionType
ALU = mybir.AluOpType

NCORES = 8
D = 1024
TP = 16384
NSEQ = 32
SQ = 64
PAST = 4096
TS = NSEQ * SQ
T = TP + TS
NBLK = T // 512
NPB = TP // 512
NT = T // 128
NPT = TP // 128
EPS = 1e-6
NEG = -30000.0
LAM_INIT = 0.8 - 0.6 * math.exp(-0.3 * 0)
TC = T // NCORES
CB = 384
NCB = TC // CB

C_ID, C_J, C_U, C_U2, C_SL, C_BL, C_ONE = [i * 128 for i in range(7)]
C_MA = 7 * 128
C_MAP = C_MA + 4 * 512
C_MB = C_MAP + 128
C_MBP = C_MB + 5 * 512
NCONST = C_MBP + 128
GLEN = 1152


class Sched:
    def __init__(self, nc, sems):
        self.nc = nc
        self.eng = {"act": nc.scalar, "dve": nc.vector, "pool": nc.gpsimd, "pe": nc.tensor, "sp": nc.sync}
        self.sem = sems
        self.cnt = {k: 0 for k in sems}
        self.waited = {e: {} for e in self.eng}

    def wait(self, e, deps):
        for d in deps:
            if d is None:
                continue
            sname, val = d
            if self.waited[e].get(sname, 0) >= val:
                continue
            self.eng[e].wait_ge(self.sem[sname], val)
            self.waited[e][sname] = val

    def op(self, e, fn, deps=(), inc=True):
        self.wait(e, deps)
        ins = fn(self.eng[e])
        if inc:
            ins.then_inc(self.sem[e], 1)
            self.cnt[e] += 1
            return (e, self.cnt[e])
        return None

    def dma(self, q, stream, out, in_, deps=(), **kw):
        self.wait(q, deps)
        self.eng[q].dma_start(out=out, in_=in_, **kw).then_inc(self.sem[stream], 16)
        self.cnt[stream] += 16
        return (stream, self.cnt[stream])

    def last(self, name):
        return (name, self.cnt[name]) if self.cnt[name] > 0 else None

    def barrier(self, streams):
        toks = [self.last(k) for k in list(self.eng) + list(streams)]
        for e in self.eng:
            self.wait(e, toks)


def _np_bucket(rel):
    rel = rel.astype(np.int32)
    ret = np.where(rel > 0, 16, 0)
    n = np.abs(rel)
    nf = np.maximum(n, 1).astype(np.float32)
    large = 8 + (np.log(nf / np.float32(8)) / np.float32(math.log(16)) * np.float32(8)).astype(np.int32)
    large = np.minimum(large, 15)
    return ret + np.where(n < 8, n, large)


def make_consts():
    p = np.arange(128)[:, None]
    f = np.arange(128)[None, :]
    c = np.zeros((128, NCONST), np.float32)
    c[:, C_ID:C_ID + 128] = (p == f)
    c[:, C_J:C_J + 128] = (p == 127 - f)
    c[:, C_U:C_U + 128] = (p <= f)
    c[:, C_U2:C_U2 + 128] = (p <= f) & (p // 64 == f // 64)
    c[:, C_SL:C_SL + 128] = (p < f)
    c[:, C_BL:C_BL + 128] = (p // 32 == f // 32) & (p % 32 >= f % 32)
    c[:, C_ONE:C_ONE + 128] = 1.0
    t = np.arange(512)[None, :]
    for r in range(4):
        c[:, C_MA + r * 512:C_MA + (r + 1) * 512] = np.where(128 * r + p <= t, 0.0, NEG)
    c[:, C_MAP:C_MAP + 128] = np.where((p <= f) & (p // 64 == f // 64), 0.0, NEG)
    sl = 127 - p
    for ri, r in enumerate(range(-1, 4)):
        valid = ((128 * r + sl) // 64) <= (t // 64)
        c[:, C_MB + ri * 512:C_MB + (ri + 1) * 512] = np.where(valid, 0.0, NEG)
    c[:, C_MBP:C_MBP + 128] = np.where(sl // 64 == f // 64, 0.0, NEG)
    m = np.arange(GLEN)
    bk = _np_bucket(511 - m)
    oh = np.zeros((32, GLEN), np.float32)
    oh[bk, m] += 1.0
    oh[15, :] -= 1.0
    return c, oh


def build_A():
    nc = bass.Bass("TRN2", target_bir_lowering=False)

    def din(name, shape):
        return nc.dram_tensor(name, shape, F32, kind="ExternalInput").ap()

    def dout(name, shape):
        return nc.dram_tensor(name, shape, F32, kind="ExternalOutput").ap()

    xT = din("xT", [D, T])
    wA = din("wA", [D, 577])
    gpre = din("gpre", [128, 8])
    bfc = din("bfc", [128, 1])
    rbh = din("rbh", [32, 1])
    consts = din("consts", [128, NCONST])
    ohT = din("oh", [32, GLEN])
    cfkT = din("cfkT", [NSEQ, 64, PAST])
    cdkT = din("cdkT", [NSEQ, 64, PAST])
    cfv = din("cfv", [NSEQ, PAST, 64])
    cdv = din("cdv", [NSEQ, PAST, 128])
    clogf = din("clogf", [128, NSEQ * 32])
    gscr = nc.dram_tensor("gscr", [1, GLEN], F32, kind="Internal").ap()

    o_fk = dout("o_fk", [T, 64])
    o_dk = dout("o_dk", [T, 64])
    o_fv = dout("o_fv", [T, 64])
    o_dv = dout("o_dv", [T, 128])
    o_lf = dout("o_lf", [NT, 128])
    o_fo = dout("o_fo", [64, T])
    o_dp = dout("o_dp", [128, T])

    es = ExitStack()
    with es:
        def sb(name, shape, dt):
            return es.enter_context(nc.sbuf_tensor(name, shape, dt))

        sem_names = ["act", "dve", "pool", "pe", "sp", "ldx0", "ldx1", "ldc", "ldm", "ldk0", "ldk1", "ldvb0", "ldvb1", "ldva0", "ldva1", "st", "stf"]
        ALLST = [k for k in sem_names if k not in ("act", "dve", "pool", "pe", "sp")]
        sems = {k: es.enter_context(nc.semaphore("s_" + k)) for k in sem_names}
        S = Sched(nc, sems)
        PS = [es.enter_context(nc.psum_tensor(f"ps{i}", [128, 512], F32)) for i in range(8)]
        SA = [PS[0], PS[1]]
        SB_ = [PS[2], PS[3]]
        OA, OB, LB, MISC = PS[4], PS[5], PS[6], PS[7]

        cst = sb("cst", [128, 7 * 128], F32)
        idb = sb("idb", [128, 128], BF16)
        jb = sb("jb", [128, 128], BF16)
        oneb = sb("oneb", [128, 128], BF16)
        maskA = sb("maskA", [128, 4 * 512 + 128], BF16)
        mbhi = sb("mbhi", [128, 5 * 512 + 128], BF16)
        mblo = sb("mblo", [128, 5 * 512 + 128], BF16)
        KTr = sb("KTr", [128, T // 2], F32)
        KT = KTr[:].bitcast(BF16)
        VA = sb("VA", [128, NT, 65], BF16)
        VBr = sb("VBr", [128, NT * 64], F32)
        VB = VBr[:].bitcast(BF16).rearrange("p (t d) -> p t d", d=128)
        RX = [sb(f"RX{i}", [128, 2048], F32) for i in range(3)]
        QT = [sb(f"QT{i}", [128, 512], BF16) for i in range(2)]
        QTS = sb("QTS", [128, TS], BF16)
        wq = sb("wq", [128, 8, 128], BF16)
        wk = sb("wk", [128, 8, 128], BF16)
        wt = sb("wt", [128, 8, 321], BF16)
        epsT = sb("epsT", [128, 1], F32)
        negbf = sb("negbf", [128, 1], F32)
        LF = sb("LF", [128, NT], F32)
        CUM = sb("CUM", [128, NT], F32)
        GT = sb("GT", [128, NPT + 1], F32)
        WN = sb("WN", [128, 16], F32)
        arg = [sb(f"arg{i}", [128, 128], F32) for i in range(2)]
        xb = [RX[i][:].bitcast(BF16).rearrange("p (k t) -> p k t", k=8) for i in range(2)]
        xsq = RX[2][:].bitcast(BF16).rearrange("p (k t) -> p k t", k=8)
        rbc = sb("rbc", [128, 512], F32)
        rtok = sb("rtok", [128, 4], F32)
        stage = sb("stage", [128, 4, 321], F32)
        tmp4 = sb("tmp4", [128, 4], F32)
        PA = [sb(f"PA{i}", [128, 512], BF16) for i in range(2)]
        PB = [sb(f"PB{i}", [128, 512], BF16) for i in range(2)]
        rl = sb("rl", [128, 512], F32)
        oasb = sb("oasb", [128, 512], F32)
        obsb = sb("obsb", [128, 512], F32)
        outA = sb("outA", [128, 512], F32)
        outB = sb("outB", [128, 512], F32)
        lacc = sb("lacc", [128, 512], F32)
        VAf = [RX[i][:].rearrange("p (j d) -> p j d", d=64) for i in range(2)]
        clf = RX[2][:, 0:NSEQ * 32]
        Cc = RX[2][:, NSEQ * 32:2 * NSEQ * 32]
        Wc = sb("Wc", [128, NSEQ * 32], F32)
        totT = sb("totT", [128, 128], F32)

        ident_f = cst[:, C_ID:C_ID + 128]
        U_f = cst[:, C_U:C_U + 128]
        U2_f = cst[:, C_U2:C_U2 + 128]
        BL_f = cst[:, C_BL:C_BL + 128]
        one_f = cst[:, C_ONE:C_ONE + 128]

        wst = VBr[:, 0:8 * 577].rearrange("p (k c) -> p k c", c=577)
        NMB = 5 * 512 + 128
        mbf = KTr[:, 0:NMB]
        mbm = KTr[:, NMB:2 * NMB]
        ohs = KTr[0:32, 2 * NMB:2 * NMB + GLEN]
        gv = KTr[0:1, 2 * NMB + GLEN:2 * NMB + 2 * GLEN]
        rbs = sb("rbs", [32, 1], F32)
        gp = sb("gp", [128, 8], F32)
        bft = sb("bft", [128, 1], F32)
        S.dma("sp", "ldc", cst[:], consts[:, 0:7 * 128])
        S.dma("pool", "ldm", maskA[:], consts[:, C_MA:C_MA + 4 * 512 + 128])
        S.dma("sp", "ldc", wst, wA.rearrange("(k p) c -> p k c", p=128))
        S.dma("sp", "ldc", gp[:], gpre[:, :])
        S.dma("sp", "ldc", bft[:], bfc[:, :])
        S.dma("sp", "ldc", ohs, ohT[:, :])
        S.dma("sp", "ldc", rbs[:], rbh[:, :])
        S.dma("sp", "ldc", mbm[:, 0:5 * 512], consts[:, C_MB:C_MB + 5 * 512])
        S.dma("sp", "ldc", mbm[:, 5 * 512:NMB], consts[:, C_MBP:C_MBP + 128])
        S.op("pool", lambda e: e.memset(epsT[:], EPS))
        S.op("pool", lambda e: e.memset(VA[:, :, 64:65], 1.0))
        S.op("pool", lambda e: e.memset(GT[:, 0:1], 0.0))
        S.barrier(["ldc", "ldm"])
        S.op("pool", lambda e: e.tensor_copy(out=idb[:], in_=cst[:, C_ID:C_ID + 128]))
        S.op("pool", lambda e: e.tensor_copy(out=jb[:], in_=cst[:, C_J:C_J + 128]))
        S.op("pool", lambda e: e.tensor_copy(out=oneb[:], in_=cst[:, C_ONE:C_ONE + 128]))
        S.op("dve", lambda e: e.tensor_scalar(out=negbf[:], in0=bft[:], scalar1=-1.0, scalar2=None, op0=ALU.mult))
        for k in range(8):
            S.op("dve", lambda e: e.tensor_scalar(out=wq[:, k, :], in0=wst[:, k, 0:128], scalar1=gp[:, k:k + 1], scalar2=0.125, op0=ALU.mult, op1=ALU.mult))
            S.op("dve", lambda e: e.tensor_scalar(out=wk[:, k, :], in0=wst[:, k, 128:256], scalar1=gp[:, k:k + 1], scalar2=None, op0=ALU.mult))
            S.op("dve", lambda e: e.tensor_scalar(out=wt[:, k, :], in0=wst[:, k, 256:577], scalar1=gp[:, k:k + 1], scalar2=None, op0=ALU.mult))
        tg = None
        for i0 in range(0, GLEN, 512):
            n = min(512, GLEN - i0)
            tmm = S.op("pe", lambda e: e.matmul(MISC[0:1, 0:n], lhsT=rbs[:, 0:1], rhs=ohs[:, i0:i0 + n], start=True, stop=True), [tg])
            tg = S.op("dve", lambda e: e.tensor_copy(out=gv[0:1, i0:i0 + n], in_=MISC[0:1, 0:n]), [tmm])
        t_gs = S.dma("sp", "stf", gscr[:, :], gv, [tg])
        S.wait("sp", [t_gs])
        for ri in range(5):
            off = 512 - 128 * ri
            S.dma("sp", "ldc", mbf[:, ri * 512:(ri + 1) * 512], bass.AP(gscr.tensor, off, [[1, 128], [1, 512]]))
        th = S.dma("sp", "ldc", mbf[:, 5 * 512:NMB], bass.AP(gscr.tensor, 384, [[1, 128], [1, 128]]))
        t1 = S.op("dve", lambda e: e.tensor_tensor(out=mbf, in0=mbf, in1=mbm, op=ALU.add), [th])
        t2 = S.op("dve", lambda e: e.tensor_copy(out=mbhi[:], in_=mbf), [t1])
        t3 = S.op("dve", lambda e: e.tensor_tensor(out=mblo[:], in0=mbf, in1=mbhi[:], op=ALU.subtract), [t2])
        S.barrier(ALLST)

        xT_v = xT.rearrange("(k p) t -> p k t", p=128)
        state = {"xld": {}, "pe_xb": [None, None], "stage_free": [], "qt_free": [None, None]}

        def issue_xload(b):
            bb = b % 2
            state["xld"][b] = S.dma("pool", f"ldx{bb}", xb[bb][:], xT_v[:, :, b * 512:(b + 1) * 512], [state["pe_xb"][bb]])

        def project(b):
            bb = b % 2
            is_s = b >= NPB
            if b + 1 < NBLK:
                issue_xload(b + 1)
            ld = state["xld"].pop(b)
            tq = S.op("pool", lambda e: e.tensor_tensor(out=xsq[:], in0=xb[bb][:], in1=xb[bb][:], op=ALU.mult), [ld, S.last("pe")])
            for k in range(8):
                tss = S.op("pe", lambda e: e.matmul(SA[0][:], lhsT=oneb[:], rhs=xsq[:, k, :], start=(k == 0), stop=(k == 7)),
                           [tq, S.last("act"), S.last("dve")] if k == 0 else (), inc=(k == 7))
            t_ln = S.op("act", lambda e: e.activation(out=rbc[:], in_=SA[0][:], func=AF.Ln, bias=epsT[:, 0:1], scale=1.0 / D), [tss, S.last("dve"), S.last("pe")])
            t_r = S.op("act", lambda e: e.activation(out=rbc[:], in_=rbc[:], func=AF.Exp, scale=-0.5), [t_ln])
            for k in range(8):
                tqm = S.op("pe", lambda e: e.matmul(SA[1][:], lhsT=wq[:, k, :], rhs=xb[bb][:, k, :], start=(k == 0), stop=(k == 7)), [ld] if k == 0 else (), inc=(k == 7))
            for k in range(8):
                tkm = S.op("pe", lambda e: e.matmul(SB_[0][:], lhsT=wk[:, k, :], rhs=xb[bb][:, k, :], start=(k == 0), stop=(k == 7)), inc=(k == 7))
            qdst = QTS[:, (b - NPB) * 512:(b - NPB + 1) * 512] if is_s else QT[bb][:]
            t_q = S.op("dve", lambda e: e.tensor_tensor(out=qdst, in0=SA[1][:], in1=rbc[:], op=ALU.mult), [tqm, t_r, state["qt_free"][bb]])
            t_k = S.op("dve", lambda e: e.tensor_tensor(out=KT[:, b * 512:(b + 1) * 512], in0=SB_[0][:], in1=rbc[:], op=ALU.mult), [tkm])
            for tt in range(4):
                trt = S.op("pe", lambda e: e.matmul(MISC[:, tt:tt + 1], lhsT=rbc[0:1, tt * 128:(tt + 1) * 128], rhs=one_f[0:1, 0:1], start=True, stop=True),
                           [t_r, S.last("dve")] if tt == 0 else (), inc=(tt == 3))
            t_rt = S.op("dve", lambda e: e.tensor_copy(out=rtok[:], in_=MISC[:, 0:4]), [trt])
            t_ev = t_rt
            for tt in range(4):
                for k in range(8):
                    ttm = S.op("pe", lambda e: e.matmul(SB_[1][:, 0:321], lhsT=xb[bb][:, k, tt * 128:(tt + 1) * 128], rhs=wt[:, k, :], start=(k == 0), stop=(k == 7)),
                               [t_ev] if k == 0 else (), inc=(k == 7))
                t_ev = S.op("dve", lambda e: e.tensor_scalar(out=stage[:, tt, :], in0=SB_[1][:, 0:321], scalar1=rtok[:, tt:tt + 1], scalar2=None, op0=ALU.mult),
                            [ttm, t_rt] + (state["stage_free"] if tt == 0 else []))
            state["pe_xb"][bb] = ttm
            t_e = S.op("act", lambda e: e.activation(out=tmp4[:], in_=stage[:, :, 320], func=AF.Exp, bias=negbf[:, 0:1], scale=-1.0), [t_ev, S.last("dve")])
            t_l = S.op("act", lambda e: e.activation(out=tmp4[:], in_=tmp4[:], func=AF.Ln, bias=1.0, scale=1.0), [t_e])
            t_lf = S.op("dve", lambda e: e.tensor_scalar(out=LF[:, 4 * b:4 * b + 4], in0=tmp4[:], scalar1=-1.0, scalar2=None, op0=ALU.mult), [t_l])
            rows = slice(b * 512, (b + 1) * 512)
            so = []
            so.append(S.dma("sp", "st", o_fk[rows, :].rearrange("(tt p) d -> p tt d", p=128), stage[:, :, 0:64], [t_ev]))
            so.append(S.dma("sp", "st", o_dk[rows, :].rearrange("(tt p) d -> p tt d", p=128), stage[:, :, 64:128]))
            so.append(S.dma("sp", "st", o_fv[rows, :].rearrange("(tt p) d -> p tt d", p=128), stage[:, :, 128:192]))
            so.append(S.dma("sp", "st", o_dv[rows, :].rearrange("(tt p) d -> p tt d", p=128), stage[:, :, 192:320]))
            tcm = S.op("pe", lambda e: e.matmul(MISC[:, 8:12], lhsT=(U2_f if is_s else U_f), rhs=LF[:, 4 * b:4 * b + 4], start=True, stop=True), [t_lf, t_rt], inc=is_s)
            if not is_s:
                tcm = S.op("pe", lambda e: e.matmul(MISC[:, 16:20], lhsT=one_f, rhs=LF[:, 4 * b:4 * b + 4], start=True, stop=True))
                tcu = None
                for tt in range(4):
                    kk = 4 * b + tt
                    S.op("dve", lambda e: e.tensor_tensor(out=CUM[:, kk:kk + 1], in0=MISC[:, 8 + tt:9 + tt], in1=GT[:, kk:kk + 1], op=ALU.add), [tcm, tcu])
                    tcu = S.op("dve", lambda e: e.tensor_tensor(out=GT[:, kk + 1:kk + 2], in0=MISC[:, 16 + tt:17 + tt], in1=GT[:, kk:kk + 1], op=ALU.add))
                state["cum"] = tcu
                tv1 = S.op("pool", lambda e: e.tensor_copy(out=VA[:, 4 * b:4 * b + 4, 0:64], in_=stage[:, :, 128:192]), [t_ev])
            else:
                sbi = b - NPB
                twn = S.op("act", lambda e: e.activation(out=WN[:, 4 * sbi:4 * sbi + 4], in_=MISC[:, 8:12], func=AF.Exp, scale=-1.0), [tcm])
                tv1 = S.op("dve", lambda e: e.tensor_tensor(out=VA[:, 4 * b:4 * b + 4, 0:64], in0=stage[:, :, 128:192],
                                                            in1=WN[:, 4 * sbi:4 * sbi + 4].unsqueeze(2).to_broadcast([128, 4, 64]), op=ALU.mult), [twn, t_ev])
                tv1 = S.op("dve", lambda e: e.tensor_copy(out=VA[:, 4 * b:4 * b + 4, 64:65], in_=WN[:, 4 * sbi:4 * sbi + 4].unsqueeze(2)), [tv1])
                state["misc_free"] = twn
            tv2 = S.op("pool", lambda e: e.tensor_copy(out=VB[:, 4 * b:4 * b + 4, :], in_=stage[:, :, 192:320]), [t_ev, tv1])
            state["stage_free"] = so + [tv1, tv2, t_e]
            state["kv"] = [t_k, tv1, tv2]
            return t_q

        def finalize(ncol, col0, t_last_pe, use_lacc=False):
            t1 = S.op("dve", lambda e: e.reciprocal(out=rl[64:65, 0:ncol], in_=OA[64:65, 0:ncol]), [t_last_pe, S.last("pe"), S.last("st")])
            t2 = S.op("dve", lambda e: e.tensor_copy(out=oasb[0:64, 0:ncol], in_=OA[0:64, 0:ncol]))
            tm = S.op("pe", lambda e: e.matmul(MISC[0:64, 0:ncol], lhsT=one_f[64:65, 0:64], rhs=rl[64:65, 0:ncol], start=True, stop=True), [t1, S.last("dve"), S.last("act")])
            t3 = S.op("dve", lambda e: e.tensor_tensor(out=outA[0:64, 0:ncol], in0=oasb[0:64, 0:ncol], in1=MISC[0:64, 0:ncol], op=ALU.mult), [tm])
            s1 = S.dma("sp", "st", o_fo[:, col0:col0 + ncol], outA[0:64, 0:ncol], [t3])
            if use_lacc:
                tl_ = S.op("pe", lambda e: e.matmul(LB[0:1, 0:ncol], lhsT=one_f[:, 0:1], rhs=lacc[:, 0:ncol], start=True, stop=True), [pst["lacc_last"]])
                pst["lacc_free"] = tl_
                t4 = S.op("dve", lambda e: e.reciprocal(out=rl[0:1, 0:ncol], in_=LB[0:1, 0:ncol]), [tl_])
            else:
                t4 = S.op("dve", lambda e: e.reciprocal(out=rl[0:1, 0:ncol], in_=LB[0:1, 0:ncol]))
            t5 = S.op("dve", lambda e: e.tensor_copy(out=obsb[:, 0:ncol], in_=OB[:, 0:ncol]))
            tm2 = S.op("pe", lambda e: e.matmul(MISC[:, 0:ncol], lhsT=one_f[0:1, 0:128], rhs=rl[0:1, 0:ncol], start=True, stop=True), [t4, t3])
            t6 = S.op("dve", lambda e: e.tensor_tensor(out=outB[:, 0:ncol], in0=obsb[:, 0:ncol], in1=MISC[:, 0:ncol], op=ALU.mult), [tm2])
            s2 = S.dma("sp", "st", o_dp[:, col0:col0 + ncol], outB[:, 0:ncol], [t6])
            return t6

        pst = {"sa_free": [None, None], "sb_free": [None, None], "pa_free": [None, None], "pb_free": [None, None], "pb_free2": [None, None], "acc_free": None, "lacc_free": None, "lacc_last": None}

        def attention(i, t_q):
            qb = i % 2
            nj = 4 * i + 4
            ab = i % 2
            t_arg = S.op("dve", lambda e: e.tensor_scalar(out=arg[ab][:, 0:nj], in0=CUM[:, 0:nj], scalar1=-1.0, scalar2=GT[:, 4 * i + 2:4 * i + 3], op0=ALU.mult, op1=ALU.add),
                         [state["cum"], S.last("act")])
            kv = state["kv"] + [t_q]

            def qk(j):
                u = j % 2
                r = j - 4 * i
                last = r < 0
                S.op("pe", lambda e: e.matmul(SA[u][:], lhsT=KT[0:64, j * 128:(j + 1) * 128], rhs=QT[qb][0:64, :], start=True, stop=last),
                     kv + [pst["sa_free"][u]], inc=False)
                if r >= 0:
                    S.op("pe", lambda e: e.matmul(SA[u][:], lhsT=idb[:], rhs=maskA[:, r * 512:(r + 1) * 512], start=False, stop=True), inc=False)
                lastb = r < -1
                S.op("pe", lambda e: e.matmul(SB_[u][:], lhsT=KT[64:128, j * 128:(j + 1) * 128], rhs=QT[qb][64:128, :], start=True, stop=lastb),
                     [pst["sb_free"][u]], inc=lastb)
                if r >= -1:
                    ri = r + 1
                    S.op("pe", lambda e: e.matmul(SB_[u][:], lhsT=jb[:], rhs=mbhi[:, ri * 512:(ri + 1) * 512], start=False, stop=False), inc=False)
                    S.op("pe", lambda e: e.matmul(SB_[u][:], lhsT=jb[:], rhs=mblo[:, ri * 512:(ri + 1) * 512], start=False, stop=True))
                return S.last("pe")

            def ex(j, tqk):
                u = j % 2
                ta = S.op("act", lambda e: e.activation(out=PA[u][:], in_=SA[u][:], func=AF.Exp, bias=arg[ab][:, j:j + 1], scale=1.0), [tqk, t_arg, pst["pa_free"][u]])
                tb = S.op("act", lambda e: e.activation(out=PB[u][:], in_=SB_[u][:], func=AF.Exp), [pst["pb_free"][u], pst["pb_free2"][u]])
                pst["sa_free"][u] = ta
                pst["sb_free"][u] = tb
                return tb

            def pv(j, tex):
                u = j % 2
                first = j == 0
                lastj = j == nj - 1
                S.op("pe", lambda e: e.matmul(OA[0:65, :], lhsT=VA[:, j, :], rhs=PA[u][:], start=first, stop=lastj), [tex, pst["acc_free"]] if first else [tex], inc=False)
                t = S.op("pe", lambda e: e.matmul(OB[:, :], lhsT=VB[:, j, :], rhs=PB[u][:], start=first, stop=lastj))
                if first:
                    td = S.op("dve", lambda e: e.tensor_copy(out=lacc[:], in_=PB[u][:]), [tex, pst["acc_free"], pst["lacc_free"]])
                else:
                    td = S.op("dve", lambda e: e.tensor_tensor(out=lacc[:], in0=lacc[:], in1=PB[u][:], op=ALU.add), [tex])
                pst["pa_free"][u] = t
                pst["pb_free"][u] = td
                pst["pb_free2"][u] = t
                pst["lacc_last"] = td
                return t

            tqk = {0: qk(0)}
            tl = None
            for j in range(nj):
                if j + 1 < nj:
                    tqk[j + 1] = qk(j + 1)
                tex = ex(j, tqk.pop(j))
                tl = pv(j, tex)
            state["qt_free"][qb] = tl
            pst["acc_free"] = finalize(512, i * 512, tl, use_lacc=True)

        issue_xload(0)
        for b in range(NPB):
            t_q = project(b)
            attention(b, t_q)
        for b in range(NPB, NBLK):
            project(b)
        S.barrier(ALLST)

        t_cl = S.dma("sp", "ldc", clf[:], clogf[:, :])
        for hh in range(2):
            tmm = S.op("pe", lambda e: e.matmul(SA[hh][:], lhsT=U_f, rhs=clf[:, hh * 512:(hh + 1) * 512], start=True, stop=True), [t_cl])
            S.op("dve", lambda e: e.tensor_copy(out=Cc[:, hh * 512:(hh + 1) * 512], in_=SA[hh][:]), [tmm])
        tprev = None
        for q8 in range(8):
            tmm = S.op("pe", lambda e: e.matmul(SB_[0][:, 0:128], lhsT=clf[:, q8 * 128:(q8 + 1) * 128], rhs=one_f, start=True, stop=True), [tprev])
            tcp = S.op("dve", lambda e: e.tensor_copy(out=totT[:], in_=SB_[0][:, 0:128]), [tmm, tprev])
            tmm2 = S.op("pe", lambda e: e.matmul(SB_[1][:, 0:128], lhsT=totT[:], rhs=BL_f, start=True, stop=True), [tcp])
            tprev = S.op("dve", lambda e: e.tensor_tensor(out=Wc[:, q8 * 128:(q8 + 1) * 128], in0=SB_[1][:, 0:128], in1=Cc[:, q8 * 128:(q8 + 1) * 128], op=ALU.subtract), [tmm2])
        t_wc = S.op("act", lambda e: e.activation(out=Wc[:], in_=Wc[:], func=AF.Exp), [tprev])
        S.barrier(ALLST)

        KTc = [KT[:, 0:PAST], KT[:, PAST:2 * PAST]]
        VBc = [VB[:, 0:32, :], VB[:, 32:64, :]]
        VAc = [VA[:, 0:32, :], VA[:, 32:64, :]]
        cst_free = [None, None]
        ldtok = {}

        def load_seq(n):
            cb = n % 2
            d = [cst_free[cb]]
            a = S.dma("pool", f"ldk{cb}", KTc[cb][0:64, :], cfkT[n, :, :], d)
            a = S.dma("pool", f"ldk{cb}", KTc[cb][64:128, :], cdkT[n, :, :])
            b_ = S.dma("pool", f"ldvb{cb}", VBc[cb], cdv[n, :, :].rearrange("(j p) d -> p j d", p=128))
            c_ = S.dma("sp", f"ldva{cb}", VAf[cb][:], cfv[n, :, :].rearrange("(j p) d -> p j d", p=128), d)
            ldtok[n] = [a, b_, c_]

        load_seq(0)
        sa_free = [None, None]
        sb_free = [None, None]
        pa_free = [None, None]
        acc_free = None
        for m in range(16):
            tt = NPT + m
            u = 0
            S.op("pe", lambda e: e.matmul(SA[u][:, 0:128], lhsT=KT[0:64, tt * 128:(tt + 1) * 128], rhs=QTS[0:64, m * 128:(m + 1) * 128], start=True, stop=False), [sa_free[u], sb_free[u]], inc=False)
            S.op("pe", lambda e: e.matmul(SA[u][:, 0:128], lhsT=idb[:], rhs=maskA[:, 4 * 512:4 * 512 + 128], start=False, stop=True), inc=False)
            S.op("pe", lambda e: e.matmul(SB_[u][:, 0:128], lhsT=KT[64:128, tt * 128:(tt + 1) * 128], rhs=QTS[64:128, m * 128:(m + 1) * 128], start=True, stop=False), inc=False)
            S.op("pe", lambda e: e.matmul(SB_[u][:, 0:128], lhsT=jb[:], rhs=mbhi[:, 5 * 512:5 * 512 + 128], start=False, stop=False), inc=False)
            tqk = S.op("pe", lambda e: e.matmul(SB_[u][:, 0:128], lhsT=jb[:], rhs=mblo[:, 5 * 512:5 * 512 + 128], start=False, stop=True))
            ta = S.op("act", lambda e: e.activation(out=PA[u][:, 0:128], in_=SA[u][:, 0:128], func=AF.Exp), [tqk, pa_free[u]])
            tb = S.op("act", lambda e: e.activation(out=PB[u][:, 0:128], in_=SB_[u][:, 0:128], func=AF.Exp))
            sa_free[u] = tb
            sb_free[u] = tb
            S.op("pe", lambda e: e.matmul(OA[0:65, 0:128], lhsT=VA[:, tt, :], rhs=PA[u][:, 0:128], start=True, stop=False), [tb, acc_free], inc=False)
            S.op("pe", lambda e: e.matmul(OB[:, 0:128], lhsT=VB[:, tt, :], rhs=PB[u][:, 0:128], start=True, stop=False), inc=False)
            tpv = S.op("pe", lambda e: e.matmul(LB[0:1, 0:128], lhsT=oneb[:, 0:1], rhs=PB[u][:, 0:128], start=True, stop=False))
            pa_free[u] = tpv
            for hh in range(2):
                n = 2 * m + hh
                cb = n % 2
                if n + 1 < NSEQ:
                    load_seq(n + 1)
                lda, ldb_, ldc_ = ldtok.pop(n)
                tva = S.op("dve", lambda e: e.tensor_tensor(out=VAc[cb][:, :, 0:64], in0=VAf[cb][:], in1=Wc[:, n * 32:(n + 1) * 32].unsqueeze(2).to_broadcast([128, 32, 64]), op=ALU.mult), [ldc_, t_wc])
                tva = S.op("dve", lambda e: e.tensor_copy(out=VAc[cb][:, :, 64:65], in_=Wc[:, n * 32:(n + 1) * 32].unsqueeze(2)))
                qcols = slice(m * 128 + hh * 64, m * 128 + hh * 64 + 64)
                ocols = slice(hh * 64, hh * 64 + 64)
                for g in range(4):
                    u = (g + 1) % 2
                    for jj in range(8):
                        j = 8 * g + jj
                        S.op("pe", lambda e: e.matmul(SA[u][:, jj * 64:(jj + 1) * 64], lhsT=KTc[cb][0:64, j * 128:(j + 1) * 128], rhs=QTS[0:64, qcols], start=True, stop=True),
                             [lda, sa_free[u], sb_free[u]] if jj == 0 else (), inc=False)
                    for jj in range(8):
                        j = 8 * g + jj
                        lastb = j != 31
                        tqk = S.op("pe", lambda e: e.matmul(SB_[u][:, jj * 64:(jj + 1) * 64], lhsT=KTc[cb][64:128, j * 128:(j + 1) * 128], rhs=QTS[64:128, qcols], start=True, stop=lastb), inc=(jj == 7 and lastb))
                        if j == 31:
                            S.op("pe", lambda e: e.matmul(SB_[u][:, jj * 64:(jj + 1) * 64], lhsT=jb[:], rhs=mbhi[:, 0:64], start=False, stop=False), inc=False)
                            tqk = S.op("pe", lambda e: e.matmul(SB_[u][:, jj * 64:(jj + 1) * 64], lhsT=jb[:], rhs=mblo[:, 0:64], start=False, stop=True))
                    ta = S.op("act", lambda e: e.activation(out=PA[u][:], in_=SA[u][:], func=AF.Exp), [tqk, pa_free[u]])
                    tb = S.op("act", lambda e: e.activation(out=PB[u][:], in_=SB_[u][:], func=AF.Exp))
                    sa_free[u] = tb
                    sb_free[u] = tb
                    for jj in range(8):
                        j = 8 * g + jj
                        lastj = (j == 31)
                        S.op("pe", lambda e: e.matmul(OA[0:65, ocols], lhsT=VAc[cb][:, j, :], rhs=PA[u][:, jj * 64:(jj + 1) * 64], start=False, stop=lastj, skip_group_check=True),
                             [tb, tva, ldb_] if jj == 0 else (), inc=False)
                        S.op("pe", lambda e: e.matmul(OB[:, ocols], lhsT=VBc[cb][:, j, :], rhs=PB[u][:, jj * 64:(jj + 1) * 64], start=False, stop=lastj, skip_group_check=True), inc=False)
                        tpv = S.op("pe", lambda e: e.matmul(LB[0:1, ocols], lhsT=oneb[:, 0:1], rhs=PB[u][:, jj * 64:(jj + 1) * 64], start=False, stop=lastj, skip_group_check=True), inc=(jj == 7))
                    pa_free[u] = tpv
                cst_free[cb] = tpv
            acc_free = finalize(128, TP + m * 128, tpv)

        for c0, n in ((0, 128), (128, 16)):
            tmm = S.op("pe", lambda e: e.matmul(MISC[0:n, 0:128], lhsT=LF[:, c0:c0 + n], rhs=ident_f, start=True, stop=True), [S.last("dve"), S.last("act")])
            tcp = S.op("dve", lambda e: e.tensor_copy(out=rl[0:n, 0:128], in_=MISC[0:n, 0:128]), [tmm, S.last("st")])
            S.dma("sp", "st", o_lf[c0:c0 + n, :], rl[0:n, 0:128], [tcp])
        S.barrier(ALLST)
    return nc


def build_C():
    nc = bass.Bass("TRN2", target_bir_lowering=False)

    def din(name, shape):
        return nc.dram_tensor(name, shape, F32, kind="ExternalInput").ap()

    AT = din("AT", [1536, TC])
    xTc = din("xTc", [D, TC])
    xtok = din("xtok", [TC, D])
    wg = din("wg", [D, 1024])
    wo = din("wo", [1024, D])
    gpre = din("gpre", [128, 8])
    gpost = din("gpost", [128, D])
    sgin = din("sg", [128, 1])
    lamv = din("lamv", [128, 256])
    consts = din("consts", [128, NCONST])
    yc = nc.dram_tensor("yc", [TC, D], F32, kind="ExternalOutput").ap()

    es = ExitStack()
    with es:
        def sb(name, shape, dt):
            return es.enter_context(nc.sbuf_tensor(name, shape, dt))

        sem_names = ["act", "dve", "pool", "pe", "sp", "ldc", "ldx", "ldx0", "ldx1", "lda0", "lda1", "ldt0", "ldt1", "st"]
        ALLST = [k for k in sem_names if k not in ("act", "dve", "pool", "pe", "sp")]
        sems = {k: es.enter_context(nc.semaphore("s_" + k)) for k in sem_names}
        S = Sched(nc, sems)
        PS = [es.enter_context(nc.psum_tensor(f"ps{i}", [128, 512], F32)) for i in range(8)]

        onef = sb("onef", [128, 128], F32)
        oneb = sb("oneb", [128, 128], BF16)
        wgs = sb("wgs", [128, 8, 1024], F32)
        wgb = sb("wgb", [128, 8, 1024], BF16)
        wob = sb("wob", [128, 8, 1024], BF16)
        gp = sb("gp", [128, 8], F32)
        gpo = sb("gpo", [128, D], F32)
        sgl = sb("sgl", [128, 1], F32)
        lv = sb("lv", [128, 256], F32)
        ltmp = sb("ltmp", [128, 128], F32)
        lsum = sb("lsum", [128, 2], F32)
        neglam = sb("neglam", [128, 1], F32)
        epsT = sb("epsT", [128, 1], F32)
        xb = [sb(f"xb{i}", [128, 8, CB], BF16) for i in range(2)]
        xsq = sb("xsq", [128, 8, CB], BF16)
        rbc = sb("rbc", [128, CB], F32)
        at = [sb(f"at{i}", [128, 12, CB], F32) for i in range(2)]
        gsb = [sb(f"gsb{i}", [128, CB], F32) for i in range(2)]
        esb = [sb(f"esb{i}", [128, CB], F32) for i in range(2)]
        sil = sb("sil", [128, 8, CB], F32)
        dsb = sb("dsb", [128, CB], F32)
        dsq = sb("dsq", [128, CB], F32)
        rs = sb("rs", [128, CB], F32)
        mixT = sb("mixT", [128, 8, CB], BF16)
        xt = [sb(f"xt{i}", [128, D], F32) for i in range(2)]
        sq = [sb(f"sq{i}", [128, D], F32) for i in range(2)]
        ysb = [sb(f"ysb{i}", [128, D], F32) for i in range(2)]
        ssq = [sb(f"ssq{i}", [128, 1], F32) for i in range(2)]
        rstd = [sb(f"rstd{i}", [128, 1], F32) for i in range(2)]

        S.dma("sp", "ldc", onef[:], consts[:, C_ONE:C_ONE + 128])
        S.dma("pool", "ldx", oneb[:], consts[:, C_ONE:C_ONE + 128])
        S.dma("sp", "ldc", wgs[:], wg.rearrange("(k p) c -> p k c", p=128))
        S.dma("pool", "ldx", wob[:], wo.rearrange("(k p) c -> p k c", p=128))
        S.dma("sp", "ldc", gp[:], gpre[:, :])
        S.dma("sp", "ldc", gpo[:], gpost[:, :])
        S.dma("sp", "ldc", sgl[:], sgin[:, :])
        S.dma("sp", "ldc", lv[:], lamv[:, :])
        S.op("pool", lambda e: e.memset(epsT[:], EPS))
        S.barrier(ALLST)
        for k in range(8):
            S.op("dve", lambda e: e.tensor_scalar(out=wgb[:, k, :], in0=wgs[:, k, :], scalar1=gp[:, k:k + 1], scalar2=None, op0=ALU.mult))
        S.op("dve", lambda e: e.tensor_scalar(out=sgl[:], in0=sgl[:], scalar1=0.5 * (1.0 - LAM_INIT), scalar2=None, op0=ALU.mult), [S.last("dve")])
        t = S.op("dve", lambda e: e.tensor_tensor(out=ltmp[:, 0:64], in0=lv[:, 0:64], in1=lv[:, 64:128], op=ALU.mult))
        t = S.op("dve", lambda e: e.tensor_tensor(out=ltmp[:, 64:128], in0=lv[:, 128:192], in1=lv[:, 192:256], op=ALU.mult))
        t = S.op("dve", lambda e: e.reduce_sum(out=lsum[:, 0:1], in_=ltmp[:, 0:64], axis=mybir.AxisListType.X), [t])
        t = S.op("dve", lambda e: e.reduce_sum(out=lsum[:, 1:2], in_=ltmp[:, 64:128], axis=mybir.AxisListType.X), [t])
        t = S.op("act", lambda e: e.activation(out=lsum[:], in_=lsum[:], func=AF.Exp), [t])
        t = S.op("dve", lambda e: e.tensor_tensor(out=neglam[:], in0=lsum[:, 1:2], in1=lsum[:, 0:1], op=ALU.subtract), [t])
        t = S.op("dve", lambda e: e.tensor_scalar(out=neglam[:], in0=neglam[:], scalar1=-LAM_INIT, scalar2=None, op0=ALU.add), [t])
        S.barrier(ALLST)

        xT_v = xTc.rearrange("(k p) t -> p k t", p=128)
        AT_v = AT.rearrange("(c p) t -> p c t", p=128)
        OP = [[PS[4], PS[5]], [PS[6], PS[7]]]
        fr = {"xb": [None, None], "at": [None, None], "xsq": None, "ps0": None, "rbc": None, "gps": [None, None], "gsb": [None, None],
              "esb": [None, None], "sil": None, "mix": None, "ps3": None, "rs": None, "dsb": None, "op": [None, None], "sq": [None, None],
              "ysb": [None, None], "xt": [None, None]}
        ld = {}

        def issue_loads(blk):
            pb = blk % 2
            cs = slice(blk * CB, (blk + 1) * CB)
            ld[blk] = (S.dma("pool", f"ldx{pb}", xb[pb][:], xT_v[:, :, cs], [fr["xb"][pb]]),
                       S.dma("sp", f"lda{pb}", at[pb][:], AT_v[:, :, cs], [fr["at"][pb]]))

        issue_loads(0)
        tile_idx = 0
        for blk in range(NCB):
            pb = blk % 2
            if blk + 1 < NCB:
                issue_loads(blk + 1)
            tlx, tla = ld.pop(blk)
            tq = S.op("pool", lambda e: e.tensor_tensor(out=xsq[:], in0=xb[pb][:], in1=xb[pb][:], op=ALU.mult), [tlx, fr["xsq"]])
            for k in range(8):
                tss = S.op("pe", lambda e: e.matmul(PS[0][:, 0:CB], lhsT=oneb[:], rhs=xsq[:, k, :], start=(k == 0), stop=(k == 7)), [tq, fr["ps0"]] if k == 0 else (), inc=(k == 7))
            fr["xsq"] = tss
            t = S.op("act", lambda e: e.activation(out=rbc[:], in_=PS[0][:, 0:CB], func=AF.Ln, bias=epsT[:, 0:1], scale=1.0 / D), [tss, fr["rbc"]])
            fr["ps0"] = t
            t_r = S.op("act", lambda e: e.activation(out=rbc[:], in_=rbc[:], func=AF.Exp, scale=-0.5), [t])
            t_mix = []
            for c8 in range(8):
                par = c8 % 2
                gps = PS[1 + par]
                for k in range(8):
                    tg = S.op("pe", lambda e: e.matmul(gps[:, 0:CB], lhsT=wgb[:, k, c8 * 128:(c8 + 1) * 128], rhs=xb[pb][:, k, :], start=(k == 0), stop=(k == 7)),
                              [tlx, fr["gps"][par]] if k == 0 else (), inc=(k == 7))
                t_g = S.op("dve", lambda e: e.tensor_tensor(out=gsb[par][:], in0=gps[:, 0:CB], in1=rbc[:], op=ALU.mult), [tg, t_r, fr["gsb"][par]])
                fr["gps"][par] = t_g
                t_th = S.op("act", lambda e: e.activation(out=esb[par][:], in_=gsb[par][:], func=AF.Tanh, scale=0.5), [t_g, fr["esb"][par]])
                t_s2 = S.op("dve", lambda e: e.scalar_tensor_tensor(out=sil[:, c8, :], in0=esb[par][:], scalar=1.0, in1=gsb[par][:], op0=ALU.add, op1=ALU.mult), [t_th, fr["sil"]])
                fr["gsb"][par] = t_s2
                fr["esb"][par] = t_s2
                if c8 < 4:
                    t_mix.append(S.op("dve", lambda e: e.scalar_tensor_tensor(out=mixT[:, c8, :], in0=sil[:, c8, :], scalar=0.5, in1=at[pb][:, c8, :], op0=ALU.mult, op1=ALU.mult), [t_s2, tla, fr["mix"]]))
            fr["xb"][pb] = tg
            fr["rbc"] = t_s2
            for h in range(4):
                c8 = 4 + h
                t = S.op("dve", lambda e: e.scalar_tensor_tensor(out=dsb[:], in0=at[pb][:, 4 + 2 * h + 1, :], scalar=neglam[:, 0:1], in1=at[pb][:, 4 + 2 * h, :], op0=ALU.mult, op1=ALU.add), [tla, fr["dsb"]])
                t = S.op("dve", lambda e: e.tensor_tensor(out=dsq[:], in0=dsb[:], in1=dsb[:], op=ALU.mult), [t, fr["ps3"]])
                tm = S.op("pe", lambda e: e.matmul(PS[3][:, 0:CB], lhsT=onef[:], rhs=dsq[:], start=True, stop=True), [t, fr["rs"]])
                fr["ps3"] = tm
                t = S.op("act", lambda e: e.activation(out=rs[:], in_=PS[3][:, 0:CB], func=AF.Ln, bias=epsT[:, 0:1], scale=1.0 / 128.0), [tm, fr["dsb"]])
                t = S.op("act", lambda e: e.activation(out=rs[:], in_=rs[:], func=AF.Exp, scale=-0.5), [t])
                fr["rs"] = t
                t = S.op("dve", lambda e: e.scalar_tensor_tensor(out=dsb[:], in0=dsb[:], scalar=sgl[:, 0:1], in1=rs[:], op0=ALU.mult, op1=ALU.mult), [t])
                t = S.op("dve", lambda e: e.tensor_tensor(out=mixT[:, c8, :], in0=dsb[:], in1=sil[:, c8, :], op=ALU.mult), [t, fr["mix"]])
                fr["dsb"] = t
                fr["rs"] = t
                t_mix.append(t)
            fr["at"][pb] = t
            fr["sil"] = t
            t_mixall = t
            for tt in range(CB // 128):
                q = tile_idx % 2
                tile_idx += 1
                r0 = blk * CB + tt * 128
                tlt = S.dma("sp", f"ldt{q}", xt[q][:], xtok[r0:r0 + 128, :], [fr["xt"][q]])
                for half in range(2):
                    for c8 in range(8):
                        to = S.op("pe", lambda e: e.matmul(OP[q][half][:], lhsT=mixT[:, c8, tt * 128:(tt + 1) * 128], rhs=wob[:, c8, half * 512:(half + 1) * 512], start=(c8 == 0), stop=(c8 == 7)),
                                  [t_mixall, fr["op"][q]] if (c8 == 0 and half == 0) else (), inc=(c8 == 7))
                for half in range(2):
                    t = S.op("act", lambda e: e.activation(out=sq[q][:, half * 512:(half + 1) * 512], in_=OP[q][half][:], func=AF.Square), [to, fr["sq"][q]])
                t = S.op("dve", lambda e: e.reduce_sum(out=ssq[q][:], in_=sq[q][:], axis=mybir.AxisListType.X), [t])
                fr["sq"][q] = t
                t = S.op("act", lambda e: e.activation(out=rstd[q][:], in_=ssq[q][:], func=AF.Ln, bias=epsT[:, 0:1], scale=1.0 / D), [t])
                t = S.op("act", lambda e: e.activation(out=rstd[q][:], in_=rstd[q][:], func=AF.Exp, scale=-0.5), [t])
                for half in range(2):
                    t = S.op("dve", lambda e: e.scalar_tensor_tensor(out=ysb[q][:, half * 512:(half + 1) * 512], in0=OP[q][half][:], scalar=rstd[q][:, 0:1], in1=gpo[:, half * 512:(half + 1) * 512], op0=ALU.mult, op1=ALU.mult),
                             [t, fr["ysb"][q]])
                fr["op"][q] = t
                t = S.op("pool", lambda e: e.tensor_tensor(out=ysb[q][:], in0=ysb[q][:], in1=xt[q][:], op=ALU.add), [t, tlt])
                fr["xt"][q] = t
                fr["ysb"][q] = S.dma("sp", "st", yc[r0:r0 + 128, :], ysb[q][:], [t])
            fr["mix"] = to
        S.barrier(ALLST)
    return nc


_CACHE = {}
O_FQ, O_FK, O_FV, O_FF, O_FG, O_DQ, O_DK, O_DV, O_DG = 0, 512, 1024, 1536, 1544, 2056, 2568, 3080, 3592


def _prep_A(inp):
    xp = np.asarray(inp["x_prompt"], np.float32)[0]
    xs = np.asarray(inp["x_sample"], np.float32).reshape(TS, D)
    xT = np.ascontiguousarray(np.concatenate([xp, xs], axis=0).T)
    w = np.asarray(inp["w_in"], np.float32)[0]
    consts, oh = make_consts()
    gpre = np.ascontiguousarray(np.asarray(inp["norm_pre_g"], np.float32)[0].reshape(8, 128).T)
    fb = np.asarray(inp["forget_bias"], np.float32)[0]
    rb = np.asarray(inp["rel_bias"], np.float32)
    cfk = np.asarray(inp["cache_fox_k"], np.float32)[0]
    cfv = np.asarray(inp["cache_fox_v"], np.float32)[0]
    clf = np.asarray(inp["cache_fox_logf"], np.float32)[0]
    cdk = np.asarray(inp["cache_diff_k"], np.float32)[0]
    cdv = np.asarray(inp["cache_diff_v"], np.float32)[0]
    maps = []
    for c in range(NCORES):
        h, part = c // 2, c % 2
        fkc = np.arange(O_FK + 64 * c, O_FK + 64 * c + 64)
        dkc = np.arange(O_DK + 128 * h + 64 * part, O_DK + 128 * h + 64 * part + 64)
        cols = np.concatenate([
            np.arange(O_FQ + 64 * c, O_FQ + 64 * c + 64),
            np.arange(O_DQ + 128 * h + 64 * part, O_DQ + 128 * h + 64 * part + 64),
            fkc, dkc, fkc, dkc,
            np.arange(O_FV + 64 * c, O_FV + 64 * c + 64),
            np.arange(O_DV + 128 * h, O_DV + 128 * h + 128),
            np.arange(O_FF + c, O_FF + c + 1)])
        maps.append({
            "xT": xT,
            "wA": np.ascontiguousarray(w[:, cols]),
            "gpre": gpre,
            "bfc": np.full((128, 1), fb[c], np.float32),
            "rbh": np.ascontiguousarray(rb[:, h:h + 1]),
            "consts": consts,
            "oh": oh,
            "cfkT": np.ascontiguousarray(cfk[:, :, c, :].transpose(0, 2, 1)),
            "cdkT": np.ascontiguousarray(cdk[:, :, h, 64 * part:64 * part + 64].transpose(0, 2, 1)),
            "cfv": np.ascontiguousarray(cfv[:, :, c, :]),
            "cdv": np.ascontiguousarray(cdv[:, :, h, :]),
            "clogf": np.ascontiguousarray(clf[:, :, c].reshape(NSEQ, 32, 128).transpose(2, 0, 1).reshape(128, NSEQ * 32)),
        })
    return maps, xT


def run_A(inp):
    if "A" not in _CACHE:
        _CACHE["A"] = build_A()
    maps, xT = _prep_A(inp)
    res = run_bass_kernel_spmd(_CACHE["A"], maps, core_ids=list(range(NCORES)))
    return res.results, xT


def _tok_cols(c):
    return np.concatenate([np.arange(2048 * c, 2048 * c + 2048), np.arange(TP + 256 * c, TP + 256 * c + 256)])


def run_C(inp, resA, xT):
    if "C" not in _CACHE:
        _CACHE["C"] = build_C()
    w = np.asarray(inp["w_in"], np.float32)[0]
    wg = np.ascontiguousarray(np.concatenate([w[:, O_FG:O_FG + 512], w[:, O_DG:O_DG + 512]], axis=1))
    wo = np.ascontiguousarray(np.asarray(inp["w_out"], np.float32)[0])
    consts, _ = make_consts()
    gpre = np.ascontiguousarray(np.asarray(inp["norm_pre_g"], np.float32)[0].reshape(8, 128).T)
    gpost = np.ascontiguousarray(np.broadcast_to(np.asarray(inp["norm_post_g"], np.float32)[0][None, :], (128, D)))
    sg = np.ascontiguousarray(np.asarray(inp["subln_g"], np.float32)[0].reshape(128, 1))
    lamv = np.concatenate([np.asarray(inp[k], np.float32)[0] for k in ("lambda_q1", "lambda_k1", "lambda_q2", "lambda_k2")])
    lamv = np.ascontiguousarray(np.broadcast_to(lamv[None, :], (128, 256)))
    ATfull = np.concatenate([resA[c]["o_fo"] for c in range(NCORES)] + [resA[c]["o_dp"] for c in range(NCORES)], axis=0)
    maps = []
    for c in range(NCORES):
        tc = _tok_cols(c)
        maps.append({
            "AT": np.ascontiguousarray(ATfull[:, tc]),
            "xTc": np.ascontiguousarray(xT[:, tc]),
            "xtok": np.ascontiguousarray(xT[:, tc].T),
            "wg": wg, "wo": wo, "gpre": gpre, "gpost": gpost, "sg": sg, "lamv": lamv, "consts": consts,
        })
    res = run_bass_kernel_spmd(_CACHE["C"], maps, core_ids=list(range(NCORES)))
    return res.results


def kernel(**inp):
    resA, xT = run_A(inp)
    resC = run_C(inp, resA, xT)
    y_p = np.concatenate([resC[c]["yc"][:2048] for c in range(NCORES)], axis=0).reshape(1, TP, D)
    y_s = np.concatenate([resC[c]["yc"][2048:] for c in range(NCORES)], axis=0).reshape(NSEQ, SQ, D)

    def heads(key, cores, width):
        a = np.stack([resA[c][key] for c in cores], axis=1)
        return a

    fk = heads("o_fk", range(8), 64)
    fv = heads("o_fv", range(8), 64)
    lf = np.stack([resA[c]["o_lf"].reshape(T) for c in range(8)], axis=1)
    dk = heads("o_dk", range(8), 64).reshape(T, 4, 128)
    dv = heads("o_dv", range(0, 8, 2), 128)
    f32 = np.float32
    outs = (y_p.astype(f32), y_s.astype(f32),
            fk[:TP].reshape(1, 1, TP, 8, 64), fv[:TP].reshape(1, 1, TP, 8, 64), lf[:TP].reshape(1, 1, TP, 8),
            dk[:TP].reshape(1, 1, TP, 4, 128), dv[:TP].reshape(1, 1, TP, 4, 128),
            fk[TP:].reshape(1, NSEQ, SQ, 8, 64), fv[TP:].reshape(1, NSEQ, SQ, 8, 64), lf[TP:].reshape(1, NSEQ, SQ, 8),
            dk[TP:].reshape(1, NSEQ, SQ, 4, 128), dv[TP:].reshape(1, NSEQ, SQ, 4, 128))
    return tuple(np.ascontiguousarray(o, dtype=f32) for o in outs)
```

```python
import math
from contextlib import ExitStack

import numpy as np
import concourse.bass as bass
import concourse.mybir as mybir
from concourse.bass_utils import run_bass_kernel_spmd

F32 = mybir.dt.float32
BF16 = mybir.dt.bfloat16
AF = mybir.ActivationFunctionType
ALU = mybir.AluOpType

NCORES = 8
D = 1024
TP = 16384
NSEQ = 32
SQ = 64
PAST = 4096
TS = NSEQ * SQ
T = TP + TS
NBLK = T // 512
NPB = TP // 512
NT = T // 128
NPT = TP // 128
EPS = 1e-6
NEG = -30000.0
LAM_INIT = 0.8 - 0.6 * math.exp(-0.3 * 0)
TC = T // NCORES
CB = 384
NCB = TC // CB

C_ID, C_J, C_U, C_U2, C_SL, C_BL, C_ONE = [i * 128 for i in range(7)]
C_MA = 7 * 128
C_MAP = C_MA + 4 * 512
C_MB = C_MAP + 128
C_MBP = C_MB + 5 * 512
NCONST = C_MBP + 128
GLEN = 1152


class Sched:
    def __init__(self, nc, sems):
        self.nc = nc
        self.eng = {"act": nc.scalar, "dve": nc.vector, "pool": nc.gpsimd, "pe": nc.tensor, "sp": nc.sync}
        self.sem = sems
        self.cnt = {k: 0 for k in sems}
        self.waited = {e: {} for e in self.eng}

    def wait(self, e, deps):
        for d in deps:
            if d is None:
                continue
            sname, val = d
            if self.waited[e].get(sname, 0) >= val:
                continue
            self.eng[e].wait_ge(self.sem[sname], val)
            self.waited[e][sname] = val

    def op(self, e, fn, deps=(), inc=True):
        self.wait(e, deps)
        ins = fn(self.eng[e])
        if inc:
            ins.then_inc(self.sem[e], 1)
            self.cnt[e] += 1
            return (e, self.cnt[e])
        return None

    def dma(self, q, stream, out, in_, deps=(), **kw):
        self.wait(q, deps)
        self.eng[q].dma_start(out=out, in_=in_, **kw).then_inc(self.sem[stream], 16)
        self.cnt[stream] += 16
        return (stream, self.cnt[stream])

    def last(self, name):
        return (name, self.cnt[name]) if self.cnt[name] > 0 else None

    def barrier(self, streams):
        toks = [self.last(k) for k in list(self.eng) + list(streams)]
        for e in self.eng:
            self.wait(e, toks)


def _np_bucket(rel):
    rel = rel.astype(np.int32)
    ret = np.where(rel > 0, 16, 0)
    n = np.abs(rel)
    nf = np.maximum(n, 1).astype(np.float32)
    large = 8 + (np.log(nf / np.float32(8)) / np.float32(math.log(16)) * np.float32(8)).astype(np.int32)
    large = np.minimum(large, 15)
    return ret + np.where(n < 8, n, large)


def make_consts():
    p = np.arange(128)[:, None]
    f = np.arange(128)[None, :]
    c = np.zeros((128, NCONST), np.float32)
    c[:, C_ID:C_ID + 128] = (p == f)
    c[:, C_J:C_J + 128] = (p == 127 - f)
    c[:, C_U:C_U + 128] = (p <= f)
    c[:, C_U2:C_U2 + 128] = (p <= f) & (p // 64 == f // 64)
    c[:, C_SL:C_SL + 128] = (p < f)
    c[:, C_BL:C_BL + 128] = (p // 32 == f // 32) & (p % 32 >= f % 32)
    c[:, C_ONE:C_ONE + 128] = 1.0
    t = np.arange(512)[None, :]
    for r in range(4):
        c[:, C_MA + r * 512:C_MA + (r + 1) * 512] = np.where(128 * r + p <= t, 0.0, NEG)
    c[:, C_MAP:C_MAP + 128] = np.where((p <= f) & (p // 64 == f // 64), 0.0, NEG)
    sl = 127 - p
    for ri, r in enumerate(range(-1, 4)):
        valid = ((128 * r + sl) // 64) <= (t // 64)
        c[:, C_MB + ri * 512:C_MB + (ri + 1) * 512] = np.where(valid, 0.0, NEG)
    c[:, C_MBP:C_MBP + 128] = np.where(sl // 64 == f // 64, 0.0, NEG)
    m = np.arange(GLEN)
    bk = _np_bucket(511 - m)
    oh = np.zeros((32, GLEN), np.float32)
    oh[bk, m] += 1.0
    oh[15, :] -= 1.0
    return c, oh


def build_A():
    nc = bass.Bass("TRN2", target_bir_lowering=False)

    def din(name, shape):
        return nc.dram_tensor(name, shape, F32, kind="ExternalInput").ap()

    def dout(name, shape):
        return nc.dram_tensor(name, shape, F32, kind="ExternalOutput").ap()

    xT = din("xT", [D, T])
    wA = din("wA", [D, 577])
    gpre = din("gpre", [128, 8])
    bfc = din("bfc", [128, 1])
    rbh = din("rbh", [32, 1])
    consts = din("consts", [128, NCONST])
    ohT = din("oh", [32, GLEN])
    cfkT = din("cfkT", [NSEQ, 64, PAST])
    cdkT = din("cdkT", [NSEQ, 64, PAST])
    cfv = din("cfv", [NSEQ, PAST, 64])
    cdv = din("cdv", [NSEQ, PAST, 128])
    clogf = din("clogf", [128, NSEQ * 32])
    gscr = nc.dram_tensor("gscr", [1, GLEN], F32, kind="Internal").ap()

    o_fk = dout("o_fk", [T, 64])
    o_dk = dout("o_dk", [T, 64])
    o_fv = dout("o_fv", [T, 64])
    o_dv = dout("o_dv", [T, 128])
    o_lf = dout("o_lf", [NT, 128])
    o_fo = dout("o_fo", [64, T])
    o_dp = dout("o_dp", [128, T])

    es = ExitStack()
    with es:
        def sb(name, shape, dt):
            return es.enter_context(nc.sbuf_tensor(name, shape, dt))

        sem_names = ["act", "dve", "pool", "pe", "sp", "ldx0", "ldx1", "ldc", "ldm", "ldk0", "ldk1", "ldvb0", "ldvb1", "ldva0", "ldva1", "st", "stf"]
        ALLST = [k for k in sem_names if k not in ("act", "dve", "pool", "pe", "sp")]
        sems = {k: es.enter_context(nc.semaphore("s_" + k)) for k in sem_names}
        S = Sched(nc, sems)
        PS = [es.enter_context(nc.psum_tensor(f"ps{i}", [128, 512], F32)) for i in range(8)]
        SA = [PS[0], PS[1]]
        SB_ = [PS[2], PS[3]]
        OA, OB, LB, MISC = PS[4], PS[5], PS[6], PS[7]

        cst = sb("cst", [128, 7 * 128], F32)
        idb = sb("idb", [128, 128], BF16)
        jb = sb("jb", [128, 128], BF16)
        oneb = sb("oneb", [128, 128], BF16)
        maskA = sb("maskA", [128, 4 * 512 + 128], BF16)
        mbhi = sb("mbhi", [128, 5 * 512 + 128], BF16)
        mblo = sb("mblo", [128, 5 * 512 + 128], BF16)
        KTr = sb("KTr", [128, T // 2], F32)
        KT = KTr[:].bitcast(BF16)
        VA = sb("VA", [128, NT, 65], BF16)
        VBr = sb("VBr", [128, NT * 64], F32)
        VB = VBr[:].bitcast(BF16).rearrange("p (t d) -> p t d", d=128)
        RX = [sb(f"RX{i}", [128, 2048], F32) for i in range(3)]
        QT = [sb(f"QT{i}", [128, 512], BF16) for i in range(2)]
        QTS = sb("QTS", [128, TS], BF16)
        wq = sb("wq", [128, 8, 128], BF16)
        wk = sb("wk", [128, 8, 128], BF16)
        wt = sb("wt", [128, 8, 321], BF16)
        epsT = sb("epsT", [128, 1], F32)
        negbf = sb("negbf", [128, 1], F32)
        LF = sb("LF", [128, NT], F32)
        CUM = sb("CUM", [128, NT], F32)
        GT = sb("GT", [128, NPT + 1], F32)
        WN = sb("WN", [128, 16], F32)
        arg = [sb(f"arg{i}", [128, 128], F32) for i in range(2)]
        xb = [RX[i][:].bitcast(BF16).rearrange("p (k t) -> p k t", k=8) for i in range(2)]
        xsq = RX[2][:].bitcast(BF16).rearrange("p (k t) -> p k t", k=8)
        rbc = sb("rbc", [128, 512], F32)
        rtok = sb("rtok", [128, 4], F32)
        stage = sb("stage", [128, 4, 321], F32)
        tmp4 = sb("tmp4", [128, 4], F32)
        PA = [sb(f"PA{i}", [128, 512], BF16) for i in range(2)]
        PB = [sb(f"PB{i}", [128, 512], BF16) for i in range(2)]
        rl = sb("rl", [128, 512], F32)
        oasb = sb("oasb", [128, 512], F32)
        obsb = sb("obsb", [128, 512], F32)
        outA = sb("outA", [128, 512], F32)
        outB = sb("outB", [128, 512], F32)
        lacc = [sb(f"lacc{i}", [128, 512], F32) for i in range(2)]
        VAf = [RX[i][:].rearrange("p (j d) -> p j d", d=64) for i in range(2)]
        clf = RX[2][:, 0:NSEQ * 32]
        Cc = RX[2][:, NSEQ * 32:2 * NSEQ * 32]
        Wc = sb("Wc", [128, NSEQ * 32], F32)
        totT = sb("totT", [128, 128], F32)

        ident_f = cst[:, C_ID:C_ID + 128]
        U_f = cst[:, C_U:C_U + 128]
        U2_f = cst[:, C_U2:C_U2 + 128]
        BL_f = cst[:, C_BL:C_BL + 128]
        one_f = cst[:, C_ONE:C_ONE + 128]

        wst = VBr[:, 0:8 * 577].rearrange("p (k c) -> p k c", c=577)
        NMB = 5 * 512 + 128
        mbf = KTr[:, 0:NMB]
        mbm = KTr[:, NMB:2 * NMB]
        ohs = KTr[0:32, 2 * NMB:2 * NMB + GLEN]
        gv = KTr[0:1, 2 * NMB + GLEN:2 * NMB + 2 * GLEN]
        rbs = sb("rbs", [32, 1], F32)
        gp = sb("gp", [128, 8], F32)
        bft = sb("bft", [128, 1], F32)
        S.dma("sp", "ldc", cst[:], consts[:, 0:7 * 128])
        S.dma("pool", "ldm", maskA[:], consts[:, C_MA:C_MA + 4 * 512 + 128])
        S.dma("sp", "ldc", wst, wA.rearrange("(k p) c -> p k c", p=128))
        S.dma("sp", "ldc", gp[:], gpre[:, :])
        S.dma("sp", "ldc", bft[:], bfc[:, :])
        S.dma("sp", "ldc", ohs, ohT[:, :])
        S.dma("sp", "ldc", rbs[:], rbh[:, :])
        S.dma("sp", "ldc", mbm[:, 0:5 * 512], consts[:, C_MB:C_MB + 5 * 512])
        S.dma("sp", "ldc", mbm[:, 5 * 512:NMB], consts[:, C_MBP:C_MBP + 128])
        S.op("pool", lambda e: e.memset(epsT[:], EPS))
        S.op("pool", lambda e: e.memset(VA[:, :, 64:65], 1.0))
        S.op("pool", lambda e: e.memset(GT[:, 0:1], 0.0))
        S.barrier(["ldc", "ldm"])
        S.op("pool", lambda e: e.tensor_copy(out=idb[:], in_=cst[:, C_ID:C_ID + 128]))
        S.op("pool", lambda e: e.tensor_copy(out=jb[:], in_=cst[:, C_J:C_J + 128]))
        S.op("pool", lambda e: e.tensor_copy(out=oneb[:], in_=cst[:, C_ONE:C_ONE + 128]))
        S.op("dve", lambda e: e.tensor_scalar(out=negbf[:], in0=bft[:], scalar1=-1.0, scalar2=None, op0=ALU.mult))
        for k in range(8):
            S.op("dve", lambda e: e.tensor_scalar(out=wq[:, k, :], in0=wst[:, k, 0:128], scalar1=gp[:, k:k + 1], scalar2=0.125, op0=ALU.mult, op1=ALU.mult))
            S.op("dve", lambda e: e.tensor_scalar(out=wk[:, k, :], in0=wst[:, k, 128:256], scalar1=gp[:, k:k + 1], scalar2=None, op0=ALU.mult))
            S.op("dve", lambda e: e.tensor_scalar(out=wt[:, k, :], in0=wst[:, k, 256:577], scalar1=gp[:, k:k + 1], scalar2=None, op0=ALU.mult))
        tg = None
        for i0 in range(0, GLEN, 512):
            n = min(512, GLEN - i0)
            tmm = S.op("pe", lambda e: e.matmul(MISC[0:1, 0:n], lhsT=rbs[:, 0:1], rhs=ohs[:, i0:i0 + n], start=True, stop=True), [tg])
            tg = S.op("dve", lambda e: e.tensor_copy(out=gv[0:1, i0:i0 + n], in_=MISC[0:1, 0:n]), [tmm])
        t_gs = S.dma("sp", "stf", gscr[:, :], gv, [tg])
        S.wait("sp", [t_gs])
        for ri in range(5):
            off = 512 - 128 * ri
            S.dma("sp", "ldc", mbf[:, ri * 512:(ri + 1) * 512], bass.AP(gscr.tensor, off, [[1, 128], [1, 512]]))
        th = S.dma("sp", "ldc", mbf[:, 5 * 512:NMB], bass.AP(gscr.tensor, 384, [[1, 128], [1, 128]]))
        t1 = S.op("dve", lambda e: e.tensor_tensor(out=mbf, in0=mbf, in1=mbm, op=ALU.add), [th])
        t2 = S.op("dve", lambda e: e.tensor_copy(out=mbhi[:], in_=mbf), [t1])
        t3 = S.op("dve", lambda e: e.tensor_tensor(out=mblo[:], in0=mbf, in1=mbhi[:], op=ALU.subtract), [t2])
        S.barrier(ALLST)

        xT_v = xT.rearrange("(k p) t -> p k t", p=128)
        state = {"xld": {}, "pe_xb": [None, None], "stage_free": [], "qt_free": [None, None]}

        def issue_xload(b):
            bb = b % 2
            state["xld"][b] = S.dma("pool", f"ldx{bb}", xb[bb][:], xT_v[:, :, b * 512:(b + 1) * 512], [state["pe_xb"][bb]])

        PJ = [LB, MISC]
        pj = {"pj": [None, None], "xsq": None, "rbc_r": [], "stage": [], "tmp4": None}

        def project_gen(b):
            bb = b % 2
            is_s = b >= NPB
            if b + 1 < NBLK:
                issue_xload(b + 1)
            ld = state["xld"].pop(b)
            tq = S.op("pool", lambda e: e.tensor_tensor(out=xsq[:], in0=xb[bb][:], in1=xb[bb][:], op=ALU.mult), [ld, pj["xsq"]])
            yield
            for k in range(8):
                tss = S.op("pe", lambda e: e.matmul(PJ[0][:], lhsT=oneb[:], rhs=xsq[:, k, :], start=(k == 0), stop=(k == 7)), [tq, pj["pj"][0]] if k == 0 else (), inc=(k == 7))
            pj["xsq"] = tss
            for k in range(8):
                tqm = S.op("pe", lambda e: e.matmul(PJ[1][:], lhsT=wq[:, k, :], rhs=xb[bb][:, k, :], start=(k == 0), stop=(k == 7)), [ld, pj["pj"][1]] if k == 0 else (), inc=(k == 7))
            yield
            yield
            t_ln = S.op("act", lambda e: e.activation(out=rbc[:], in_=PJ[0][:], func=AF.Ln, bias=epsT[:, 0:1], scale=1.0 / D), [tss] + pj["rbc_r"])
            pj["pj"][0] = t_ln
            t_r = S.op("act", lambda e: e.activation(out=rbc[:], in_=rbc[:], func=AF.Exp, scale=-0.5), [t_ln])
            yield
            for k in range(8):
                tkm = S.op("pe", lambda e: e.matmul(PJ[0][:], lhsT=wk[:, k, :], rhs=xb[bb][:, k, :], start=(k == 0), stop=(k == 7)), [pj["pj"][0]] if k == 0 else (), inc=(k == 7))
            yield
            qdst = QTS[:, (b - NPB) * 512:(b - NPB + 1) * 512] if is_s else QT[bb][:]
            t_q = S.op("dve", lambda e: e.tensor_tensor(out=qdst, in0=PJ[1][:], in1=rbc[:], op=ALU.mult), [tqm, t_r, state["qt_free"][bb]])
            pj["pj"][1] = t_q
            t_k = S.op("dve", lambda e: e.tensor_tensor(out=KT[:, b * 512:(b + 1) * 512], in0=PJ[0][:], in1=rbc[:], op=ALU.mult), [tkm])
            pj["pj"][0] = t_k
            yield
            for tt in range(4):
                trt = S.op("pe", lambda e: e.matmul(PJ[1][:, tt:tt + 1], lhsT=rbc[0:1, tt * 128:(tt + 1) * 128], rhs=one_f[0:1, 0:1], start=True, stop=True),
                           [t_r, pj["pj"][1]] if tt == 0 else (), inc=(tt == 3))
            pj["rbc_r"] = [t_k, trt]
            yield
            t_rt = S.op("dve", lambda e: e.tensor_copy(out=rtok[:], in_=PJ[1][:, 0:4]), [trt])
            pj["pj"][1] = t_rt
            t_ev = None
            ttm = None
            for tt in range(4):
                bank = PJ[tt % 2]
                for k in range(8):
                    ttm = S.op("pe", lambda e: e.matmul(bank[:, 0:321], lhsT=xb[bb][:, k, tt * 128:(tt + 1) * 128], rhs=wt[:, k, :], start=(k == 0), stop=(k == 7)),
                               [pj["pj"][tt % 2]] if k == 0 else (), inc=(k == 7))
                yield
                t_ev = S.op("dve", lambda e: e.tensor_scalar(out=stage[:, tt, :], in0=bank[:, 0:321], scalar1=rtok[:, tt:tt + 1], scalar2=None, op0=ALU.mult),
                            [ttm, t_rt] + (pj["stage"] if tt == 0 else []))
                pj["pj"][tt % 2] = t_ev
            state["pe_xb"][bb] = ttm
            yield
            t_e = S.op("act", lambda e: e.activation(out=tmp4[:], in_=stage[:, :, 320], func=AF.Exp, bias=negbf[:, 0:1], scale=-1.0), [t_ev, pj["tmp4"]])
            t_l = S.op("act", lambda e: e.activation(out=tmp4[:], in_=tmp4[:], func=AF.Ln, bias=1.0, scale=1.0), [t_e])
            rows = slice(b * 512, (b + 1) * 512)
            so = []
            so.append(S.dma("sp", "st", o_fk[rows, :].rearrange("(tt p) d -> p tt d", p=128), stage[:, :, 0:64], [t_ev]))
            so.append(S.dma("sp", "st", o_dk[rows, :].rearrange("(tt p) d -> p tt d", p=128), stage[:, :, 64:128]))
            so.append(S.dma("sp", "st", o_fv[rows, :].rearrange("(tt p) d -> p tt d", p=128), stage[:, :, 128:192]))
            so.append(S.dma("sp", "st", o_dv[rows, :].rearrange("(tt p) d -> p tt d", p=128), stage[:, :, 192:320]))
            yield
            t_lf = S.op("dve", lambda e: e.tensor_scalar(out=LF[:, 4 * b:4 * b + 4], in0=tmp4[:], scalar1=-1.0, scalar2=None, op0=ALU.mult), [t_l])
            pj["tmp4"] = t_lf
            yield
            tcm = S.op("pe", lambda e: e.matmul(PJ[0][:, 8:12], lhsT=(U2_f if is_s else U_f), rhs=LF[:, 4 * b:4 * b + 4], start=True, stop=True), [t_lf, pj["pj"][0]], inc=is_s)
            if not is_s:
                tcm = S.op("pe", lambda e: e.matmul(PJ[0][:, 16:20], lhsT=one_f, rhs=LF[:, 4 * b:4 * b + 4], start=True, stop=True))
                tv1 = S.op("pool", lambda e: e.tensor_copy(out=VA[:, 4 * b:4 * b + 4, 0:64], in_=stage[:, :, 128:192]), [t_ev])
                yield
                tcu = None
                for tt in range(4):
                    kk = 4 * b + tt
                    S.op("dve", lambda e: e.tensor_tensor(out=CUM[:, kk:kk + 1], in0=PJ[0][:, 8 + tt:9 + tt], in1=GT[:, kk:kk + 1], op=ALU.add), [tcm, tcu])
                    tcu = S.op("dve", lambda e: e.tensor_tensor(out=GT[:, kk + 1:kk + 2], in0=PJ[0][:, 16 + tt:17 + tt], in1=GT[:, kk:kk + 1], op=ALU.add))
                state["cum"] = tcu
                pj["pj"][0] = tcu
            else:
                sbi = b - NPB
                yield
                twn = S.op("act", lambda e: e.activation(out=WN[:, 4 * sbi:4 * sbi + 4], in_=PJ[0][:, 8:12], func=AF.Exp, scale=-1.0), [tcm])
                pj["pj"][0] = twn
                tv1 = S.op("dve", lambda e: e.tensor_tensor(out=VA[:, 4 * b:4 * b + 4, 0:64], in0=stage[:, :, 128:192],
                                                            in1=WN[:, 4 * sbi:4 * sbi + 4].unsqueeze(2).to_broadcast([128, 4, 64]), op=ALU.mult), [twn, t_ev])
                tv1 = S.op("dve", lambda e: e.tensor_copy(out=VA[:, 4 * b:4 * b + 4, 64:65], in_=WN[:, 4 * sbi:4 * sbi + 4].unsqueeze(2)), [tv1])
            tv2 = S.op("pool", lambda e: e.tensor_copy(out=VB[:, 4 * b:4 * b + 4, :], in_=stage[:, :, 192:320]), [t_ev, tv1])
            pj["stage"] = so + [tv1, tv2, t_e]
            state["kv"] = [t_k, tv1, tv2]
            state["tq"] = t_q

        def run_all(g):
            for _ in g:
                pass

        def finalize_gen(i, t_last_pe, par):
            col0 = i * 512
            tl_ = S.op("pe", lambda e: e.matmul(PJ[0][0:1, :], lhsT=one_f[:, 0:1], rhs=lacc[par][:], start=True, stop=True), [pst["lacc_last"], pj["pj"][0]])
            pst["lacc_free"][par] = tl_
            yield
            t1 = S.op("dve", lambda e: e.reciprocal(out=rl[64:65, :], in_=oasb[64:65, :]), [pst["fin_pe"], pst["fin_st"]])
            t4 = S.op("dve", lambda e: e.reciprocal(out=rl[0:1, :], in_=PJ[0][0:1, :]), [tl_])
            yield
            tm = S.op("pe", lambda e: e.matmul(PJ[1][0:64, :], lhsT=one_f[64:65, 0:64], rhs=rl[64:65, :], start=True, stop=True), [t1, pj["pj"][1]], inc=False)
            tm2 = S.op("pe", lambda e: e.matmul(PJ[0][:, :], lhsT=one_f[0:1, 0:128], rhs=rl[0:1, :], start=True, stop=True), [t4])
            pst["fin_pe"] = tm2
            yield
            yield
            t3 = S.op("dve", lambda e: e.tensor_tensor(out=outA[0:64, :], in0=oasb[0:64, :], in1=PJ[1][0:64, :], op=ALU.mult), [tm2])
            t6 = S.op("dve", lambda e: e.tensor_tensor(out=outB[:, :], in0=obsb[:, :], in1=PJ[0][:, :], op=ALU.mult))
            pj["pj"][0] = t6
            pj["pj"][1] = t6
            pst["sb_copy_free"] = t6
            S.dma("sp", "st", o_fo[:, col0:col0 + 512], outA[0:64, :], [t6])
            pst["fin_st"] = S.dma("sp", "st", o_dp[:, col0:col0 + 512], outB[:, :])
            yield

        def finalize(ncol, col0, t_last_pe, use_lacc=False):
            t1 = S.op("dve", lambda e: e.reciprocal(out=rl[64:65, 0:ncol], in_=OA[64:65, 0:ncol]), [t_last_pe, S.last("pe"), S.last("st")])
            t2 = S.op("dve", lambda e: e.tensor_copy(out=oasb[0:64, 0:ncol], in_=OA[0:64, 0:ncol]))
            tm = S.op("pe", lambda e: e.matmul(MISC[0:64, 0:ncol], lhsT=one_f[64:65, 0:64], rhs=rl[64:65, 0:ncol], start=True, stop=True), [t1, S.last("dve"), S.last("act")])
            t3 = S.op("dve", lambda e: e.tensor_tensor(out=outA[0:64, 0:ncol], in0=oasb[0:64, 0:ncol], in1=MISC[0:64, 0:ncol], op=ALU.mult), [tm])
            s1 = S.dma("sp", "st", o_fo[:, col0:col0 + ncol], outA[0:64, 0:ncol], [t3])
            if use_lacc:
                tl_ = S.op("pe", lambda e: e.matmul(LB[0:1, 0:ncol], lhsT=one_f[:, 0:1], rhs=lacc[:, 0:ncol], start=True, stop=True), [pst["lacc_last"]])
                pst["lacc_free"] = tl_
                t4 = S.op("dve", lambda e: e.reciprocal(out=rl[0:1, 0:ncol], in_=LB[0:1, 0:ncol]), [tl_])
            else:
                t4 = S.op("dve", lambda e: e.reciprocal(out=rl[0:1, 0:ncol], in_=LB[0:1, 0:ncol]))
            t5 = S.op("dve", lambda e: e.tensor_copy(out=obsb[:, 0:ncol], in_=OB[:, 0:ncol]))
            tm2 = S.op("pe", lambda e: e.matmul(MISC[:, 0:ncol], lhsT=one_f[0:1, 0:128], rhs=rl[0:1, 0:ncol], start=True, stop=True), [t4, t3])
            t6 = S.op("dve", lambda e: e.tensor_tensor(out=outB[:, 0:ncol], in0=obsb[:, 0:ncol], in1=MISC[:, 0:ncol], op=ALU.mult), [tm2])
            s2 = S.dma("sp", "st", o_dp[:, col0:col0 + ncol], outB[:, 0:ncol], [t6])
            return t6

        pst = {"sa_free": [None, None], "sb_free": [None, None], "pa_free": [None, None], "pb_free": [None, None], "pb_free2": [None, None], "acc_free": None, "lacc_free": [None, None], "lacc_last": None, "fin_pe": None, "fin_st": None, "sb_copy_free": None}

        def attention(i, t_q, gens):
            qb = i % 2
            lp = i % 2
            nj = 4 * i + 4
            ab = i % 2
            t_arg = S.op("dve", lambda e: e.tensor_scalar(out=arg[ab][:, 0:nj], in0=CUM[:, 0:nj], scalar1=-1.0, scalar2=GT[:, 4 * i + 2:4 * i + 3], op0=ALU.mult, op1=ALU.add),
                         [state["cum"], S.last("act")])
            kv = state["kv"] + [t_q]

            def qk(j):
                u = j % 2
                r = j - 4 * i
                last = r < 0
                S.op("pe", lambda e: e.matmul(SA[u][:], lhsT=KT[0:64, j * 128:(j + 1) * 128], rhs=QT[qb][0:64, :], start=True, stop=last),
                     kv + [pst["sa_free"][u]], inc=False)
                if r >= 0:
                    S.op("pe", lambda e: e.matmul(SA[u][:], lhsT=idb[:], rhs=maskA[:, r * 512:(r + 1) * 512], start=False, stop=True), inc=False)
                lastb = r < -1
                S.op("pe", lambda e: e.matmul(SB_[u][:], lhsT=KT[64:128, j * 128:(j + 1) * 128], rhs=QT[qb][64:128, :], start=True, stop=lastb),
                     [pst["sb_free"][u]], inc=lastb)
                if r >= -1:
                    ri = r + 1
                    S.op("pe", lambda e: e.matmul(SB_[u][:], lhsT=jb[:], rhs=mbhi[:, ri * 512:(ri + 1) * 512], start=False, stop=False), inc=False)
                    S.op("pe", lambda e: e.matmul(SB_[u][:], lhsT=jb[:], rhs=mblo[:, ri * 512:(ri + 1) * 512], start=False, stop=True))
                return S.last("pe")

            def ex(j, tqk):
                u = j % 2
                ta = S.op("act", lambda e: e.activation(out=PA[u][:], in_=SA[u][:], func=AF.Exp, bias=arg[ab][:, j:j + 1], scale=1.0), [tqk, t_arg, pst["pa_free"][u]])
                tb = S.op("act", lambda e: e.activation(out=PB[u][:], in_=SB_[u][:], func=AF.Exp), [pst["pb_free"][u], pst["pb_free2"][u]])
                pst["sa_free"][u] = ta
                pst["sb_free"][u] = tb
                return tb

            def pv(j, tex):
                u = j % 2
                first = j == 0
                lastj = j == nj - 1
                S.op("pe", lambda e: e.matmul(OA[0:65, :], lhsT=VA[:, j, :], rhs=PA[u][:], start=first, stop=lastj), [tex, pst["acc_free"]] if first else [tex], inc=False)
                t = S.op("pe", lambda e: e.matmul(OB[:, :], lhsT=VB[:, j, :], rhs=PB[u][:], start=first, stop=lastj))
                if first:
                    td = S.op("dve", lambda e: e.tensor_copy(out=lacc[lp][:], in_=PB[u][:]), [tex, pst["lacc_free"][lp]])
                else:
                    td = S.op("dve", lambda e: e.tensor_tensor(out=lacc[lp][:], in0=lacc[lp][:], in1=PB[u][:], op=ALU.add), [tex])
                pst["pa_free"][u] = t
                pst["pb_free"][u] = td
                pst["pb_free2"][u] = t
                pst["lacc_last"] = td
                return t

            tqk = {0: qk(0)}
            tl = None
            gi = 0
            for j in range(nj):
                if j + 1 < nj:
                    tqk[j + 1] = qk(j + 1)
                tex = ex(j, tqk.pop(j))
                tl = pv(j, tex)
                while gi < len(gens):
                    try:
                        next(gens[gi])
                        break
                    except StopIteration:
                        gi += 1
            for g in gens[gi:]:
                run_all(g)
            state["qt_free"][qb] = tl
            tc1 = S.op("dve", lambda e: e.tensor_copy(out=oasb[0:65, :], in_=OA[0:65, :]), [tl, pst["sb_copy_free"]])
            tc2 = S.op("dve", lambda e: e.tensor_copy(out=obsb[:, :], in_=OB[:, :]))
            pst["acc_free"] = tc2
            return finalize_gen(i, tl, lp)

        issue_xload(0)
        run_all(project_gen(0))
        fin = None
        for b in range(NPB):
            gens = ([fin] if fin is not None else []) + [project_gen(b + 1)]
            fin = attention(b, state["tq"], gens)
        run_all(fin)
        for b in range(NPB + 1, NBLK):
            run_all(project_gen(b))
        S.barrier(ALLST)

        t_cl = S.dma("sp", "ldc", clf[:], clogf[:, :])
        for hh in range(2):
            tmm = S.op("pe", lambda e: e.matmul(SA[hh][:], lhsT=U_f, rhs=clf[:, hh * 512:(hh + 1) * 512], start=True, stop=True), [t_cl])
            S.op("dve", lambda e: e.tensor_copy(out=Cc[:, hh * 512:(hh + 1) * 512], in_=SA[hh][:]), [tmm])
        tprev = None
        for q8 in range(8):
            tmm = S.op("pe", lambda e: e.matmul(SB_[0][:, 0:128], lhsT=clf[:, q8 * 128:(q8 + 1) * 128], rhs=one_f, start=True, stop=True), [tprev])
            tcp = S.op("dve", lambda e: e.tensor_copy(out=totT[:], in_=SB_[0][:, 0:128]), [tmm, tprev])
            tmm2 = S.op("pe", lambda e: e.matmul(SB_[1][:, 0:128], lhsT=totT[:], rhs=BL_f, start=True, stop=True), [tcp])
            tprev = S.op("dve", lambda e: e.tensor_tensor(out=Wc[:, q8 * 128:(q8 + 1) * 128], in0=SB_[1][:, 0:128], in1=Cc[:, q8 * 128:(q8 + 1) * 128], op=ALU.subtract), [tmm2])
        t_wc = S.op("act", lambda e: e.activation(out=Wc[:], in_=Wc[:], func=AF.Exp), [tprev])
        S.barrier(ALLST)

        KTc = [KT[:, 0:PAST], KT[:, PAST:2 * PAST]]
        VBc = [VB[:, 0:32, :], VB[:, 32:64, :]]
        VAc = [VA[:, 0:32, :], VA[:, 32:64, :]]
        cst_free = [None, None]
        ldtok = {}

        def load_seq(n):
            cb = n % 2
            d = [cst_free[cb]]
            a = S.dma("pool", f"ldk{cb}", KTc[cb][0:64, :], cfkT[n, :, :], d)
            a = S.dma("pool", f"ldk{cb}", KTc[cb][64:128, :], cdkT[n, :, :])
            b_ = S.dma("pool", f"ldvb{cb}", VBc[cb], cdv[n, :, :].rearrange("(j p) d -> p j d", p=128))
            c_ = S.dma("sp", f"ldva{cb}", VAf[cb][:], cfv[n, :, :].rearrange("(j p) d -> p j d", p=128), d)
            ldtok[n] = [a, b_, c_]

        load_seq(0)
        sa_free = [None, None]
        sb_free = [None, None]
        pa_free = [None, None]
        acc_free = None
        for m in range(16):
            tt = NPT + m
            u = 0
            S.op("pe", lambda e: e.matmul(SA[u][:, 0:128], lhsT=KT[0:64, tt * 128:(tt + 1) * 128], rhs=QTS[0:64, m * 128:(m + 1) * 128], start=True, stop=False), [sa_free[u], sb_free[u]], inc=False)
            S.op("pe", lambda e: e.matmul(SA[u][:, 0:128], lhsT=idb[:], rhs=maskA[:, 4 * 512:4 * 512 + 128], start=False, stop=True), inc=False)
            S.op("pe", lambda e: e.matmul(SB_[u][:, 0:128], lhsT=KT[64:128, tt * 128:(tt + 1) * 128], rhs=QTS[64:128, m * 128:(m + 1) * 128], start=True, stop=False), inc=False)
            S.op("pe", lambda e: e.matmul(SB_[u][:, 0:128], lhsT=jb[:], rhs=mbhi[:, 5 * 512:5 * 512 + 128], start=False, stop=False), inc=False)
            tqk = S.op("pe", lambda e: e.matmul(SB_[u][:, 0:128], lhsT=jb[:], rhs=mblo[:, 5 * 512:5 * 512 + 128], start=False, stop=True))
            ta = S.op("act", lambda e: e.activation(out=PA[u][:, 0:128], in_=SA[u][:, 0:128], func=AF.Exp), [tqk, pa_free[u]])
            tb = S.op("act", lambda e: e.activation(out=PB[u][:, 0:128], in_=SB_[u][:, 0:128], func=AF.Exp))
            sa_free[u] = tb
            sb_free[u] = tb
            S.op("pe", lambda e: e.matmul(OA[0:65, 0:128], lhsT=VA[:, tt, :], rhs=PA[u][:, 0:128], start=True, stop=False), [tb, acc_free], inc=False)
            S.op("pe", lambda e: e.matmul(OB[:, 0:128], lhsT=VB[:, tt, :], rhs=PB[u][:, 0:128], start=True, stop=False), inc=False)
            tpv = S.op("pe", lambda e: e.matmul(LB[0:1, 0:128], lhsT=oneb[:, 0:1], rhs=PB[u][:, 0:128], start=True, stop=False))
            pa_free[u] = tpv
            for hh in range(2):
                n = 2 * m + hh
                cb = n % 2
                if n + 1 < NSEQ:
                    load_seq(n + 1)
                lda, ldb_, ldc_ = ldtok.pop(n)
                tva = S.op("dve", lambda e: e.tensor_tensor(out=VAc[cb][:, :, 0:64], in0=VAf[cb][:], in1=Wc[:, n * 32:(n + 1) * 32].unsqueeze(2).to_broadcast([128, 32, 64]), op=ALU.mult), [ldc_, t_wc])
                tva = S.op("dve", lambda e: e.tensor_copy(out=VAc[cb][:, :, 64:65], in_=Wc[:, n * 32:(n + 1) * 32].unsqueeze(2)))
                qcols = slice(m * 128 + hh * 64, m * 128 + hh * 64 + 64)
                ocols = slice(hh * 64, hh * 64 + 64)
                for g in range(4):
                    u = (g + 1) % 2
                    for jj in range(8):
                        j = 8 * g + jj
                        S.op("pe", lambda e: e.matmul(SA[u][:, jj * 64:(jj + 1) * 64], lhsT=KTc[cb][0:64, j * 128:(j + 1) * 128], rhs=QTS[0:64, qcols], start=True, stop=True),
                             [lda, sa_free[u], sb_free[u]] if jj == 0 else (), inc=False)
                    for jj in range(8):
                        j = 8 * g + jj
                        lastb = j != 31
                        tqk = S.op("pe", lambda e: e.matmul(SB_[u][:, jj * 64:(jj + 1) * 64], lhsT=KTc[cb][64:128, j * 128:(j + 1) * 128], rhs=QTS[64:128, qcols], start=True, stop=lastb), inc=(jj == 7 and lastb))
                        if j == 31:
                            S.op("pe", lambda e: e.matmul(SB_[u][:, jj * 64:(jj + 1) * 64], lhsT=jb[:], rhs=mbhi[:, 0:64], start=False, stop=False), inc=False)
                            tqk = S.op("pe", lambda e: e.matmul(SB_[u][:, jj * 64:(jj + 1) * 64], lhsT=jb[:], rhs=mblo[:, 0:64], start=False, stop=True))
                    ta = S.op("act", lambda e: e.activation(out=PA[u][:], in_=SA[u][:], func=AF.Exp), [tqk, pa_free[u]])
                    tb = S.op("act", lambda e: e.activation(out=PB[u][:], in_=SB_[u][:], func=AF.Exp))
                    sa_free[u] = tb
                    sb_free[u] = tb
                    for jj in range(8):
                        j = 8 * g + jj
                        lastj = (j == 31)
                        S.op("pe", lambda e: e.matmul(OA[0:65, ocols], lhsT=VAc[cb][:, j, :], rhs=PA[u][:, jj * 64:(jj + 1) * 64], start=False, stop=lastj, skip_group_check=True),
                             [tb, tva, ldb_] if jj == 0 else (), inc=False)
                        S.op("pe", lambda e: e.matmul(OB[:, ocols], lhsT=VBc[cb][:, j, :], rhs=PB[u][:, jj * 64:(jj + 1) * 64], start=False, stop=lastj, skip_group_check=True), inc=False)
                        tpv = S.op("pe", lambda e: e.matmul(LB[0:1, ocols], lhsT=oneb[:, 0:1], rhs=PB[u][:, jj * 64:(jj + 1) * 64], start=False, stop=lastj, skip_group_check=True), inc=(jj == 7))
                    pa_free[u] = tpv
                cst_free[cb] = tpv
            acc_free = finalize(128, TP + m * 128, tpv)

        for c0, n in ((0, 128), (128, 16)):
            tmm = S.op("pe", lambda e: e.matmul(MISC[0:n, 0:128], lhsT=LF[:, c0:c0 + n], rhs=ident_f, start=True, stop=True), [S.last("dve"), S.last("act")])
            tcp = S.op("dve", lambda e: e.tensor_copy(out=rl[0:n, 0:128], in_=MISC[0:n, 0:128]), [tmm, S.last("st")])
            S.dma("sp", "st", o_lf[c0:c0 + n, :], rl[0:n, 0:128], [tcp])
        S.barrier(ALLST)
    return nc


def build_C():
    nc = bass.Bass("TRN2", target_bir_lowering=False)

    def din(name, shape):
        return nc.dram_tensor(name, shape, F32, kind="ExternalInput").ap()

    AT = din("AT", [1536, TC])
    xTc = din("xTc", [D, TC])
    xtok = din("xtok", [TC, D])
    wg = din("wg", [D, 1024])
    wo = din("wo", [1024, D])
    gpre = din("gpre", [128, 8])
    gpost = din("gpost", [128, D])
    sgin = din("sg", [128, 1])
    lamv = din("lamv", [128, 256])
    consts = din("consts", [128, NCONST])
    yc = nc.dram_tensor("yc", [TC, D], F32, kind="ExternalOutput").ap()

    es = ExitStack()
    with es:
        def sb(name, shape, dt):
            return es.enter_context(nc.sbuf_tensor(name, shape, dt))

        sem_names = ["act", "dve", "pool", "pe", "sp", "ldc", "ldx", "ldx0", "ldx1", "lda0", "lda1", "ldt0", "ldt1", "st"]
        ALLST = [k for k in sem_names if k not in ("act", "dve", "pool", "pe", "sp")]
        sems = {k: es.enter_context(nc.semaphore("s_" + k)) for k in sem_names}
        S = Sched(nc, sems)
        PS = [es.enter_context(nc.psum_tensor(f"ps{i}", [128, 512], F32)) for i in range(8)]

        onef = sb("onef", [128, 128], F32)
        oneb = sb("oneb", [128, 128], BF16)
        wgs = sb("wgs", [128, 8, 1024], F32)
        wgb = sb("wgb", [128, 8, 1024], BF16)
        wob = sb("wob", [128, 8, 1024], BF16)
        gp = sb("gp", [128, 8], F32)
        gpo = sb("gpo", [128, D], F32)
        sgl = sb("sgl", [128, 1], F32)
        lv = sb("lv", [128, 256], F32)
        ltmp = sb("ltmp", [128, 128], F32)
        lsum = sb("lsum", [128, 2], F32)
        neglam = sb("neglam", [128, 1], F32)
        epsT = sb("epsT", [128, 1], F32)
        xb = [sb(f"xb{i}", [128, 8, CB], BF16) for i in range(2)]
        xsq = sb("xsq", [128, 8, CB], BF16)
        rbc = sb("rbc", [128, CB], F32)
        at = [sb(f"at{i}", [128, 12, CB], F32) for i in range(2)]
        gsb = [sb(f"gsb{i}", [128, CB], F32) for i in range(2)]
        esb = [sb(f"esb{i}", [128, CB], F32) for i in range(2)]
        sil = sb("sil", [128, 8, CB], F32)
        dsb = sb("dsb", [128, CB], F32)
        dsq = sb("dsq", [128, CB], F32)
        rs = sb("rs", [128, CB], F32)
        mixT = sb("mixT", [128, 8, CB], BF16)
        xt = [sb(f"xt{i}", [128, D], F32) for i in range(2)]
        sq = [sb(f"sq{i}", [128, D], F32) for i in range(2)]
        ysb = [sb(f"ysb{i}", [128, D], F32) for i in range(2)]
        ssq = [sb(f"ssq{i}", [128, 1], F32) for i in range(2)]
        rstd = [sb(f"rstd{i}", [128, 1], F32) for i in range(2)]

        S.dma("sp", "ldc", onef[:], consts[:, C_ONE:C_ONE + 128])
        S.dma("pool", "ldx", oneb[:], consts[:, C_ONE:C_ONE + 128])
        S.dma("sp", "ldc", wgs[:], wg.rearrange("(k p) c -> p k c", p=128))
        S.dma("pool", "ldx", wob[:], wo.rearrange("(k p) c -> p k c", p=128))
        S.dma("sp", "ldc", gp[:], gpre[:, :])
        S.dma("sp", "ldc", gpo[:], gpost[:, :])
        S.dma("sp", "ldc", sgl[:], sgin[:, :])
        S.dma("sp", "ldc", lv[:], lamv[:, :])
        S.op("pool", lambda e: e.memset(epsT[:], EPS))
        S.barrier(ALLST)
        for k in range(8):
            S.op("dve", lambda e: e.tensor_scalar(out=wgb[:, k, :], in0=wgs[:, k, :], scalar1=gp[:, k:k + 1], scalar2=None, op0=ALU.mult))
        S.op("dve", lambda e: e.tensor_scalar(out=sgl[:], in0=sgl[:], scalar1=0.5 * (1.0 - LAM_INIT), scalar2=None, op0=ALU.mult), [S.last("dve")])
        t = S.op("dve", lambda e: e.tensor_tensor(out=ltmp[:, 0:64], in0=lv[:, 0:64], in1=lv[:, 64:128], op=ALU.mult))
        t = S.op("dve", lambda e: e.tensor_tensor(out=ltmp[:, 64:128], in0=lv[:, 128:192], in1=lv[:, 192:256], op=ALU.mult))
        t = S.op("dve", lambda e: e.reduce_sum(out=lsum[:, 0:1], in_=ltmp[:, 0:64], axis=mybir.AxisListType.X), [t])
        t = S.op("dve", lambda e: e.reduce_sum(out=lsum[:, 1:2], in_=ltmp[:, 64:128], axis=mybir.AxisListType.X), [t])
        t = S.op("act", lambda e: e.activation(out=lsum[:], in_=lsum[:], func=AF.Exp), [t])
        t = S.op("dve", lambda e: e.tensor_tensor(out=neglam[:], in0=lsum[:, 1:2], in1=lsum[:, 0:1], op=ALU.subtract), [t])
        t = S.op("dve", lambda e: e.tensor_scalar(out=neglam[:], in0=neglam[:], scalar1=-LAM_INIT, scalar2=None, op0=ALU.add), [t])
        S.barrier(ALLST)

        xT_v = xTc.rearrange("(k p) t -> p k t", p=128)
        AT_v = AT.rearrange("(c p) t -> p c t", p=128)
        OP = [[PS[4], PS[5]], [PS[6], PS[7]]]
        fr = {"xb": [None, None], "at": [None, None], "xsq": None, "ps0": None, "rbc": None, "gps": [None, None], "gsb": [None, None],
              "esb": [None, None], "sil": None, "mix": None, "ps3": None, "rs": None, "dsb": None, "op": [None, None], "sq": [None, None],
              "ysb": [None, None], "xt": [None, None]}
        ld = {}

        def issue_loads(blk):
            pb = blk % 2
            cs = slice(blk * CB, (blk + 1) * CB)
            ld[blk] = (S.dma("pool", f"ldx{pb}", xb[pb][:], xT_v[:, :, cs], [fr["xb"][pb]]),
                       S.dma("sp", f"lda{pb}", at[pb][:], AT_v[:, :, cs], [fr["at"][pb]]))

        issue_loads(0)
        tile_idx = 0
        for blk in range(NCB):
            pb = blk % 2
            if blk + 1 < NCB:
                issue_loads(blk + 1)
            tlx, tla = ld.pop(blk)
            tq = S.op("pool", lambda e: e.tensor_tensor(out=xsq[:], in0=xb[pb][:], in1=xb[pb][:], op=ALU.mult), [tlx, fr["xsq"]])
            for k in range(8):
                tss = S.op("pe", lambda e: e.matmul(PS[0][:, 0:CB], lhsT=oneb[:], rhs=xsq[:, k, :], start=(k == 0), stop=(k == 7)), [tq, fr["ps0"]] if k == 0 else (), inc=(k == 7))
            fr["xsq"] = tss
            t = S.op("act", lambda e: e.activation(out=rbc[:], in_=PS[0][:, 0:CB], func=AF.Ln, bias=epsT[:, 0:1], scale=1.0 / D), [tss, fr["rbc"]])
            fr["ps0"] = t
            t_r = S.op("act", lambda e: e.activation(out=rbc[:], in_=rbc[:], func=AF.Exp, scale=-0.5), [t])
            t_mix = []
            for c8 in range(8):
                par = c8 % 2
                gps = PS[1 + par]
                for k in range(8):
                    tg = S.op("pe", lambda e: e.matmul(gps[:, 0:CB], lhsT=wgb[:, k, c8 * 128:(c8 + 1) * 128], rhs=xb[pb][:, k, :], start=(k == 0), stop=(k == 7)),
                              [tlx, fr["gps"][par]] if k == 0 else (), inc=(k == 7))
                t_g = S.op("dve", lambda e: e.tensor_tensor(out=gsb[par][:], in0=gps[:, 0:CB], in1=rbc[:], op=ALU.mult), [tg, t_r, fr["gsb"][par]])
                fr["gps"][par] = t_g
                t_th = S.op("act", lambda e: e.activation(out=esb[par][:], in_=gsb[par][:], func=AF.Tanh, scale=0.5), [t_g, fr["esb"][par]])
                t_s2 = S.op("dve", lambda e: e.scalar_tensor_tensor(out=sil[:, c8, :], in0=esb[par][:], scalar=1.0, in1=gsb[par][:], op0=ALU.add, op1=ALU.mult), [t_th, fr["sil"]])
                fr["gsb"][par] = t_s2
                fr["esb"][par] = t_s2
                if c8 < 4:
                    t_mix.append(S.op("dve", lambda e: e.scalar_tensor_tensor(out=mixT[:, c8, :], in0=sil[:, c8, :], scalar=0.5, in1=at[pb][:, c8, :], op0=ALU.mult, op1=ALU.mult), [t_s2, tla, fr["mix"]]))
            fr["xb"][pb] = tg
            fr["rbc"] = t_s2
            for h in range(4):
                c8 = 4 + h
                t = S.op("dve", lambda e: e.scalar_tensor_tensor(out=dsb[:], in0=at[pb][:, 4 + 2 * h + 1, :], scalar=neglam[:, 0:1], in1=at[pb][:, 4 + 2 * h, :], op0=ALU.mult, op1=ALU.add), [tla, fr["dsb"]])
                t = S.op("dve", lambda e: e.tensor_tensor(out=dsq[:], in0=dsb[:], in1=dsb[:], op=ALU.mult), [t, fr["ps3"]])
                tm = S.op("pe", lambda e: e.matmul(PS[3][:, 0:CB], lhsT=onef[:], rhs=dsq[:], start=True, stop=True), [t, fr["rs"]])
                fr["ps3"] = tm
                t = S.op("act", lambda e: e.activation(out=rs[:], in_=PS[3][:, 0:CB], func=AF.Ln, bias=epsT[:, 0:1], scale=1.0 / 128.0), [tm, fr["dsb"]])
                t = S.op("act", lambda e: e.activation(out=rs[:], in_=rs[:], func=AF.Exp, scale=-0.5), [t])
                fr["rs"] = t
                t = S.op("dve", lambda e: e.scalar_tensor_tensor(out=dsb[:], in0=dsb[:], scalar=sgl[:, 0:1], in1=rs[:], op0=ALU.mult, op1=ALU.mult), [t])
                t = S.op("dve", lambda e: e.tensor_tensor(out=mixT[:, c8, :], in0=dsb[:], in1=sil[:, c8, :], op=ALU.mult), [t, fr["mix"]])
                fr["dsb"] = t
                fr["rs"] = t
                t_mix.append(t)
            fr["at"][pb] = t
            fr["sil"] = t
            t_mixall = t
            for tt in range(CB // 128):
                q = tile_idx % 2
                tile_idx += 1
                r0 = blk * CB + tt * 128
                tlt = S.dma("sp", f"ldt{q}", xt[q][:], xtok[r0:r0 + 128, :], [fr["xt"][q]])
                for half in range(2):
                    for c8 in range(8):
                        to = S.op("pe", lambda e: e.matmul(OP[q][half][:], lhsT=mixT[:, c8, tt * 128:(tt + 1) * 128], rhs=wob[:, c8, half * 512:(half + 1) * 512], start=(c8 == 0), stop=(c8 == 7)),
                                  [t_mixall, fr["op"][q]] if (c8 == 0 and half == 0) else (), inc=(c8 == 7))
                for half in range(2):
                    t = S.op("act", lambda e: e.activation(out=sq[q][:, half * 512:(half + 1) * 512], in_=OP[q][half][:], func=AF.Square), [to, fr["sq"][q]])
                t = S.op("dve", lambda e: e.reduce_sum(out=ssq[q][:], in_=sq[q][:], axis=mybir.AxisListType.X), [t])
                fr["sq"][q] = t
                t = S.op("act", lambda e: e.activation(out=rstd[q][:], in_=ssq[q][:], func=AF.Ln, bias=epsT[:, 0:1], scale=1.0 / D), [t])
                t = S.op("act", lambda e: e.activation(out=rstd[q][:], in_=rstd[q][:], func=AF.Exp, scale=-0.5), [t])
                for half in range(2):
                    t = S.op("dve", lambda e: e.scalar_tensor_tensor(out=ysb[q][:, half * 512:(half + 1) * 512], in0=OP[q][half][:], scalar=rstd[q][:, 0:1], in1=gpo[:, half * 512:(half + 1) * 512], op0=ALU.mult, op1=ALU.mult),
                             [t, fr["ysb"][q]])
                fr["op"][q] = t
                t = S.op("pool", lambda e: e.tensor_tensor(out=ysb[q][:], in0=ysb[q][:], in1=xt[q][:], op=ALU.add), [t, tlt])
                fr["xt"][q] = t
                fr["ysb"][q] = S.dma("sp", "st", yc[r0:r0 + 128, :], ysb[q][:], [t])
            fr["mix"] = to
        S.barrier(ALLST)
    return nc


_CACHE = {}
O_FQ, O_FK, O_FV, O_FF, O_FG, O_DQ, O_DK, O_DV, O_DG = 0, 512, 1024, 1536, 1544, 2056, 2568, 3080, 3592


def _prep_A(inp):
    xp = np.asarray(inp["x_prompt"], np.float32)[0]
    xs = np.asarray(inp["x_sample"], np.float32).reshape(TS, D)
    xT = np.ascontiguousarray(np.concatenate([xp, xs], axis=0).T)
    w = np.asarray(inp["w_in"], np.float32)[0]
    consts, oh = make_consts()
    gpre = np.ascontiguousarray(np.asarray(inp["norm_pre_g"], np.float32)[0].reshape(8, 128).T)
    fb = np.asarray(inp["forget_bias"], np.float32)[0]
    rb = np.asarray(inp["rel_bias"], np.float32)
    cfk = np.asarray(inp["cache_fox_k"], np.float32)[0]
    cfv = np.asarray(inp["cache_fox_v"], np.float32)[0]
    clf = np.asarray(inp["cache_fox_logf"], np.float32)[0]
    cdk = np.asarray(inp["cache_diff_k"], np.float32)[0]
    cdv = np.asarray(inp["cache_diff_v"], np.float32)[0]
    maps = []
    for c in range(NCORES):
        h, part = c // 2, c % 2
        fkc = np.arange(O_FK + 64 * c, O_FK + 64 * c + 64)
        dkc = np.arange(O_DK + 128 * h + 64 * part, O_DK + 128 * h + 64 * part + 64)
        cols = np.concatenate([
            np.arange(O_FQ + 64 * c, O_FQ + 64 * c + 64),
            np.arange(O_DQ + 128 * h + 64 * part, O_DQ + 128 * h + 64 * part + 64),
            fkc, dkc, fkc, dkc,
            np.arange(O_FV + 64 * c, O_FV + 64 * c + 64),
            np.arange(O_DV + 128 * h, O_DV + 128 * h + 128),
            np.arange(O_FF + c, O_FF + c + 1)])
        maps.append({
            "xT": xT,
            "wA": np.ascontiguousarray(w[:, cols]),
            "gpre": gpre,
            "bfc": np.full((128, 1), fb[c], np.float32),
            "rbh": np.ascontiguousarray(rb[:, h:h + 1]),
            "consts": consts,
            "oh": oh,
            "cfkT": np.ascontiguousarray(cfk[:, :, c, :].transpose(0, 2, 1)),
            "cdkT": np.ascontiguousarray(cdk[:, :, h, 64 * part:64 * part + 64].transpose(0, 2, 1)),
            "cfv": np.ascontiguousarray(cfv[:, :, c, :]),
            "cdv": np.ascontiguousarray(cdv[:, :, h, :]),
            "clogf": np.ascontiguousarray(clf[:, :, c].reshape(NSEQ, 32, 128).transpose(2, 0, 1).reshape(128, NSEQ * 32)),
        })
    return maps, xT


def run_A(inp):
    if "A" not in _CACHE:
        _CACHE["A"] = build_A()
    maps, xT = _prep_A(inp)
    res = run_bass_kernel_spmd(_CACHE["A"], maps, core_ids=list(range(NCORES)))
    return res.results, xT


def _tok_cols(c):
    return np.concatenate([np.arange(2048 * c, 2048 * c + 2048), np.arange(TP + 256 * c, TP + 256 * c + 256)])


def run_C(inp, resA, xT):
    if "C" not in _CACHE:
        _CACHE["C"] = build_C()
    w = np.asarray(inp["w_in"], np.float32)[0]
    wg = np.ascontiguousarray(np.concatenate([w[:, O_FG:O_FG + 512], w[:, O_DG:O_DG + 512]], axis=1))
    wo = np.ascontiguousarray(np.asarray(inp["w_out"], np.float32)[0])
    consts, _ = make_consts()
    gpre = np.ascontiguousarray(np.asarray(inp["norm_pre_g"], np.float32)[0].reshape(8, 128).T)
    gpost = np.ascontiguousarray(np.broadcast_to(np.asarray(inp["norm_post_g"], np.float32)[0][None, :], (128, D)))
    sg = np.ascontiguousarray(np.asarray(inp["subln_g"], np.float32)[0].reshape(128, 1))
    lamv = np.concatenate([np.asarray(inp[k], np.float32)[0] for k in ("lambda_q1", "lambda_k1", "lambda_q2", "lambda_k2")])
    lamv = np.ascontiguousarray(np.broadcast_to(lamv[None, :], (128, 256)))
    ATfull = np.concatenate([resA[c]["o_fo"] for c in range(NCORES)] + [resA[c]["o_dp"] for c in range(NCORES)], axis=0)
    maps = []
    for c in range(NCORES):
        tc = _tok_cols(c)
        maps.append({
            "AT": np.ascontiguousarray(ATfull[:, tc]),
            "xTc": np.ascontiguousarray(xT[:, tc]),
            "xtok": np.ascontiguousarray(xT[:, tc].T),
            "wg": wg, "wo": wo, "gpre": gpre, "gpost": gpost, "sg": sg, "lamv": lamv, "consts": consts,
        })
    res = run_bass_kernel_spmd(_CACHE["C"], maps, core_ids=list(range(NCORES)))
    return res.results


def kernel(**inp):
    resA, xT = run_A(inp)
    resC = run_C(inp, resA, xT)
    y_p = np.concatenate([resC[c]["yc"][:2048] for c in range(NCORES)], axis=0).reshape(1, TP, D)
    y_s = np.concatenate([resC[c]["yc"][2048:] for c in range(NCORES)], axis=0).reshape(NSEQ, SQ, D)

    def heads(key, cores, width):
        a = np.stack([resA[c][key] for c in cores], axis=1)
        return a

    fk = heads("o_fk", range(8), 64)
    fv = heads("o_fv", range(8), 64)
    lf = np.stack([resA[c]["o_lf"].reshape(T) for c in range(8)], axis=1)
    dk = heads("o_dk", range(8), 64).reshape(T, 4, 128)
    dv = heads("o_dv", range(0, 8, 2), 128)
    f32 = np.float32
    outs = (y_p.astype(f32), y_s.astype(f32),
            fk[:TP].reshape(1, 1, TP, 8, 64), fv[:TP].reshape(1, 1, TP, 8, 64), lf[:TP].reshape(1, 1, TP, 8),
            dk[:TP].reshape(1, 1, TP, 4, 128), dv[:TP].reshape(1, 1, TP, 4, 128),
            fk[TP:].reshape(1, NSEQ, SQ, 8, 64), fv[TP:].reshape(1, NSEQ, SQ, 8, 64), lf[TP:].reshape(1, NSEQ, SQ, 8),
            dk[TP:].reshape(1, NSEQ, SQ, 4, 128), dv[TP:].reshape(1, NSEQ, SQ, 4, 128))
    return tuple(np.ascontiguousarray(o, dtype=f32) for o in outs)
```

```python
import math
from contextlib import ExitStack

import numpy as np
import concourse.bass as bass
import concourse.mybir as mybir
from concourse.bass_utils import run_bass_kernel_spmd

F32 = mybir.dt.float32
BF16 = mybir.dt.bfloat16
AF = mybir.ActivationFunctionType
ALU = mybir.AluOpType

NCORES = 8
D = 1024
TP = 16384
NSEQ = 32
SQ = 64
PAST = 4096
TS = NSEQ * SQ
T = TP + TS
NBLK = T // 512
NPB = TP // 512
NT = T // 128
NPT = TP // 128
EPS = 1e-6
NEG = -30000.0
LAM_INIT = 0.8 - 0.6 * math.exp(-0.3 * 0)
TC = T // NCORES
CB = 384
NCB = TC // CB

C_ID, C_J, C_U, C_U2, C_SL, C_BL, C_ONE = [i * 128 for i in range(7)]
C_MA = 7 * 128
C_MAP = C_MA + 4 * 512
C_MB = C_MAP + 128
C_MBP = C_MB + 5 * 512
NCONST = C_MBP + 128
GLEN = 1152


class Sched:
    def __init__(self, nc, sems):
        self.nc = nc
        self.eng = {"act": nc.scalar, "dve": nc.vector, "pool": nc.gpsimd, "pe": nc.tensor, "sp": nc.sync}
        self.sem = sems
        self.cnt = {k: 0 for k in sems}
        self.waited = {e: {} for e in self.eng}

    def wait(self, e, deps):
        for d in deps:
            if d is None:
                continue
            sname, val = d
            if self.waited[e].get(sname, 0) >= val:
                continue
            self.eng[e].wait_ge(self.sem[sname], val)
            self.waited[e][sname] = val

    def op(self, e, fn, deps=(), inc=True):
        self.wait(e, deps)
        ins = fn(self.eng[e])
        if inc:
            ins.then_inc(self.sem[e], 1)
            self.cnt[e] += 1
            return (e, self.cnt[e])
        return None

    def dma(self, q, stream, out, in_, deps=(), **kw):
        self.wait(q, deps)
        self.eng[q].dma_start(out=out, in_=in_, **kw).then_inc(self.sem[stream], 16)
        self.cnt[stream] += 16
        return (stream, self.cnt[stream])

    def last(self, name):
        return (name, self.cnt[name]) if self.cnt[name] > 0 else None

    def barrier(self, streams):
        toks = [self.last(k) for k in list(self.eng) + list(streams)]
        for e in self.eng:
            self.wait(e, toks)


def _np_bucket(rel):
    rel = rel.astype(np.int32)
    ret = np.where(rel > 0, 16, 0)
    n = np.abs(rel)
    nf = np.maximum(n, 1).astype(np.float32)
    large = 8 + (np.log(nf / np.float32(8)) / np.float32(math.log(16)) * np.float32(8)).astype(np.int32)
    large = np.minimum(large, 15)
    return ret + np.where(n < 8, n, large)


def make_consts():
    p = np.arange(128)[:, None]
    f = np.arange(128)[None, :]
    c = np.zeros((128, NCONST), np.float32)
    c[:, C_ID:C_ID + 128] = (p == f)
    c[:, C_J:C_J + 128] = (p == 127 - f)
    c[:, C_U:C_U + 128] = (p <= f)
    c[:, C_U2:C_U2 + 128] = (p <= f) & (p // 64 == f // 64)
    c[:, C_SL:C_SL + 128] = (p < f)
    c[:, C_BL:C_BL + 128] = (p // 32 == f // 32) & (p % 32 >= f % 32)
    c[:, C_ONE:C_ONE + 128] = 1.0
    t = np.arange(512)[None, :]
    for r in range(4):
        c[:, C_MA + r * 512:C_MA + (r + 1) * 512] = np.where(128 * r + p <= t, 0.0, NEG)
    c[:, C_MAP:C_MAP + 128] = np.where((p <= f) & (p // 64 == f // 64), 0.0, NEG)
    sl = 127 - p
    for ri, r in enumerate(range(-1, 4)):
        valid = ((128 * r + sl) // 64) <= (t // 64)
        c[:, C_MB + ri * 512:C_MB + (ri + 1) * 512] = np.where(valid, 0.0, NEG)
    c[:, C_MBP:C_MBP + 128] = np.where(sl // 64 == f // 64, 0.0, NEG)
    m = np.arange(GLEN)
    bk = _np_bucket(511 - m)
    oh = np.zeros((32, GLEN), np.float32)
    oh[bk, m] += 1.0
    oh[15, :] -= 1.0
    return c, oh


def build_A():
    nc = bass.Bass("TRN2", target_bir_lowering=False)

    def din(name, shape):
        return nc.dram_tensor(name, shape, F32, kind="ExternalInput").ap()

    def dout(name, shape):
        return nc.dram_tensor(name, shape, F32, kind="ExternalOutput").ap()

    xT = din("xT", [D, T])
    wA = din("wA", [D, 577])
    gpre = din("gpre", [128, 8])
    bfc = din("bfc", [128, 1])
    rbh = din("rbh", [32, 1])
    consts = din("consts", [128, NCONST])
    ohT = din("oh", [32, GLEN])
    cfkT = din("cfkT", [NSEQ, 64, PAST])
    cdkT = din("cdkT", [NSEQ, 64, PAST])
    cfv = din("cfv", [NSEQ, PAST, 64])
    cdv = din("cdv", [NSEQ, PAST, 128])
    clogf = din("clogf", [128, NSEQ * 32])
    gscr = nc.dram_tensor("gscr", [1, GLEN], F32, kind="Internal").ap()

    o_fk = dout("o_fk", [T, 64])
    o_dk = dout("o_dk", [T, 64])
    o_fv = dout("o_fv", [T, 64])
    o_dv = dout("o_dv", [T, 128])
    o_lf = dout("o_lf", [NT, 128])
    o_fo = dout("o_fo", [64, T])
    o_dp = dout("o_dp", [128, T])

    es = ExitStack()
    with es:
        def sb(name, shape, dt):
            return es.enter_context(nc.sbuf_tensor(name, shape, dt))

        sem_names = ["act", "dve", "pool", "pe", "sp", "ldx0", "ldx1", "ldc", "ldm", "ldk0", "ldk1", "ldvb0", "ldvb1", "ldva0", "ldva1", "st", "stf"]
        ALLST = [k for k in sem_names if k not in ("act", "dve", "pool", "pe", "sp")]
        sems = {k: es.enter_context(nc.semaphore("s_" + k)) for k in sem_names}
        S = Sched(nc, sems)
        PS = [es.enter_context(nc.psum_tensor(f"ps{i}", [128, 512], F32)) for i in range(8)]
        SA = [PS[0], PS[1]]
        SB_ = [PS[2], PS[3]]
        OA, OB, LB, MISC = PS[4], PS[5], PS[6], PS[7]

        cst = sb("cst", [128, 7 * 128], F32)
        idb = sb("idb", [128, 128], BF16)
        jb = sb("jb", [128, 128], BF16)
        oneb = sb("oneb", [128, 128], BF16)
        maskA = sb("maskA", [128, 4 * 512 + 128], BF16)
        mbhi = sb("mbhi", [128, 5 * 512 + 128], BF16)
        mblo = sb("mblo", [128, 5 * 512 + 128], BF16)
        KTr = sb("KTr", [128, T // 2], F32)
        KT = KTr[:].bitcast(BF16)
        VA = sb("VA", [128, NT, 65], BF16)
        VBr = sb("VBr", [128, NT * 64], F32)
        VB = VBr[:].bitcast(BF16).rearrange("p (t d) -> p t d", d=128)
        RX = [sb(f"RX{i}", [128, 2048], F32) for i in range(3)]
        QT = [sb(f"QT{i}", [128, 512], BF16) for i in range(2)]
        QTS = sb("QTS", [128, TS], BF16)
        wq = sb("wq", [128, 8, 128], BF16)
        wk = sb("wk", [128, 8, 128], BF16)
        wt = sb("wt", [128, 8, 321], BF16)
        epsT = sb("epsT", [128, 1], F32)
        negbf = sb("negbf", [128, 1], F32)
        LF = sb("LF", [128, NT], F32)
        CUM = sb("CUM", [128, NT], F32)
        GT = sb("GT", [128, NPT + 1], F32)
        WN = sb("WN", [128, 16], F32)
        arg = [sb(f"arg{i}", [128, 128], F32) for i in range(2)]
        xb = [RX[i][:].bitcast(BF16).rearrange("p (k t) -> p k t", k=8) for i in range(2)]
        xsq = RX[2][:].bitcast(BF16).rearrange("p (k t) -> p k t", k=8)
        rbc = sb("rbc", [128, 512], F32)
        rtok = sb("rtok", [128, 4], F32)
        stage = sb("stage", [128, 4, 321], F32)
        tmp4 = sb("tmp4", [128, 4], F32)
        PA = [sb(f"PA{i}", [128, 512], BF16) for i in range(2)]
        PB = [sb(f"PB{i}", [128, 512], BF16) for i in range(2)]
        rl = sb("rl", [128, 512], F32)
        oasb = sb("oasb", [128, 512], F32)
        obsb = sb("obsb", [128, 512], F32)
        outA = sb("outA", [128, 512], F32)
        outB = sb("outB", [128, 512], F32)
        lacc = [sb(f"lacc{i}", [128, 512], F32) for i in range(2)]
        VAf = [RX[i][:].rearrange("p (j d) -> p j d", d=64) for i in range(2)]
        clf = RX[2][:, 0:NSEQ * 32]
        Cc = RX[2][:, NSEQ * 32:2 * NSEQ * 32]
        Wc = sb("Wc", [128, NSEQ * 32], F32)
        totT = sb("totT", [128, 128], F32)

        ident_f = cst[:, C_ID:C_ID + 128]
        U_f = cst[:, C_U:C_U + 128]
        U2_f = cst[:, C_U2:C_U2 + 128]
        BL_f = cst[:, C_BL:C_BL + 128]
        one_f = cst[:, C_ONE:C_ONE + 128]

        wst = VBr[:, 0:8 * 577].rearrange("p (k c) -> p k c", c=577)
        NMB = 5 * 512 + 128
        mbf = KTr[:, 0:NMB]
        mbm = KTr[:, NMB:2 * NMB]
        ohs = KTr[0:32, 2 * NMB:2 * NMB + GLEN]
        gv = KTr[0:1, 2 * NMB + GLEN:2 * NMB + 2 * GLEN]
        rbs = sb("rbs", [32, 1], F32)
        gp = sb("gp", [128, 8], F32)
        bft = sb("bft", [128, 1], F32)
        S.dma("sp", "ldc", cst[:], consts[:, 0:7 * 128])
        S.dma("pool", "ldm", maskA[:], consts[:, C_MA:C_MA + 4 * 512 + 128])
        S.dma("sp", "ldc", wst, wA.rearrange("(k p) c -> p k c", p=128))
        S.dma("sp", "ldc", gp[:], gpre[:, :])
        S.dma("sp", "ldc", bft[:], bfc[:, :])
        S.dma("sp", "ldc", ohs, ohT[:, :])
        S.dma("sp", "ldc", rbs[:], rbh[:, :])
        S.dma("sp", "ldc", mbm[:, 0:5 * 512], consts[:, C_MB:C_MB + 5 * 512])
        S.dma("sp", "ldc", mbm[:, 5 * 512:NMB], consts[:, C_MBP:C_MBP + 128])
        S.op("pool", lambda e: e.memset(epsT[:], EPS))
        S.op("pool", lambda e: e.memset(VA[:, :, 64:65], 1.0))
        S.op("pool", lambda e: e.memset(GT[:, 0:1], 0.0))
        S.barrier(["ldc", "ldm"])
        S.op("pool", lambda e: e.tensor_copy(out=idb[:], in_=cst[:, C_ID:C_ID + 128]))
        S.op("pool", lambda e: e.tensor_copy(out=jb[:], in_=cst[:, C_J:C_J + 128]))
        S.op("pool", lambda e: e.tensor_copy(out=oneb[:], in_=cst[:, C_ONE:C_ONE + 128]))
        S.op("dve", lambda e: e.tensor_scalar(out=negbf[:], in0=bft[:], scalar1=-1.0, scalar2=None, op0=ALU.mult))
        for k in range(8):
            S.op("dve", lambda e: e.tensor_scalar(out=wq[:, k, :], in0=wst[:, k, 0:128], scalar1=gp[:, k:k + 1], scalar2=0.125, op0=ALU.mult, op1=ALU.mult))
            S.op("dve", lambda e: e.tensor_scalar(out=wk[:, k, :], in0=wst[:, k, 128:256], scalar1=gp[:, k:k + 1], scalar2=None, op0=ALU.mult))
            S.op("dve", lambda e: e.tensor_scalar(out=wt[:, k, :], in0=wst[:, k, 256:577], scalar1=gp[:, k:k + 1], scalar2=None, op0=ALU.mult))
        tg = None
        for i0 in range(0, GLEN, 512):
            n = min(512, GLEN - i0)
            tmm = S.op("pe", lambda e: e.matmul(MISC[0:1, 0:n], lhsT=rbs[:, 0:1], rhs=ohs[:, i0:i0 + n], start=True, stop=True), [tg])
            tg = S.op("dve", lambda e: e.tensor_copy(out=gv[0:1, i0:i0 + n], in_=MISC[0:1, 0:n]), [tmm])
        t_gs = S.dma("sp", "stf", gscr[:, :], gv, [tg])
        S.wait("sp", [t_gs])
        for ri in range(5):
            off = 512 - 128 * ri
            S.dma("sp", "ldc", mbf[:, ri * 512:(ri + 1) * 512], bass.AP(gscr.tensor, off, [[1, 128], [1, 512]]))
        th = S.dma("sp", "ldc", mbf[:, 5 * 512:NMB], bass.AP(gscr.tensor, 384, [[1, 128], [1, 128]]))
        t1 = S.op("dve", lambda e: e.tensor_tensor(out=mbf, in0=mbf, in1=mbm, op=ALU.add), [th])
        t2 = S.op("dve", lambda e: e.tensor_copy(out=mbhi[:], in_=mbf), [t1])
        t3 = S.op("dve", lambda e: e.tensor_tensor(out=mblo[:], in0=mbf, in1=mbhi[:], op=ALU.subtract), [t2])
        S.barrier(ALLST)

        xT_v = xT.rearrange("(k p) t -> p k t", p=128)
        state = {"xld": {}, "pe_xb": [None, None], "stage_free": [], "qt_free": [None, None]}

        def issue_xload(b):
            bb = b % 2
            state["xld"][b] = S.dma("pool", f"ldx{bb}", xb[bb][:], xT_v[:, :, b * 512:(b + 1) * 512], [state["pe_xb"][bb]])

        PJ = [LB, MISC]
        pj = {"pj": [None, None], "xsq": None, "rbc_r": [], "stage": [], "tmp4": None}

        def project_gen(b):
            bb = b % 2
            is_s = b >= NPB
            if b + 1 < NBLK:
                issue_xload(b + 1)
            ld = state["xld"].pop(b)
            tq = S.op("pool", lambda e: e.tensor_tensor(out=xsq[:], in0=xb[bb][:], in1=xb[bb][:], op=ALU.mult), [ld, pj["xsq"]])
            yield
            for k in range(8):
                tss = S.op("pe", lambda e: e.matmul(PJ[0][:], lhsT=oneb[:], rhs=xsq[:, k, :], start=(k == 0), stop=(k == 7)), [tq, pj["pj"][0]] if k == 0 else (), inc=(k == 7))
            pj["xsq"] = tss
            for k in range(8):
                tqm = S.op("pe", lambda e: e.matmul(PJ[1][:], lhsT=wq[:, k, :], rhs=xb[bb][:, k, :], start=(k == 0), stop=(k == 7)), [ld, pj["pj"][1]] if k == 0 else (), inc=(k == 7))
            yield
            yield
            t_ln = S.op("act", lambda e: e.activation(out=rbc[:], in_=PJ[0][:], func=AF.Ln, bias=epsT[:, 0:1], scale=1.0 / D), [tss] + pj["rbc_r"])
            pj["pj"][0] = t_ln
            t_r = S.op("act", lambda e: e.activation(out=rbc[:], in_=rbc[:], func=AF.Exp, scale=-0.5), [t_ln])
            yield
            for k in range(8):
                tkm = S.op("pe", lambda e: e.matmul(PJ[0][:], lhsT=wk[:, k, :], rhs=xb[bb][:, k, :], start=(k == 0), stop=(k == 7)), [pj["pj"][0]] if k == 0 else (), inc=(k == 7))
            yield
            qdst = QTS[:, (b - NPB) * 512:(b - NPB + 1) * 512] if is_s else QT[bb][:]
            t_q = S.op("dve", lambda e: e.tensor_tensor(out=qdst, in0=PJ[1][:], in1=rbc[:], op=ALU.mult), [tqm, t_r, state["qt_free"][bb]])
            pj["pj"][1] = t_q
            t_k = S.op("dve", lambda e: e.tensor_tensor(out=KT[:, b * 512:(b + 1) * 512], in0=PJ[0][:], in1=rbc[:], op=ALU.mult), [tkm])
            pj["pj"][0] = t_k
            yield
            for tt in range(4):
                trt = S.op("pe", lambda e: e.matmul(PJ[1][:, tt:tt + 1], lhsT=rbc[0:1, tt * 128:(tt + 1) * 128], rhs=one_f[0:1, 0:1], start=True, stop=True),
                           [t_r, pj["pj"][1]] if tt == 0 else (), inc=(tt == 3))
            pj["rbc_r"] = [t_k, trt]
            yield
            t_rt = S.op("dve", lambda e: e.tensor_copy(out=rtok[:], in_=PJ[1][:, 0:4]), [trt])
            pj["pj"][1] = t_rt
            t_ev = None
            ttm = None
            for tt in range(4):
                bank = PJ[tt % 2]
                for k in range(8):
                    ttm = S.op("pe", lambda e: e.matmul(bank[:, 0:321], lhsT=xb[bb][:, k, tt * 128:(tt + 1) * 128], rhs=wt[:, k, :], start=(k == 0), stop=(k == 7)),
                               [pj["pj"][tt % 2]] if k == 0 else (), inc=(k == 7))
                yield
                t_ev = S.op("dve", lambda e: e.tensor_scalar(out=stage[:, tt, :], in0=bank[:, 0:321], scalar1=rtok[:, tt:tt + 1], scalar2=None, op0=ALU.mult),
                            [ttm, t_rt] + (pj["stage"] if tt == 0 else []))
                pj["pj"][tt % 2] = t_ev
            state["pe_xb"][bb] = ttm
            yield
            t_e = S.op("act", lambda e: e.activation(out=tmp4[:], in_=stage[:, :, 320], func=AF.Exp, bias=negbf[:, 0:1], scale=-1.0), [t_ev, pj["tmp4"]])
            t_l = S.op("act", lambda e: e.activation(out=tmp4[:], in_=tmp4[:], func=AF.Ln, bias=1.0, scale=1.0), [t_e])
            rows = slice(b * 512, (b + 1) * 512)
            so = []
            so.append(S.dma("sp", "st", o_fk[rows, :].rearrange("(tt p) d -> p tt d", p=128), stage[:, :, 0:64], [t_ev]))
            so.append(S.dma("sp", "st", o_dk[rows, :].rearrange("(tt p) d -> p tt d", p=128), stage[:, :, 64:128]))
            so.append(S.dma("sp", "st", o_fv[rows, :].rearrange("(tt p) d -> p tt d", p=128), stage[:, :, 128:192]))
            so.append(S.dma("sp", "st", o_dv[rows, :].rearrange("(tt p) d -> p tt d", p=128), stage[:, :, 192:320]))
            yield
            t_lf = S.op("dve", lambda e: e.tensor_scalar(out=LF[:, 4 * b:4 * b + 4], in0=tmp4[:], scalar1=-1.0, scalar2=None, op0=ALU.mult), [t_l])
            pj["tmp4"] = t_lf
            yield
            tcm = S.op("pe", lambda e: e.matmul(PJ[0][:, 8:12], lhsT=(U2_f if is_s else U_f), rhs=LF[:, 4 * b:4 * b + 4], start=True, stop=True), [t_lf, pj["pj"][0]], inc=is_s)
            if not is_s:
                tcm = S.op("pe", lambda e: e.matmul(PJ[0][:, 16:20], lhsT=one_f, rhs=LF[:, 4 * b:4 * b + 4], start=True, stop=True))
                tv1 = S.op("pool", lambda e: e.tensor_copy(out=VA[:, 4 * b:4 * b + 4, 0:64], in_=stage[:, :, 128:192]), [t_ev])
                yield
                tcu = None
                for tt in range(4):
                    kk = 4 * b + tt
                    S.op("dve", lambda e: e.tensor_tensor(out=CUM[:, kk:kk + 1], in0=PJ[0][:, 8 + tt:9 + tt], in1=GT[:, kk:kk + 1], op=ALU.add), [tcm, tcu])
                    tcu = S.op("dve", lambda e: e.tensor_tensor(out=GT[:, kk + 1:kk + 2], in0=PJ[0][:, 16 + tt:17 + tt], in1=GT[:, kk:kk + 1], op=ALU.add))
                state["cum"] = tcu
                pj["pj"][0] = tcu
            else:
                sbi = b - NPB
                yield
                twn = S.op("act", lambda e: e.activation(out=WN[:, 4 * sbi:4 * sbi + 4], in_=PJ[0][:, 8:12], func=AF.Exp, scale=-1.0), [tcm])
                pj["pj"][0] = twn
                tv1 = S.op("dve", lambda e: e.tensor_tensor(out=VA[:, 4 * b:4 * b + 4, 0:64], in0=stage[:, :, 128:192],
                                                            in1=WN[:, 4 * sbi:4 * sbi + 4].unsqueeze(2).to_broadcast([128, 4, 64]), op=ALU.mult), [twn, t_ev])
                tv1 = S.op("dve", lambda e: e.tensor_copy(out=VA[:, 4 * b:4 * b + 4, 64:65], in_=WN[:, 4 * sbi:4 * sbi + 4].unsqueeze(2)), [tv1])
            tv2 = S.op("pool", lambda e: e.tensor_copy(out=VB[:, 4 * b:4 * b + 4, :], in_=stage[:, :, 192:320]), [t_ev, tv1])
            pj["stage"] = so + [tv1, tv2, t_e]
            state["kv"] = [t_k, tv1, tv2]
            state["tq"] = t_q

        def run_all(g):
            for _ in g:
                pass

        def finalize_gen(i, t_last_pe, par):
            col0 = i * 512
            tl_ = S.op("pe", lambda e: e.matmul(PJ[0][0:1, :], lhsT=one_f[:, 0:1], rhs=lacc[par][:], start=True, stop=True), [pst["lacc_last"], pj["pj"][0]])
            pst["lacc_free"][par] = tl_
            yield
            t1 = S.op("dve", lambda e: e.reciprocal(out=rl[64:65, :], in_=oasb[64:65, :]), [pst["fin_pe"], pst["fin_st"]])
            t4 = S.op("dve", lambda e: e.reciprocal(out=rl[0:1, :], in_=PJ[0][0:1, :]), [tl_])
            yield
            tm = S.op("pe", lambda e: e.matmul(PJ[1][0:64, :], lhsT=one_f[64:65, 0:64], rhs=rl[64:65, :], start=True, stop=True), [t1, pj["pj"][1]], inc=False)
            tm2 = S.op("pe", lambda e: e.matmul(PJ[0][:, :], lhsT=one_f[0:1, 0:128], rhs=rl[0:1, :], start=True, stop=True), [t4])
            pst["fin_pe"] = tm2
            yield
            yield
            t3 = S.op("dve", lambda e: e.tensor_tensor(out=outA[0:64, :], in0=oasb[0:64, :], in1=PJ[1][0:64, :], op=ALU.mult), [tm2])
            t6 = S.op("dve", lambda e: e.tensor_tensor(out=outB[:, :], in0=obsb[:, :], in1=PJ[0][:, :], op=ALU.mult))
            pj["pj"][0] = t6
            pj["pj"][1] = t6
            pst["sb_copy_free"] = t6
            S.dma("sp", "st", o_fo[:, col0:col0 + 512], outA[0:64, :], [t6])
            pst["fin_st"] = S.dma("sp", "st", o_dp[:, col0:col0 + 512], outB[:, :])
            yield

        def finalize(ncol, col0, t_last_pe, use_lacc=False):
            t1 = S.op("dve", lambda e: e.reciprocal(out=rl[64:65, 0:ncol], in_=OA[64:65, 0:ncol]), [t_last_pe, S.last("pe"), S.last("st")])
            t2 = S.op("dve", lambda e: e.tensor_copy(out=oasb[0:64, 0:ncol], in_=OA[0:64, 0:ncol]))
            tm = S.op("pe", lambda e: e.matmul(MISC[0:64, 0:ncol], lhsT=one_f[64:65, 0:64], rhs=rl[64:65, 0:ncol], start=True, stop=True), [t1, S.last("dve"), S.last("act")])
            t3 = S.op("dve", lambda e: e.tensor_tensor(out=outA[0:64, 0:ncol], in0=oasb[0:64, 0:ncol], in1=MISC[0:64, 0:ncol], op=ALU.mult), [tm])
            s1 = S.dma("sp", "st", o_fo[:, col0:col0 + ncol], outA[0:64, 0:ncol], [t3])
            if use_lacc:
                tl_ = S.op("pe", lambda e: e.matmul(LB[0:1, 0:ncol], lhsT=one_f[:, 0:1], rhs=lacc[:, 0:ncol], start=True, stop=True), [pst["lacc_last"]])
                pst["lacc_free"] = tl_
                t4 = S.op("dve", lambda e: e.reciprocal(out=rl[0:1, 0:ncol], in_=LB[0:1, 0:ncol]), [tl_])
            else:
                t4 = S.op("dve", lambda e: e.reciprocal(out=rl[0:1, 0:ncol], in_=LB[0:1, 0:ncol]))
            t5 = S.op("dve", lambda e: e.tensor_copy(out=obsb[:, 0:ncol], in_=OB[:, 0:ncol]))
            tm2 = S.op("pe", lambda e: e.matmul(MISC[:, 0:ncol], lhsT=one_f[0:1, 0:128], rhs=rl[0:1, 0:ncol], start=True, stop=True), [t4, t3])
            t6 = S.op("dve", lambda e: e.tensor_tensor(out=outB[:, 0:ncol], in0=obsb[:, 0:ncol], in1=MISC[:, 0:ncol], op=ALU.mult), [tm2])
            s2 = S.dma("sp", "st", o_dp[:, col0:col0 + ncol], outB[:, 0:ncol], [t6])
            return t6

        pst = {"sa_free": [None, None], "sb_free": [None, None], "pa_free": [None, None], "pb_free": [None, None], "pb_free2": [None, None], "acc_free": None, "lacc_free": [None, None], "lacc_last": None, "fin_pe": None, "fin_st": None, "sb_copy_free": None}

        def attention(i, t_q, gens):
            qb = i % 2
            lp = i % 2
            nj = 4 * i + 4
            ab = i % 2
            t_arg = S.op("dve", lambda e: e.tensor_scalar(out=arg[ab][:, 0:nj], in0=CUM[:, 0:nj], scalar1=-1.0, scalar2=GT[:, 4 * i + 2:4 * i + 3], op0=ALU.mult, op1=ALU.add),
                         [state["cum"], S.last("act")])
            kv = state["kv"] + [t_q]

            def qk(j):
                u = j % 2
                r = j - 4 * i
                last = r < 0
                S.op("pe", lambda e: e.matmul(SA[u][:], lhsT=KT[0:64, j * 128:(j + 1) * 128], rhs=QT[qb][0:64, :], start=True, stop=last),
                     kv + [pst["sa_free"][u]], inc=False)
                if r >= 0:
                    S.op("pe", lambda e: e.matmul(SA[u][:], lhsT=idb[:], rhs=maskA[:, r * 512:(r + 1) * 512], start=False, stop=True), inc=False)
                lastb = r < -1
                S.op("pe", lambda e: e.matmul(SB_[u][:], lhsT=KT[64:128, j * 128:(j + 1) * 128], rhs=QT[qb][64:128, :], start=True, stop=lastb),
                     [pst["sb_free"][u]], inc=lastb)
                if r >= -1:
                    ri = r + 1
                    S.op("pe", lambda e: e.matmul(SB_[u][:], lhsT=jb[:], rhs=mbhi[:, ri * 512:(ri + 1) * 512], start=False, stop=False), inc=False)
                    S.op("pe", lambda e: e.matmul(SB_[u][:], lhsT=jb[:], rhs=mblo[:, ri * 512:(ri + 1) * 512], start=False, stop=True))
                return S.last("pe")

            def ex(j, tqk):
                u = j % 2
                ta = S.op("act", lambda e: e.activation(out=PA[u][:], in_=SA[u][:], func=AF.Exp, bias=arg[ab][:, j:j + 1], scale=1.0), [tqk, t_arg, pst["pa_free"][u]])
                tb = S.op("act", lambda e: e.activation(out=PB[u][:], in_=SB_[u][:], func=AF.Exp), [pst["pb_free"][u], pst["pb_free2"][u]])
                pst["sa_free"][u] = ta
                pst["sb_free"][u] = tb
                return tb

            def pv(j, tex):
                u = j % 2
                first = j == 0
                lastj = j == nj - 1
                S.op("pe", lambda e: e.matmul(OA[0:65, :], lhsT=VA[:, j, :], rhs=PA[u][:], start=first, stop=lastj), [tex, pst["acc_free"]] if first else [tex], inc=False)
                t = S.op("pe", lambda e: e.matmul(OB[:, :], lhsT=VB[:, j, :], rhs=PB[u][:], start=first, stop=lastj))
                if first:
                    td = S.op("dve", lambda e: e.tensor_copy(out=lacc[lp][:], in_=PB[u][:]), [tex, pst["lacc_free"][lp]])
                else:
                    td = S.op("dve", lambda e: e.tensor_tensor(out=lacc[lp][:], in0=lacc[lp][:], in1=PB[u][:], op=ALU.add), [tex])
                pst["pa_free"][u] = t
                pst["pb_free"][u] = td
                pst["pb_free2"][u] = t
                pst["lacc_last"] = td
                return t

            tqk = {0: qk(0)}
            tl = None
            gi = 0
            for j in range(nj):
                if j + 1 < nj:
                    tqk[j + 1] = qk(j + 1)
                tex = ex(j, tqk.pop(j))
                tl = pv(j, tex)
                while gi < len(gens):
                    try:
                        next(gens[gi])
                        break
                    except StopIteration:
                        gi += 1
            for g in gens[gi:]:
                run_all(g)
            state["qt_free"][qb] = tl
            tc1 = S.op("dve", lambda e: e.tensor_copy(out=oasb[0:65, :], in_=OA[0:65, :]), [tl, pst["sb_copy_free"]])
            tc2 = S.op("dve", lambda e: e.tensor_copy(out=obsb[:, :], in_=OB[:, :]))
            pst["acc_free"] = tc2
            return finalize_gen(i, tl, lp)

        issue_xload(0)
        run_all(project_gen(0))
        fin = None
        for b in range(NPB):
            gens = ([fin] if fin is not None else []) + [project_gen(b + 1)]
            fin = attention(b, state["tq"], gens)
        run_all(fin)
        for b in range(NPB + 1, NBLK):
            run_all(project_gen(b))
        S.barrier(ALLST)

        t_cl = S.dma("sp", "ldc", clf[:], clogf[:, :])
        for hh in range(2):
            tmm = S.op("pe", lambda e: e.matmul(SA[hh][:], lhsT=U_f, rhs=clf[:, hh * 512:(hh + 1) * 512], start=True, stop=True), [t_cl])
            S.op("dve", lambda e: e.tensor_copy(out=Cc[:, hh * 512:(hh + 1) * 512], in_=SA[hh][:]), [tmm])
        tprev = None
        for q8 in range(8):
            tmm = S.op("pe", lambda e: e.matmul(SB_[0][:, 0:128], lhsT=clf[:, q8 * 128:(q8 + 1) * 128], rhs=one_f, start=True, stop=True), [tprev])
            tcp = S.op("dve", lambda e: e.tensor_copy(out=totT[:], in_=SB_[0][:, 0:128]), [tmm, tprev])
            tmm2 = S.op("pe", lambda e: e.matmul(SB_[1][:, 0:128], lhsT=totT[:], rhs=BL_f, start=True, stop=True), [tcp])
            tprev = S.op("dve", lambda e: e.tensor_tensor(out=Wc[:, q8 * 128:(q8 + 1) * 128], in0=SB_[1][:, 0:128], in1=Cc[:, q8 * 128:(q8 + 1) * 128], op=ALU.subtract), [tmm2])
        t_wc = S.op("act", lambda e: e.activation(out=Wc[:], in_=Wc[:], func=AF.Exp), [tprev])
        S.barrier(ALLST)

        KTc = [KT[:, 0:PAST], KT[:, PAST:2 * PAST]]
        VBc = [VB[:, 0:32, :], VB[:, 32:64, :]]
        VAc = [VA[:, 0:32, :], VA[:, 32:64, :]]
        cst_free = [None, None]
        ldtok = {}

        def load_seq(n):
            cb = n % 2
            d = [cst_free[cb]]
            a = S.dma("pool", f"ldk{cb}", KTc[cb][0:64, :], cfkT[n, :, :], d)
            a = S.dma("pool", f"ldk{cb}", KTc[cb][64:128, :], cdkT[n, :, :])
            b_ = S.dma("pool", f"ldvb{cb}", VBc[cb], cdv[n, :, :].rearrange("(j p) d -> p j d", p=128))
            c_ = S.dma("sp", f"ldva{cb}", VAf[cb][:], cfv[n, :, :].rearrange("(j p) d -> p j d", p=128), d)
            ldtok[n] = [a, b_, c_]

        redt = sb("redt", [128, 128], F32)
        red = [redt[:, 0:64], redt[:, 64:128]]
        load_seq(0)
        sa_free = [None, None]
        pa_free = [[], []]
        sm = {"acc_free": None, "seq": {}, "lacc_free": [None, None], "red_free": [None, None]}
        items = []
        for m in range(16):
            items.append(("new", m, 0, 0))
            for hh in range(2):
                for g in range(4):
                    items.append(("grp", m, hh, g))

        def s_qk(it, u):
            kind, m, hh, g = it
            tt = NPT + m
            if kind == "new":
                S.op("pe", lambda e: e.matmul(SA[u][:, 0:128], lhsT=KT[0:64, tt * 128:(tt + 1) * 128], rhs=QTS[0:64, m * 128:(m + 1) * 128], start=True, stop=False), [sa_free[u]], inc=False)
                S.op("pe", lambda e: e.matmul(SB_[u][:, 0:128], lhsT=KT[64:128, tt * 128:(tt + 1) * 128], rhs=QTS[64:128, m * 128:(m + 1) * 128], start=True, stop=False), inc=False)
                S.op("pe", lambda e: e.matmul(SA[u][:, 0:128], lhsT=idb[:], rhs=maskA[:, 4 * 512:4 * 512 + 128], start=False, stop=True), inc=False)
                S.op("pe", lambda e: e.matmul(SB_[u][:, 0:128], lhsT=jb[:], rhs=mbhi[:, 5 * 512:5 * 512 + 128], start=False, stop=False), inc=False)
                return S.op("pe", lambda e: e.matmul(SB_[u][:, 0:128], lhsT=jb[:], rhs=mblo[:, 5 * 512:5 * 512 + 128], start=False, stop=True))
            n = 2 * m + hh
            cb = n % 2
            if g == 0:
                lda, ldb_, ldc_ = ldtok.pop(n)
                tva = S.op("dve", lambda e: e.tensor_tensor(out=VAc[cb][:, :, 0:64], in0=VAf[cb][:], in1=Wc[:, n * 32:(n + 1) * 32].unsqueeze(2).to_broadcast([128, 32, 64]), op=ALU.mult), [ldc_, t_wc])
                tva = S.op("dve", lambda e: e.tensor_copy(out=VAc[cb][:, :, 64:65], in_=Wc[:, n * 32:(n + 1) * 32].unsqueeze(2)))
                sm["seq"][n] = (lda, ldb_, tva)
            lda, ldb_, tva = sm["seq"][n]
            qcols = slice(m * 128 + hh * 64, m * 128 + hh * 64 + 64)
            tqk = None
            for jj in range(8):
                j = 8 * g + jj
                cs = slice(jj * 64, (jj + 1) * 64)
                S.op("pe", lambda e: e.matmul(SA[u][:, cs], lhsT=KTc[cb][0:64, j * 128:(j + 1) * 128], rhs=QTS[0:64, qcols], start=True, stop=True),
                     [lda, sa_free[u]] if jj == 0 else (), inc=False)
                lastb = j != 31
                tqk = S.op("pe", lambda e: e.matmul(SB_[u][:, cs], lhsT=KTc[cb][64:128, j * 128:(j + 1) * 128], rhs=QTS[64:128, qcols], start=True, stop=lastb), inc=(jj == 7 and lastb))
                if j == 31:
                    S.op("pe", lambda e: e.matmul(SB_[u][:, cs], lhsT=jb[:], rhs=mbhi[:, 0:64], start=False, stop=False), inc=False)
                    tqk = S.op("pe", lambda e: e.matmul(SB_[u][:, cs], lhsT=jb[:], rhs=mblo[:, 0:64], start=False, stop=True))
            return tqk

        def s_ex(it, u, tqk):
            ncol = 128 if it[0] == "new" else 512
            S.op("act", lambda e: e.activation(out=PA[u][:, 0:ncol], in_=SA[u][:, 0:ncol], func=AF.Exp), [tqk] + pa_free[u], inc=False)
            tb = S.op("act", lambda e: e.activation(out=PB[u][:, 0:ncol], in_=SB_[u][:, 0:ncol], func=AF.Exp))
            sa_free[u] = tb
            return tb

        def s_pv(it, u, tex):
            kind, m, hh, g = it
            tt = NPT + m
            if kind == "new":
                S.op("pe", lambda e: e.matmul(OA[0:65, 0:128], lhsT=VA[:, tt, :], rhs=PA[u][:, 0:128], start=True, stop=False), [tex, sm["acc_free"]], inc=False)
                S.op("pe", lambda e: e.matmul(OB[:, 0:128], lhsT=VB[:, tt, :], rhs=PB[u][:, 0:128], start=True, stop=False), inc=False)
                tpv = S.op("pe", lambda e: e.matmul(LB[0:1, 0:128], lhsT=oneb[:, 0:1], rhs=PB[u][:, 0:128], start=True, stop=False))
                pa_free[u] = [tpv]
                return
            n = 2 * m + hh
            cb = n % 2
            lp = n % 2
            lda, ldb_, tva = sm["seq"][n]
            ocols = slice(hh * 64, hh * 64 + 64)
            tpv = None
            for jj in range(8):
                j = 8 * g + jj
                cs = slice(jj * 64, (jj + 1) * 64)
                lastj = (j == 31)
                S.op("pe", lambda e: e.matmul(OA[0:65, ocols], lhsT=VAc[cb][:, j, :], rhs=PA[u][:, cs], start=False, stop=lastj, skip_group_check=True),
                     [tex, tva, ldb_] if jj == 0 else (), inc=False)
                tpv = S.op("pe", lambda e: e.matmul(OB[:, ocols], lhsT=VBc[cb][:, j, :], rhs=PB[u][:, cs], start=False, stop=lastj, skip_group_check=True), inc=(jj == 7))
            if g == 0:
                if n + 1 < NSEQ:
                    load_seq(n + 1)
                td = S.op("dve", lambda e: e.tensor_copy(out=lacc[lp][:], in_=PB[u][:]), [tex, sm["lacc_free"][lp]])
            else:
                td = S.op("dve", lambda e: e.tensor_tensor(out=lacc[lp][:], in0=lacc[lp][:], in1=PB[u][:], op=ALU.add), [tex])
            pa_free[u] = [tpv, td]
            if g == 3:
                tr = S.op("dve", lambda e: e.tensor_reduce(out=red[lp], in_=lacc[lp][:].rearrange("p (j q) -> p q j", q=64), axis=mybir.AxisListType.X, op=ALU.add), [td, sm["red_free"][lp]])
                sm["lacc_free"][lp] = tr
                tpv = S.op("pe", lambda e: e.matmul(LB[0:1, ocols], lhsT=one_f[:, 0:1], rhs=red[lp], start=False, stop=True, skip_group_check=True), [tr])
                sm["red_free"][lp] = tpv
                cst_free[cb] = tpv
                del sm["seq"][n]
                if hh == 1:
                    sm["acc_free"] = finalize(128, TP + m * 128, tpv)

        tq = s_qk(items[0], 0)
        for idx, it in enumerate(items):
            u = idx % 2
            tq_next = s_qk(items[idx + 1], (idx + 1) % 2) if idx + 1 < len(items) else None
            tex = s_ex(it, u, tq)
            s_pv(it, u, tex)
            tq = tq_next

        for c0, n in ((0, 128), (128, 16)):
            tmm = S.op("pe", lambda e: e.matmul(MISC[0:n, 0:128], lhsT=LF[:, c0:c0 + n], rhs=ident_f, start=True, stop=True), [S.last("dve"), S.last("act")])
            tcp = S.op("dve", lambda e: e.tensor_copy(out=rl[0:n, 0:128], in_=MISC[0:n, 0:128]), [tmm, S.last("st")])
            S.dma("sp", "st", o_lf[c0:c0 + n, :], rl[0:n, 0:128], [tcp])
        S.barrier(ALLST)
    return nc


def build_C():
    nc = bass.Bass("TRN2", target_bir_lowering=False)

    def din(name, shape):
        return nc.dram_tensor(name, shape, F32, kind="ExternalInput").ap()

    AT = din("AT", [1536, TC])
    xTc = din("xTc", [D, TC])
    xtok = din("xtok", [TC, D])
    wg = din("wg", [D, 1024])
    wo = din("wo", [1024, D])
    gpre = din("gpre", [128, 8])
    gpost = din("gpost", [128, D])
    sgin = din("sg", [128, 1])
    lamv = din("lamv", [128, 256])
    consts = din("consts", [128, NCONST])
    yc = nc.dram_tensor("yc", [TC, D], F32, kind="ExternalOutput").ap()

    es = ExitStack()
    with es:
        def sb(name, shape, dt):
            return es.enter_context(nc.sbuf_tensor(name, shape, dt))

        sem_names = ["act", "dve", "pool", "pe", "sp", "ldc", "ldx", "ldx0", "ldx1", "lda0", "lda1", "ldt0", "ldt1", "st"]
        ALLST = [k for k in sem_names if k not in ("act", "dve", "pool", "pe", "sp")]
        sems = {k: es.enter_context(nc.semaphore("s_" + k)) for k in sem_names}
        S = Sched(nc, sems)
        PS = [es.enter_context(nc.psum_tensor(f"ps{i}", [128, 512], F32)) for i in range(8)]

        onef = sb("onef", [128, 128], F32)
        oneb = sb("oneb", [128, 128], BF16)
        wgs = sb("wgs", [128, 8, 1024], F32)
        wgb = sb("wgb", [128, 8, 1024], BF16)
        wob = sb("wob", [128, 8, 1024], BF16)
        gp = sb("gp", [128, 8], F32)
        gpo = sb("gpo", [128, D], F32)
        sgl = sb("sgl", [128, 1], F32)
        lv = sb("lv", [128, 256], F32)
        ltmp = sb("ltmp", [128, 128], F32)
        lsum = sb("lsum", [128, 2], F32)
        neglam = sb("neglam", [128, 1], F32)
        epsT = sb("epsT", [128, 1], F32)
        xb = [sb(f"xb{i}", [128, 8, CB], BF16) for i in range(2)]
        xsq = sb("xsq", [128, 8, CB], BF16)
        rbc = sb("rbc", [128, CB], F32)
        at = [sb(f"at{i}", [128, 12, CB], F32) for i in range(2)]
        gsb = [sb(f"gsb{i}", [128, CB], F32) for i in range(2)]
        esb = [sb(f"esb{i}", [128, CB], F32) for i in range(2)]
        sil = sb("sil", [128, 8, CB], F32)
        dsb = sb("dsb", [128, CB], F32)
        dsq = sb("dsq", [128, CB], F32)
        rs = sb("rs", [128, CB], F32)
        mixT = sb("mixT", [128, 8, CB], BF16)
        xt = [sb(f"xt{i}", [128, D], F32) for i in range(2)]
        sq = [sb(f"sq{i}", [128, D], F32) for i in range(2)]
        ysb = [sb(f"ysb{i}", [128, D], F32) for i in range(2)]
        ssq = [sb(f"ssq{i}", [128, 1], F32) for i in range(2)]
        rstd = [sb(f"rstd{i}", [128, 1], F32) for i in range(2)]

        S.dma("sp", "ldc", onef[:], consts[:, C_ONE:C_ONE + 128])
        S.dma("pool", "ldx", oneb[:], consts[:, C_ONE:C_ONE + 128])
        S.dma("sp", "ldc", wgs[:], wg.rearrange("(k p) c -> p k c", p=128))
        S.dma("pool", "ldx", wob[:], wo.rearrange("(k p) c -> p k c", p=128))
        S.dma("sp", "ldc", gp[:], gpre[:, :])
        S.dma("sp", "ldc", gpo[:], gpost[:, :])
        S.dma("sp", "ldc", sgl[:], sgin[:, :])
        S.dma("sp", "ldc", lv[:], lamv[:, :])
        S.op("pool", lambda e: e.memset(epsT[:], EPS))
        S.barrier(ALLST)
        for k in range(8):
            S.op("dve", lambda e: e.tensor_scalar(out=wgb[:, k, :], in0=wgs[:, k, :], scalar1=gp[:, k:k + 1], scalar2=None, op0=ALU.mult))
        S.op("dve", lambda e: e.tensor_scalar(out=sgl[:], in0=sgl[:], scalar1=0.5 * (1.0 - LAM_INIT), scalar2=None, op0=ALU.mult), [S.last("dve")])
        t = S.op("dve", lambda e: e.tensor_tensor(out=ltmp[:, 0:64], in0=lv[:, 0:64], in1=lv[:, 64:128], op=ALU.mult))
        t = S.op("dve", lambda e: e.tensor_tensor(out=ltmp[:, 64:128], in0=lv[:, 128:192], in1=lv[:, 192:256], op=ALU.mult))
        t = S.op("dve", lambda e: e.reduce_sum(out=lsum[:, 0:1], in_=ltmp[:, 0:64], axis=mybir.AxisListType.X), [t])
        t = S.op("dve", lambda e: e.reduce_sum(out=lsum[:, 1:2], in_=ltmp[:, 64:128], axis=mybir.AxisListType.X), [t])
        t = S.op("act", lambda e: e.activation(out=lsum[:], in_=lsum[:], func=AF.Exp), [t])
        t = S.op("dve", lambda e: e.tensor_tensor(out=neglam[:], in0=lsum[:, 1:2], in1=lsum[:, 0:1], op=ALU.subtract), [t])
        t = S.op("dve", lambda e: e.tensor_scalar(out=neglam[:], in0=neglam[:], scalar1=-LAM_INIT, scalar2=None, op0=ALU.add), [t])
        S.barrier(ALLST)

        xT_v = xTc.rearrange("(k p) t -> p k t", p=128)
        AT_v = AT.rearrange("(c p) t -> p c t", p=128)
        OP = [[PS[4], PS[5]], [PS[6], PS[7]]]
        fr = {"xb": [None, None], "at": [None, None], "xsq": None, "ps0": None, "rbc": None, "gps": [None, None], "gsb": [None, None],
              "esb": [None, None], "sil": None, "mix": None, "ps3": None, "rs": None, "dsb": None, "op": [None, None], "sq": [None, None],
              "ysb": [None, None], "xt": [None, None]}
        ld = {}

        def issue_loads(blk):
            pb = blk % 2
            cs = slice(blk * CB, (blk + 1) * CB)
            ld[blk] = (S.dma("pool", f"ldx{pb}", xb[pb][:], xT_v[:, :, cs], [fr["xb"][pb]]),
                       S.dma("sp", f"lda{pb}", at[pb][:], AT_v[:, :, cs], [fr["at"][pb]]))

        issue_loads(0)
        tile_idx = 0
        for blk in range(NCB):
            pb = blk % 2
            if blk + 1 < NCB:
                issue_loads(blk + 1)
            tlx, tla = ld.pop(blk)
            tq = S.op("pool", lambda e: e.tensor_tensor(out=xsq[:], in0=xb[pb][:], in1=xb[pb][:], op=ALU.mult), [tlx, fr["xsq"]])
            for k in range(8):
                tss = S.op("pe", lambda e: e.matmul(PS[0][:, 0:CB], lhsT=oneb[:], rhs=xsq[:, k, :], start=(k == 0), stop=(k == 7)), [tq, fr["ps0"]] if k == 0 else (), inc=(k == 7))
            fr["xsq"] = tss
            t = S.op("act", lambda e: e.activation(out=rbc[:], in_=PS[0][:, 0:CB], func=AF.Ln, bias=epsT[:, 0:1], scale=1.0 / D), [tss, fr["rbc"]])
            fr["ps0"] = t
            t_r = S.op("act", lambda e: e.activation(out=rbc[:], in_=rbc[:], func=AF.Exp, scale=-0.5), [t])
            t_mix = []
            for c8 in range(8):
                par = c8 % 2
                gps = PS[1 + par]
                for k in range(8):
                    tg = S.op("pe", lambda e: e.matmul(gps[:, 0:CB], lhsT=wgb[:, k, c8 * 128:(c8 + 1) * 128], rhs=xb[pb][:, k, :], start=(k == 0), stop=(k == 7)),
                              [tlx, fr["gps"][par]] if k == 0 else (), inc=(k == 7))
                t_g = S.op("dve", lambda e: e.tensor_tensor(out=gsb[par][:], in0=gps[:, 0:CB], in1=rbc[:], op=ALU.mult), [tg, t_r, fr["gsb"][par]])
                fr["gps"][par] = t_g
                t_th = S.op("act", lambda e: e.activation(out=esb[par][:], in_=gsb[par][:], func=AF.Tanh, scale=0.5), [t_g, fr["esb"][par]])
                t_s2 = S.op("dve", lambda e: e.scalar_tensor_tensor(out=sil[:, c8, :], in0=esb[par][:], scalar=1.0, in1=gsb[par][:], op0=ALU.add, op1=ALU.mult), [t_th, fr["sil"]])
                fr["gsb"][par] = t_s2
                fr["esb"][par] = t_s2
                if c8 < 4:
                    t_mix.append(S.op("dve", lambda e: e.scalar_tensor_tensor(out=mixT[:, c8, :], in0=sil[:, c8, :], scalar=0.5, in1=at[pb][:, c8, :], op0=ALU.mult, op1=ALU.mult), [t_s2, tla, fr["mix"]]))
            fr["xb"][pb] = tg
            fr["rbc"] = t_s2
            for h in range(4):
                c8 = 4 + h
                t = S.op("dve", lambda e: e.scalar_tensor_tensor(out=dsb[:], in0=at[pb][:, 4 + 2 * h + 1, :], scalar=neglam[:, 0:1], in1=at[pb][:, 4 + 2 * h, :], op0=ALU.mult, op1=ALU.add), [tla, fr["dsb"]])
                t = S.op("dve", lambda e: e.tensor_tensor(out=dsq[:], in0=dsb[:], in1=dsb[:], op=ALU.mult), [t, fr["ps3"]])
                tm = S.op("pe", lambda e: e.matmul(PS[3][:, 0:CB], lhsT=onef[:], rhs=dsq[:], start=True, stop=True), [t, fr["rs"]])
                fr["ps3"] = tm
                t = S.op("act", lambda e: e.activation(out=rs[:], in_=PS[3][:, 0:CB], func=AF.Ln, bias=epsT[:, 0:1], scale=1.0 / 128.0), [tm, fr["dsb"]])
                t = S.op("act", lambda e: e.activation(out=rs[:], in_=rs[:], func=AF.Exp, scale=-0.5), [t])
                fr["rs"] = t
                t = S.op("dve", lambda e: e.scalar_tensor_tensor(out=dsb[:], in0=dsb[:], scalar=sgl[:, 0:1], in1=rs[:], op0=ALU.mult, op1=ALU.mult), [t])
                t = S.op("dve", lambda e: e.tensor_tensor(out=mixT[:, c8, :], in0=dsb[:], in1=sil[:, c8, :], op=ALU.mult), [t, fr["mix"]])
                fr["dsb"] = t
                fr["rs"] = t
                t_mix.append(t)
            fr["at"][pb] = t
            fr["sil"] = t
            t_mixall = t
            for tt in range(CB // 128):
                q = tile_idx % 2
                tile_idx += 1
                r0 = blk * CB + tt * 128
                tlt = S.dma("sp", f"ldt{q}", xt[q][:], xtok[r0:r0 + 128, :], [fr["xt"][q]])
                for half in range(2):
                    for c8 in range(8):
                        to = S.op("pe", lambda e: e.matmul(OP[q][half][:], lhsT=mixT[:, c8, tt * 128:(tt + 1) * 128], rhs=wob[:, c8, half * 512:(half + 1) * 512], start=(c8 == 0), stop=(c8 == 7)),
                                  [t_mixall, fr["op"][q]] if (c8 == 0 and half == 0) else (), inc=(c8 == 7))
                for half in range(2):
                    t = S.op("act", lambda e: e.activation(out=sq[q][:, half * 512:(half + 1) * 512], in_=OP[q][half][:], func=AF.Square), [to, fr["sq"][q]])
                t = S.op("dve", lambda e: e.reduce_sum(out=ssq[q][:], in_=sq[q][:], axis=mybir.AxisListType.X), [t])
                fr["sq"][q] = t
                t = S.op("act", lambda e: e.activation(out=rstd[q][:], in_=ssq[q][:], func=AF.Ln, bias=epsT[:, 0:1], scale=1.0 / D), [t])
                t = S.op("act", lambda e: e.activation(out=rstd[q][:], in_=rstd[q][:], func=AF.Exp, scale=-0.5), [t])
                for half in range(2):
                    t = S.op("dve", lambda e: e.scalar_tensor_tensor(out=ysb[q][:, half * 512:(half + 1) * 512], in0=OP[q][half][:], scalar=rstd[q][:, 0:1], in1=gpo[:, half * 512:(half + 1) * 512], op0=ALU.mult, op1=ALU.mult),
                             [t, fr["ysb"][q]])
                fr["op"][q] = t
                t = S.op("pool", lambda e: e.tensor_tensor(out=ysb[q][:], in0=ysb[q][:], in1=xt[q][:], op=ALU.add), [t, tlt])
                fr["xt"][q] = t
                fr["ysb"][q] = S.dma("sp", "st", yc[r0:r0 + 128, :], ysb[q][:], [t])
            fr["mix"] = to
        S.barrier(ALLST)
    return nc


_CACHE = {}
O_FQ, O_FK, O_FV, O_FF, O_FG, O_DQ, O_DK, O_DV, O_DG = 0, 512, 1024, 1536, 1544, 2056, 2568, 3080, 3592


def _prep_A(inp):
    xp = np.asarray(inp["x_prompt"], np.float32)[0]
    xs = np.asarray(inp["x_sample"], np.float32).reshape(TS, D)
    xT = np.ascontiguousarray(np.concatenate([xp, xs], axis=0).T)
    w = np.asarray(inp["w_in"], np.float32)[0]
    consts, oh = make_consts()
    gpre = np.ascontiguousarray(np.asarray(inp["norm_pre_g"], np.float32)[0].reshape(8, 128).T)
    fb = np.asarray(inp["forget_bias"], np.float32)[0]
    rb = np.asarray(inp["rel_bias"], np.float32)
    cfk = np.asarray(inp["cache_fox_k"], np.float32)[0]
    cfv = np.asarray(inp["cache_fox_v"], np.float32)[0]
    clf = np.asarray(inp["cache_fox_logf"], np.float32)[0]
    cdk = np.asarray(inp["cache_diff_k"], np.float32)[0]
    cdv = np.asarray(inp["cache_diff_v"], np.float32)[0]
    maps = []
    for c in range(NCORES):
        h, part = c // 2, c % 2
        fkc = np.arange(O_FK + 64 * c, O_FK + 64 * c + 64)
        dkc = np.arange(O_DK + 128 * h + 64 * part, O_DK + 128 * h + 64 * part + 64)
        cols = np.concatenate([
            np.arange(O_FQ + 64 * c, O_FQ + 64 * c + 64),
            np.arange(O_DQ + 128 * h + 64 * part, O_DQ + 128 * h + 64 * part + 64),
            fkc, dkc, fkc, dkc,
            np.arange(O_FV + 64 * c, O_FV + 64 * c + 64),
            np.arange(O_DV + 128 * h, O_DV + 128 * h + 128),
            np.arange(O_FF + c, O_FF + c + 1)])
        maps.append({
            "xT": xT,
            "wA": np.ascontiguousarray(w[:, cols]),
            "gpre": gpre,
            "bfc": np.full((128, 1), fb[c], np.float32),
            "rbh": np.ascontiguousarray(rb[:, h:h + 1]),
            "consts": consts,
            "oh": oh,
            "cfkT": np.ascontiguousarray(cfk[:, :, c, :].transpose(0, 2, 1)),
            "cdkT": np.ascontiguousarray(cdk[:, :, h, 64 * part:64 * part + 64].transpose(0, 2, 1)),
            "cfv": np.ascontiguousarray(cfv[:, :, c, :]),
            "cdv": np.ascontiguousarray(cdv[:, :, h, :]),
            "clogf": np.ascontiguousarray(clf[:, :, c].reshape(NSEQ, 32, 128).transpose(2, 0, 1).reshape(128, NSEQ * 32)),
        })
    return maps, xT


def run_A(inp):
    if "A" not in _CACHE:
        _CACHE["A"] = build_A()
    maps, xT = _prep_A(inp)
    res = run_bass_kernel_spmd(_CACHE["A"], maps, core_ids=list(range(NCORES)))
    return res.results, xT


def _tok_cols(c):
    return np.concatenate([np.arange(2048 * c, 2048 * c + 2048), np.arange(TP + 256 * c, TP + 256 * c + 256)])


def run_C(inp, resA, xT):
    if "C" not in _CACHE:
        _CACHE["C"] = build_C()
    w = np.asarray(inp["w_in"], np.float32)[0]
    wg = np.ascontiguousarray(np.concatenate([w[:, O_FG:O_FG + 512], w[:, O_DG:O_DG + 512]], axis=1))
    wo = np.ascontiguousarray(np.asarray(inp["w_out"], np.float32)[0])
    consts, _ = make_consts()
    gpre = np.ascontiguousarray(np.asarray(inp["norm_pre_g"], np.float32)[0].reshape(8, 128).T)
    gpost = np.ascontiguousarray(np.broadcast_to(np.asarray(inp["norm_post_g"], np.float32)[0][None, :], (128, D)))
    sg = np.ascontiguousarray(np.asarray(inp["subln_g"], np.float32)[0].reshape(128, 1))
    lamv = np.concatenate([np.asarray(inp[k], np.float32)[0] for k in ("lambda_q1", "lambda_k1", "lambda_q2", "lambda_k2")])
    lamv = np.ascontiguousarray(np.broadcast_to(lamv[None, :], (128, 256)))
    ATfull = np.concatenate([resA[c]["o_fo"] for c in range(NCORES)] + [resA[c]["o_dp"] for c in range(NCORES)], axis=0)
    maps = []
    for c in range(NCORES):
        tc = _tok_cols(c)
        maps.append({
            "AT": np.ascontiguousarray(ATfull[:, tc]),
            "xTc": np.ascontiguousarray(xT[:, tc]),
            "xtok": np.ascontiguousarray(xT[:, tc].T),
            "wg": wg, "wo": wo, "gpre": gpre, "gpost": gpost, "sg": sg, "lamv": lamv, "consts": consts,
        })
    res = run_bass_kernel_spmd(_CACHE["C"], maps, core_ids=list(range(NCORES)))
    return res.results


def kernel(**inp):
    resA, xT = run_A(inp)
    resC = run_C(inp, resA, xT)
    y_p = np.concatenate([resC[c]["yc"][:2048] for c in range(NCORES)], axis=0).reshape(1, TP, D)
    y_s = np.concatenate([resC[c]["yc"][2048:] for c in range(NCORES)], axis=0).reshape(NSEQ, SQ, D)

    def heads(key, cores, width):
        a = np.stack([resA[c][key] for c in cores], axis=1)
        return a

    fk = heads("o_fk", range(8), 64)
    fv = heads("o_fv", range(8), 64)
    lf = np.stack([resA[c]["o_lf"].reshape(T) for c in range(8)], axis=1)
    dk = heads("o_dk", range(8), 64).reshape(T, 4, 128)
    dv = heads("o_dv", range(0, 8, 2), 128)
    f32 = np.float32
    outs = (y_p.astype(f32), y_s.astype(f32),
            fk[:TP].reshape(1, 1, TP, 8, 64), fv[:TP].reshape(1, 1, TP, 8, 64), lf[:TP].reshape(1, 1, TP, 8),
            dk[:TP].reshape(1, 1, TP, 4, 128), dv[:TP].reshape(1, 1, TP, 4, 128),
            fk[TP:].reshape(1, NSEQ, SQ, 8, 64), fv[TP:].reshape(1, NSEQ, SQ, 8, 64), lf[TP:].reshape(1, NSEQ, SQ, 8),
            dk[TP:].reshape(1, NSEQ, SQ, 4, 128), dv[TP:].reshape(1, NSEQ, SQ, 4, 128))
    return tuple(np.ascontiguousarray(o, dtype=f32) for o in outs)
```

```python
import math
from contextlib import ExitStack

import numpy as np
import concourse.bass as bass
import concourse.mybir as mybir
from concourse.bass_utils import run_bass_kernel_spmd

F32 = mybir.dt.float32
BF16 = mybir.dt.bfloat16
AF = mybir.ActivationFunctionType
ALU = mybir.AluOpType

NCORES = 8
D = 1024
TP = 16384
NSEQ = 32
SQ = 64
PAST = 4096
TS = NSEQ * SQ
T = TP + TS
NBLK = T // 512
NPB = TP // 512
NT = T // 128
NPT = TP // 128
EPS = 1e-6
NEG = -30000.0
LAM_INIT = 0.8 - 0.6 * math.exp(-0.3 * 0)
TC = T // NCORES
CB = 384
NCB = TC // CB

C_ID, C_J, C_U, C_U2, C_SL, C_BL, C_ONE = [i * 128 for i in range(7)]
C_MA = 7 * 128
C_MAP = C_MA + 4 * 512
C_MB = C_MAP + 128
C_MBP = C_MB + 5 * 512
NCONST = C_MBP + 128
GLEN = 1152


class Sched:
    def __init__(self, nc, sems):
        self.nc = nc
        self.eng = {"act": nc.scalar, "dve": nc.vector, "pool": nc.gpsimd, "pe": nc.tensor, "sp": nc.sync}
        self.sem = sems
        self.cnt = {k: 0 for k in sems}
        self.waited = {e: {} for e in self.eng}

    def wait(self, e, deps):
        for d in deps:
            if d is None:
                continue
            sname, val = d
            if self.waited[e].get(sname, 0) >= val:
                continue
            self.eng[e].wait_ge(self.sem[sname], val)
            self.waited[e][sname] = val

    def op(self, e, fn, deps=(), inc=True):
        self.wait(e, deps)
        ins = fn(self.eng[e])
        if inc:
            ins.then_inc(self.sem[e], 1)
            self.cnt[e] += 1
            return (e, self.cnt[e])
        return None

    def dma(self, q, stream, out, in_, deps=(), **kw):
        self.wait(q, deps)
        self.eng[q].dma_start(out=out, in_=in_, **kw).then_inc(self.sem[stream], 16)
        self.cnt[stream] += 16
        return (stream, self.cnt[stream])

    def last(self, name):
        return (name, self.cnt[name]) if self.cnt[name] > 0 else None

    def barrier(self, streams):
        toks = [self.last(k) for k in list(self.eng) + list(streams)]
        for e in self.eng:
            self.wait(e, toks)


def _np_bucket(rel):
    rel = rel.astype(np.int32)
    ret = np.where(rel > 0, 16, 0)
    n = np.abs(rel)
    nf = np.maximum(n, 1).astype(np.float32)
    large = 8 + (np.log(nf / np.float32(8)) / np.float32(math.log(16)) * np.float32(8)).astype(np.int32)
    large = np.minimum(large, 15)
    return ret + np.where(n < 8, n, large)


def make_consts():
    p = np.arange(128)[:, None]
    f = np.arange(128)[None, :]
    c = np.zeros((128, NCONST), np.float32)
    c[:, C_ID:C_ID + 128] = (p == f)
    c[:, C_J:C_J + 128] = (p == 127 - f)
    c[:, C_U:C_U + 128] = (p <= f)
    c[:, C_U2:C_U2 + 128] = (p <= f) & (p // 64 == f // 64)
    c[:, C_SL:C_SL + 128] = (p < f)
    c[:, C_BL:C_BL + 128] = (p // 32 == f // 32) & (p % 32 >= f % 32)
    c[:, C_ONE:C_ONE + 128] = 1.0
    t = np.arange(512)[None, :]
    for r in range(4):
        c[:, C_MA + r * 512:C_MA + (r + 1) * 512] = np.where(128 * r + p <= t, 0.0, NEG)
    c[:, C_MAP:C_MAP + 128] = np.where((p <= f) & (p // 64 == f // 64), 0.0, NEG)
    sl = 127 - p
    for ri, r in enumerate(range(-1, 4)):
        valid = ((128 * r + sl) // 64) <= (t // 64)
        c[:, C_MB + ri * 512:C_MB + (ri + 1) * 512] = np.where(valid, 0.0, NEG)
    c[:, C_MBP:C_MBP + 128] = np.where(sl // 64 == f // 64, 0.0, NEG)
    m = np.arange(GLEN)
    bk = _np_bucket(511 - m)
    oh = np.zeros((32, GLEN), np.float32)
    oh[bk, m] += 1.0
    oh[15, :] -= 1.0
    return c, oh


def build_A():
    nc = bass.Bass("TRN2", target_bir_lowering=False)

    def din(name, shape):
        return nc.dram_tensor(name, shape, F32, kind="ExternalInput").ap()

    def dout(name, shape):
        return nc.dram_tensor(name, shape, F32, kind="ExternalOutput").ap()

    xT = din("xT", [D, T])
    wA = din("wA", [D, 577])
    gpre = din("gpre", [128, 8])
    bfc = din("bfc", [128, 1])
    rbh = din("rbh", [32, 1])
    consts = din("consts", [128, NCONST])
    ohT = din("oh", [32, GLEN])
    cfkT = din("cfkT", [NSEQ, 64, PAST])
    cdkT = din("cdkT", [NSEQ, 64, PAST])
    cfv = din("cfv", [NSEQ, PAST, 64])
    cdv = din("cdv", [NSEQ, PAST, 128])
    clogf = din("clogf", [128, NSEQ * 32])
    gscr = nc.dram_tensor("gscr", [1, GLEN], F32, kind="Internal").ap()

    o_fk = dout("o_fk", [T, 64])
    o_dk = dout("o_dk", [T, 64])
    o_fv = dout("o_fv", [T, 64])
    o_dv = dout("o_dv", [T, 128])
    o_lf = dout("o_lf", [NT, 128])
    o_fo = dout("o_fo", [64, T])
    o_dp = dout("o_dp", [128, T])

    es = ExitStack()
    with es:
        def sb(name, shape, dt):
            return es.enter_context(nc.sbuf_tensor(name, shape, dt))

        sem_names = ["act", "dve", "pool", "pe", "sp", "ldx0", "ldx1", "ldc", "ldm", "ldk0", "ldk1", "ldvb0", "ldvb1", "ldva0", "ldva1", "st", "stf"]
        ALLST = [k for k in sem_names if k not in ("act", "dve", "pool", "pe", "sp")]
        sems = {k: es.enter_context(nc.semaphore("s_" + k)) for k in sem_names}
        S = Sched(nc, sems)
        PS = [es.enter_context(nc.psum_tensor(f"ps{i}", [128, 512], F32)) for i in range(8)]
        SA = [PS[0], PS[1]]
        SB_ = [PS[2], PS[3]]
        OA, OB, LB, MISC = PS[4], PS[5], PS[6], PS[7]

        cst = sb("cst", [128, 7 * 128], F32)
        idb = sb("idb", [128, 128], BF16)
        jb = sb("jb", [128, 128], BF16)
        oneb = sb("oneb", [128, 128], BF16)
        maskA = sb("maskA", [128, 4 * 512 + 128], BF16)
        mbhi = sb("mbhi", [128, 5 * 512 + 128], BF16)
        mblo = sb("mblo", [128, 5 * 512 + 128], BF16)
        KTr = sb("KTr", [128, T // 2], F32)
        KT = KTr[:].bitcast(BF16)
        VA = sb("VA", [128, NT, 65], BF16)
        VBr = sb("VBr", [128, NT * 64], F32)
        VB = VBr[:].bitcast(BF16).rearrange("p (t d) -> p t d", d=128)
        RX = [sb(f"RX{i}", [128, 2048], F32) for i in range(3)]
        QT = [sb(f"QT{i}", [128, 512], BF16) for i in range(2)]
        QTS = sb("QTS", [128, TS], BF16)
        wq = sb("wq", [128, 8, 128], BF16)
        wk = sb("wk", [128, 8, 128], BF16)
        wt = sb("wt", [128, 8, 321], BF16)
        epsT = sb("epsT", [128, 1], F32)
        negbf = sb("negbf", [128, 1], F32)
        LF = sb("LF", [128, NT], F32)
        CUM = sb("CUM", [128, NT], F32)
        GT = sb("GT", [128, NPT + 1], F32)
        WN = sb("WN", [128, 16], F32)
        arg = [sb(f"arg{i}", [128, 128], F32) for i in range(2)]
        xb = [RX[i][:].bitcast(BF16).rearrange("p (k t) -> p k t", k=8) for i in range(2)]
        xsq = RX[2][:].bitcast(BF16).rearrange("p (k t) -> p k t", k=8)
        rbc = sb("rbc", [128, 512], F32)
        rtok = sb("rtok", [128, 4], F32)
        stage = sb("stage", [128, 4, 321], F32)
        tmp4 = sb("tmp4", [128, 4], F32)
        PA = [sb(f"PA{i}", [128, 512], BF16) for i in range(2)]
        PB = [sb(f"PB{i}", [128, 512], BF16) for i in range(2)]
        rl = sb("rl", [128, 512], F32)
        oasb = sb("oasb", [128, 512], F32)
        obsb = sb("obsb", [128, 512], F32)
        outA = sb("outA", [128, 512], F32)
        outB = sb("outB", [128, 512], F32)
        lacc = [sb(f"lacc{i}", [128, 512], F32) for i in range(2)]
        VAf = [RX[i][:].rearrange("p (j d) -> p j d", d=64) for i in range(2)]
        clf = RX[2][:, 0:NSEQ * 32]
        Cc = RX[2][:, NSEQ * 32:2 * NSEQ * 32]
        Wc = sb("Wc", [128, NSEQ * 32], F32)
        totT = sb("totT", [128, 128], F32)

        ident_f = cst[:, C_ID:C_ID + 128]
        U_f = cst[:, C_U:C_U + 128]
        U2_f = cst[:, C_U2:C_U2 + 128]
        BL_f = cst[:, C_BL:C_BL + 128]
        one_f = cst[:, C_ONE:C_ONE + 128]

        wst = VBr[:, 0:8 * 577].rearrange("p (k c) -> p k c", c=577)
        NMB = 5 * 512 + 128
        mbf = KTr[:, 0:NMB]
        mbm = KTr[:, NMB:2 * NMB]
        ohs = KTr[0:32, 2 * NMB:2 * NMB + GLEN]
        gv = KTr[0:1, 2 * NMB + GLEN:2 * NMB + 2 * GLEN]
        rbs = sb("rbs", [32, 1], F32)
        gp = sb("gp", [128, 8], F32)
        bft = sb("bft", [128, 1], F32)
        S.dma("sp", "ldc", cst[:], consts[:, 0:7 * 128])
        S.dma("pool", "ldm", maskA[:], consts[:, C_MA:C_MA + 4 * 512 + 128])
        S.dma("sp", "ldc", wst, wA.rearrange("(k p) c -> p k c", p=128))
        S.dma("sp", "ldc", gp[:], gpre[:, :])
        S.dma("sp", "ldc", bft[:], bfc[:, :])
        S.dma("sp", "ldc", ohs, ohT[:, :])
        S.dma("sp", "ldc", rbs[:], rbh[:, :])
        S.dma("sp", "ldc", mbm[:, 0:5 * 512], consts[:, C_MB:C_MB + 5 * 512])
        S.dma("sp", "ldc", mbm[:, 5 * 512:NMB], consts[:, C_MBP:C_MBP + 128])
        S.op("pool", lambda e: e.memset(epsT[:], EPS))
        S.op("pool", lambda e: e.memset(VA[:, :, 64:65], 1.0))
        S.op("pool", lambda e: e.memset(GT[:, 0:1], 0.0))
        S.barrier(["ldc", "ldm"])
        S.op("pool", lambda e: e.tensor_copy(out=idb[:], in_=cst[:, C_ID:C_ID + 128]))
        S.op("pool", lambda e: e.tensor_copy(out=jb[:], in_=cst[:, C_J:C_J + 128]))
        S.op("pool", lambda e: e.tensor_copy(out=oneb[:], in_=cst[:, C_ONE:C_ONE + 128]))
        S.op("dve", lambda e: e.tensor_scalar(out=negbf[:], in0=bft[:], scalar1=-1.0, scalar2=None, op0=ALU.mult))
        for k in range(8):
            S.op("dve", lambda e: e.tensor_scalar(out=wq[:, k, :], in0=wst[:, k, 0:128], scalar1=gp[:, k:k + 1], scalar2=0.125, op0=ALU.mult, op1=ALU.mult))
            S.op("dve", lambda e: e.tensor_scalar(out=wk[:, k, :], in0=wst[:, k, 128:256], scalar1=gp[:, k:k + 1], scalar2=None, op0=ALU.mult))
            S.op("dve", lambda e: e.tensor_scalar(out=wt[:, k, :], in0=wst[:, k, 256:577], scalar1=gp[:, k:k + 1], scalar2=None, op0=ALU.mult))
        tg = None
        for i0 in range(0, GLEN, 512):
            n = min(512, GLEN - i0)
            tmm = S.op("pe", lambda e: e.matmul(MISC[0:1, 0:n], lhsT=rbs[:, 0:1], rhs=ohs[:, i0:i0 + n], start=True, stop=True), [tg])
            tg = S.op("dve", lambda e: e.tensor_copy(out=gv[0:1, i0:i0 + n], in_=MISC[0:1, 0:n]), [tmm])
        t_gs = S.dma("sp", "stf", gscr[:, :], gv, [tg])
        S.wait("sp", [t_gs])
        for ri in range(5):
            off = 512 - 128 * ri
            S.dma("sp", "ldc", mbf[:, ri * 512:(ri + 1) * 512], bass.AP(gscr.tensor, off, [[1, 128], [1, 512]]))
        th = S.dma("sp", "ldc", mbf[:, 5 * 512:NMB], bass.AP(gscr.tensor, 384, [[1, 128], [1, 128]]))
        t1 = S.op("dve", lambda e: e.tensor_tensor(out=mbf, in0=mbf, in1=mbm, op=ALU.add), [th])
        t2 = S.op("dve", lambda e: e.tensor_copy(out=mbhi[:], in_=mbf), [t1])
        t3 = S.op("dve", lambda e: e.tensor_tensor(out=mblo[:], in0=mbf, in1=mbhi[:], op=ALU.subtract), [t2])
        S.barrier(ALLST)

        xT_v = xT.rearrange("(k p) t -> p k t", p=128)
        state = {"xld": {}, "pe_xb": [None, None], "stage_free": [], "qt_free": [None, None]}

        def issue_xload(b):
            bb = b % 2
            state["xld"][b] = S.dma("pool", f"ldx{bb}", xb[bb][:], xT_v[:, :, b * 512:(b + 1) * 512], [state["pe_xb"][bb]])

        PJ = [LB, MISC]
        pj = {"pj": [None, None], "xsq": None, "rbc_r": [], "stage": [], "tmp4": None}

        def project_gen(b):
            bb = b % 2
            is_s = b >= NPB
            if b + 1 < NBLK:
                issue_xload(b + 1)
            ld = state["xld"].pop(b)
            tq = S.op("pool", lambda e: e.tensor_tensor(out=xsq[:], in0=xb[bb][:], in1=xb[bb][:], op=ALU.mult), [ld, pj["xsq"]])
            yield
            for k in range(8):
                tss = S.op("pe", lambda e: e.matmul(PJ[0][:], lhsT=oneb[:], rhs=xsq[:, k, :], start=(k == 0), stop=(k == 7)), [tq, pj["pj"][0]] if k == 0 else (), inc=(k == 7))
            pj["xsq"] = tss
            for k in range(8):
                tqm = S.op("pe", lambda e: e.matmul(PJ[1][:], lhsT=wq[:, k, :], rhs=xb[bb][:, k, :], start=(k == 0), stop=(k == 7)), [ld, pj["pj"][1]] if k == 0 else (), inc=(k == 7))
            yield
            yield
            t_ln = S.op("act", lambda e: e.activation(out=rbc[:], in_=PJ[0][:], func=AF.Ln, bias=epsT[:, 0:1], scale=1.0 / D), [tss] + pj["rbc_r"])
            pj["pj"][0] = t_ln
            t_r = S.op("act", lambda e: e.activation(out=rbc[:], in_=rbc[:], func=AF.Exp, scale=-0.5), [t_ln])
            yield
            for k in range(8):
                tkm = S.op("pe", lambda e: e.matmul(PJ[0][:], lhsT=wk[:, k, :], rhs=xb[bb][:, k, :], start=(k == 0), stop=(k == 7)), [pj["pj"][0]] if k == 0 else (), inc=(k == 7))
            yield
            qdst = QTS[:, (b - NPB) * 512:(b - NPB + 1) * 512] if is_s else QT[bb][:]
            t_q = S.op("dve", lambda e: e.tensor_tensor(out=qdst, in0=PJ[1][:], in1=rbc[:], op=ALU.mult), [tqm, t_r, state["qt_free"][bb]])
            pj["pj"][1] = t_q
            t_k = S.op("dve", lambda e: e.tensor_tensor(out=KT[:, b * 512:(b + 1) * 512], in0=PJ[0][:], in1=rbc[:], op=ALU.mult), [tkm])
            pj["pj"][0] = t_k
            yield
            for tt in range(4):
                trt = S.op("pe", lambda e: e.matmul(PJ[1][:, tt:tt + 1], lhsT=rbc[0:1, tt * 128:(tt + 1) * 128], rhs=one_f[0:1, 0:1], start=True, stop=True),
                           [t_r, pj["pj"][1]] if tt == 0 else (), inc=(tt == 3))
            pj["rbc_r"] = [t_k, trt]
            yield
            t_rt = S.op("dve", lambda e: e.tensor_copy(out=rtok[:], in_=PJ[1][:, 0:4]), [trt])
            pj["pj"][1] = t_rt
            t_ev = None
            ttm = None
            for tt in range(4):
                bank = PJ[tt % 2]
                for k in range(8):
                    ttm = S.op("pe", lambda e: e.matmul(bank[:, 0:321], lhsT=xb[bb][:, k, tt * 128:(tt + 1) * 128], rhs=wt[:, k, :], start=(k == 0), stop=(k == 7)),
                               [pj["pj"][tt % 2]] if k == 0 else (), inc=(k == 7))
                yield
                t_ev = S.op("dve", lambda e: e.tensor_scalar(out=stage[:, tt, :], in0=bank[:, 0:321], scalar1=rtok[:, tt:tt + 1], scalar2=None, op0=ALU.mult),
                            [ttm, t_rt] + (pj["stage"] if tt == 0 else []))
                pj["pj"][tt % 2] = t_ev
            state["pe_xb"][bb] = ttm
            yield
            t_e = S.op("act", lambda e: e.activation(out=tmp4[:], in_=stage[:, :, 320], func=AF.Exp, bias=negbf[:, 0:1], scale=-1.0), [t_ev, pj["tmp4"]])
            t_l = S.op("act", lambda e: e.activation(out=tmp4[:], in_=tmp4[:], func=AF.Ln, bias=1.0, scale=1.0), [t_e])
            rows = slice(b * 512, (b + 1) * 512)
            so = []
            so.append(S.dma("sp", "st", o_fk[rows, :].rearrange("(tt p) d -> p tt d", p=128), stage[:, :, 0:64], [t_ev]))
            so.append(S.dma("sp", "st", o_dk[rows, :].rearrange("(tt p) d -> p tt d", p=128), stage[:, :, 64:128]))
            so.append(S.dma("sp", "st", o_fv[rows, :].rearrange("(tt p) d -> p tt d", p=128), stage[:, :, 128:192]))
            so.append(S.dma("sp", "st", o_dv[rows, :].rearrange("(tt p) d -> p tt d", p=128), stage[:, :, 192:320]))
            yield
            t_lf = S.op("dve", lambda e: e.tensor_scalar(out=LF[:, 4 * b:4 * b + 4], in0=tmp4[:], scalar1=-1.0, scalar2=None, op0=ALU.mult), [t_l])
            pj["tmp4"] = t_lf
            yield
            tcm = S.op("pe", lambda e: e.matmul(PJ[0][:, 8:12], lhsT=(U2_f if is_s else U_f), rhs=LF[:, 4 * b:4 * b + 4], start=True, stop=True), [t_lf, pj["pj"][0]], inc=is_s)
            if not is_s:
                tcm = S.op("pe", lambda e: e.matmul(PJ[0][:, 16:20], lhsT=one_f, rhs=LF[:, 4 * b:4 * b + 4], start=True, stop=True))
                tv1 = S.op("pool", lambda e: e.tensor_copy(out=VA[:, 4 * b:4 * b + 4, 0:64], in_=stage[:, :, 128:192]), [t_ev])
                yield
                tcu = None
                for tt in range(4):
                    kk = 4 * b + tt
                    S.op("dve", lambda e: e.tensor_tensor(out=CUM[:, kk:kk + 1], in0=PJ[0][:, 8 + tt:9 + tt], in1=GT[:, kk:kk + 1], op=ALU.add), [tcm, tcu])
                    tcu = S.op("dve", lambda e: e.tensor_tensor(out=GT[:, kk + 1:kk + 2], in0=PJ[0][:, 16 + tt:17 + tt], in1=GT[:, kk:kk + 1], op=ALU.add))
                state["cum"] = tcu
                pj["pj"][0] = tcu
            else:
                sbi = b - NPB
                yield
                twn = S.op("act", lambda e: e.activation(out=WN[:, 4 * sbi:4 * sbi + 4], in_=PJ[0][:, 8:12], func=AF.Exp, scale=-1.0), [tcm])
                pj["pj"][0] = twn
                tv1 = S.op("dve", lambda e: e.tensor_tensor(out=VA[:, 4 * b:4 * b + 4, 0:64], in0=stage[:, :, 128:192],
                                                            in1=WN[:, 4 * sbi:4 * sbi + 4].unsqueeze(2).to_broadcast([128, 4, 64]), op=ALU.mult), [twn, t_ev])
                tv1 = S.op("dve", lambda e: e.tensor_copy(out=VA[:, 4 * b:4 * b + 4, 64:65], in_=WN[:, 4 * sbi:4 * sbi + 4].unsqueeze(2)), [tv1])
            tv2 = S.op("pool", lambda e: e.tensor_copy(out=VB[:, 4 * b:4 * b + 4, :], in_=stage[:, :, 192:320]), [t_ev, tv1])
            pj["stage"] = so + [tv1, tv2, t_e]
            state["kv"] = [t_k, tv1, tv2]
            state["tq"] = t_q

        def run_all(g):
            for _ in g:
                pass

        def finalize_gen(i, t_last_pe, par):
            col0 = i * 512
            tl_ = S.op("pe", lambda e: e.matmul(PJ[0][0:1, :], lhsT=one_f[:, 0:1], rhs=lacc[par][:], start=True, stop=True), [pst["lacc_last"], pj["pj"][0]])
            pst["lacc_free"][par] = tl_
            yield
            t1 = S.op("dve", lambda e: e.reciprocal(out=rl[64:65, :], in_=oasb[64:65, :]), [pst["fin_pe"], pst["fin_st"]])
            t4 = S.op("dve", lambda e: e.reciprocal(out=rl[0:1, :], in_=PJ[0][0:1, :]), [tl_])
            yield
            tm = S.op("pe", lambda e: e.matmul(PJ[1][0:64, :], lhsT=one_f[64:65, 0:64], rhs=rl[64:65, :], start=True, stop=True), [t1, pj["pj"][1]], inc=False)
            tm2 = S.op("pe", lambda e: e.matmul(PJ[0][:, :], lhsT=one_f[0:1, 0:128], rhs=rl[0:1, :], start=True, stop=True), [t4])
            pst["fin_pe"] = tm2
            yield
            yield
            t3 = S.op("dve", lambda e: e.tensor_tensor(out=outA[0:64, :], in0=oasb[0:64, :], in1=PJ[1][0:64, :], op=ALU.mult), [tm2])
            t6 = S.op("dve", lambda e: e.tensor_tensor(out=outB[:, :], in0=obsb[:, :], in1=PJ[0][:, :], op=ALU.mult))
            pj["pj"][0] = t6
            pj["pj"][1] = t6
            pst["sb_copy_free"] = t6
            S.dma("sp", "st", o_fo[:, col0:col0 + 512], outA[0:64, :], [t6])
            pst["fin_st"] = S.dma("sp", "st", o_dp[:, col0:col0 + 512], outB[:, :])
            yield

        def finalize(ncol, col0, t_last_pe, use_lacc=False):
            t1 = S.op("dve", lambda e: e.reciprocal(out=rl[64:65, 0:ncol], in_=OA[64:65, 0:ncol]), [t_last_pe, S.last("pe"), S.last("st")])
            t2 = S.op("dve", lambda e: e.tensor_copy(out=oasb[0:64, 0:ncol], in_=OA[0:64, 0:ncol]))
            tm = S.op("pe", lambda e: e.matmul(MISC[0:64, 0:ncol], lhsT=one_f[64:65, 0:64], rhs=rl[64:65, 0:ncol], start=True, stop=True), [t1, S.last("dve"), S.last("act")])
            t3 = S.op("dve", lambda e: e.tensor_tensor(out=outA[0:64, 0:ncol], in0=oasb[0:64, 0:ncol], in1=MISC[0:64, 0:ncol], op=ALU.mult), [tm])
            s1 = S.dma("sp", "st", o_fo[:, col0:col0 + ncol], outA[0:64, 0:ncol], [t3])
            if use_lacc:
                tl_ = S.op("pe", lambda e: e.matmul(LB[0:1, 0:ncol], lhsT=one_f[:, 0:1], rhs=lacc[:, 0:ncol], start=True, stop=True), [pst["lacc_last"]])
                pst["lacc_free"] = tl_
                t4 = S.op("dve", lambda e: e.reciprocal(out=rl[0:1, 0:ncol], in_=LB[0:1, 0:ncol]), [tl_])
            else:
                t4 = S.op("dve", lambda e: e.reciprocal(out=rl[0:1, 0:ncol], in_=LB[0:1, 0:ncol]))
            t5 = S.op("dve", lambda e: e.tensor_copy(out=obsb[:, 0:ncol], in_=OB[:, 0:ncol]))
            tm2 = S.op("pe", lambda e: e.matmul(MISC[:, 0:ncol], lhsT=one_f[0:1, 0:128], rhs=rl[0:1, 0:ncol], start=True, stop=True), [t4, t3])
            t6 = S.op("dve", lambda e: e.tensor_tensor(out=outB[:, 0:ncol], in0=obsb[:, 0:ncol], in1=MISC[:, 0:ncol], op=ALU.mult), [tm2])
            s2 = S.dma("sp", "st", o_dp[:, col0:col0 + ncol], outB[:, 0:ncol], [t6])
            return t6

        pst = {"sa_free": [None, None], "sb_free": [None, None], "pa_free": [None, None], "pb_free": [None, None], "pb_free2": [None, None], "acc_free": None, "lacc_free": [None, None], "lacc_last": None, "fin_pe": None, "fin_st": None, "sb_copy_free": None}

        def attention(i, t_q, gens):
            qb = i % 2
            lp = i % 2
            nj = 4 * i + 4
            ab = i % 2
            t_arg = S.op("dve", lambda e: e.tensor_scalar(out=arg[ab][:, 0:nj], in0=CUM[:, 0:nj], scalar1=-1.0, scalar2=GT[:, 4 * i + 2:4 * i + 3], op0=ALU.mult, op1=ALU.add),
                         [state["cum"], S.last("act")])
            kv = state["kv"] + [t_q]

            def qk(j):
                u = j % 2
                r = j - 4 * i
                last = r < 0
                S.op("pe", lambda e: e.matmul(SA[u][:], lhsT=KT[0:64, j * 128:(j + 1) * 128], rhs=QT[qb][0:64, :], start=True, stop=last),
                     kv + [pst["sa_free"][u]], inc=False)
                if r >= 0:
                    S.op("pe", lambda e: e.matmul(SA[u][:], lhsT=idb[:], rhs=maskA[:, r * 512:(r + 1) * 512], start=False, stop=True), inc=False)
                lastb = r < -1
                S.op("pe", lambda e: e.matmul(SB_[u][:], lhsT=KT[64:128, j * 128:(j + 1) * 128], rhs=QT[qb][64:128, :], start=True, stop=lastb),
                     [pst["sb_free"][u]], inc=lastb)
                if r >= -1:
                    ri = r + 1
                    S.op("pe", lambda e: e.matmul(SB_[u][:], lhsT=jb[:], rhs=mbhi[:, ri * 512:(ri + 1) * 512], start=False, stop=False), inc=False)
                    S.op("pe", lambda e: e.matmul(SB_[u][:], lhsT=jb[:], rhs=mblo[:, ri * 512:(ri + 1) * 512], start=False, stop=True))
                return S.last("pe")

            def ex(j, tqk):
                u = j % 2
                ta = S.op("act", lambda e: e.activation(out=PA[u][:], in_=SA[u][:], func=AF.Exp, bias=arg[ab][:, j:j + 1], scale=1.0), [tqk, t_arg, pst["pa_free"][u]])
                tb = S.op("act", lambda e: e.activation(out=PB[u][:], in_=SB_[u][:], func=AF.Exp), [pst["pb_free"][u], pst["pb_free2"][u]])
                pst["sa_free"][u] = ta
                pst["sb_free"][u] = tb
                return tb

            def pv(j, tex):
                u = j % 2
                first = j == 0
                lastj = j == nj - 1
                S.op("pe", lambda e: e.matmul(OA[0:65, :], lhsT=VA[:, j, :], rhs=PA[u][:], start=first, stop=lastj), [tex, pst["acc_free"]] if first else [tex], inc=False)
                t = S.op("pe", lambda e: e.matmul(OB[:, :], lhsT=VB[:, j, :], rhs=PB[u][:], start=first, stop=lastj))
                if first:
                    td = S.op("dve", lambda e: e.tensor_copy(out=lacc[lp][:], in_=PB[u][:]), [tex, pst["lacc_free"][lp]])
                else:
                    td = S.op("dve", lambda e: e.tensor_tensor(out=lacc[lp][:], in0=lacc[lp][:], in1=PB[u][:], op=ALU.add), [tex])
                pst["pa_free"][u] = t
                pst["pb_free"][u] = td
                pst["pb_free2"][u] = t
                pst["lacc_last"] = td
                return t

            tqk = {0: qk(0)}
            tl = None
            gi = 0
            for j in range(nj):
                if j + 1 < nj:
                    tqk[j + 1] = qk(j + 1)
                tex = ex(j, tqk.pop(j))
                tl = pv(j, tex)
                while gi < len(gens):
                    try:
                        next(gens[gi])
                        break
                    except StopIteration:
                        gi += 1
            for g in gens[gi:]:
                run_all(g)
            state["qt_free"][qb] = tl
            tc1 = S.op("dve", lambda e: e.tensor_copy(out=oasb[0:65, :], in_=OA[0:65, :]), [tl, pst["sb_copy_free"]])
            tc2 = S.op("dve", lambda e: e.tensor_copy(out=obsb[:, :], in_=OB[:, :]))
            pst["acc_free"] = tc2
            return finalize_gen(i, tl, lp)

        issue_xload(0)
        run_all(project_gen(0))
        fin = None
        for b in range(NPB):
            gens = ([fin] if fin is not None else []) + [project_gen(b + 1)]
            fin = attention(b, state["tq"], gens)
        run_all(fin)
        for b in range(NPB + 1, NBLK):
            run_all(project_gen(b))
        S.barrier(ALLST)

        t_cl = S.dma("sp", "ldc", clf[:], clogf[:, :])
        for hh in range(2):
            tmm = S.op("pe", lambda e: e.matmul(SA[hh][:], lhsT=U_f, rhs=clf[:, hh * 512:(hh + 1) * 512], start=True, stop=True), [t_cl])
            S.op("dve", lambda e: e.tensor_copy(out=Cc[:, hh * 512:(hh + 1) * 512], in_=SA[hh][:]), [tmm])
        tprev = None
        for q8 in range(8):
            tmm = S.op("pe", lambda e: e.matmul(SB_[0][:, 0:128], lhsT=clf[:, q8 * 128:(q8 + 1) * 128], rhs=one_f, start=True, stop=True), [tprev])
            tcp = S.op("dve", lambda e: e.tensor_copy(out=totT[:], in_=SB_[0][:, 0:128]), [tmm, tprev])
            tmm2 = S.op("pe", lambda e: e.matmul(SB_[1][:, 0:128], lhsT=totT[:], rhs=BL_f, start=True, stop=True), [tcp])
            tprev = S.op("dve", lambda e: e.tensor_tensor(out=Wc[:, q8 * 128:(q8 + 1) * 128], in0=SB_[1][:, 0:128], in1=Cc[:, q8 * 128:(q8 + 1) * 128], op=ALU.subtract), [tmm2])
        t_wc = S.op("act", lambda e: e.activation(out=Wc[:], in_=Wc[:], func=AF.Exp), [tprev])
        S.barrier(ALLST)

        KTc = [KT[:, 0:PAST], KT[:, PAST:2 * PAST]]
        VBc = [VB[:, 0:32, :], VB[:, 32:64, :]]
        VAc = [VA[:, 0:32, :], VA[:, 32:64, :]]
        cst_free = [None, None]
        ldtok = {}

        def load_seq(n):
            cb = n % 2
            d = [cst_free[cb]]
            a = S.dma("pool", f"ldk{cb}", KTc[cb][0:64, :], cfkT[n, :, :], d)
            a = S.dma("pool", f"ldk{cb}", KTc[cb][64:128, :], cdkT[n, :, :])
            b_ = S.dma("pool", f"ldvb{cb}", VBc[cb], cdv[n, :, :].rearrange("(j p) d -> p j d", p=128))
            c_ = S.dma("sp", f"ldva{cb}", VAf[cb][:], cfv[n, :, :].rearrange("(j p) d -> p j d", p=128), d)
            ldtok[n] = [a, b_, c_]

        redt = sb("redt", [128, 128], F32)
        red = [redt[:, 0:64], redt[:, 64:128]]
        load_seq(0)
        sa_free = [None, None]
        pa_free = [[], []]
        sm = {"acc_free": None, "seq": {}, "lacc_free": [None, None], "red_free": [None, None]}
        items = []
        for m in range(16):
            items.append(("new", m, 0, 0))
            for hh in range(2):
                for g in range(4):
                    items.append(("grp", m, hh, g))

        def s_qk(it, u):
            kind, m, hh, g = it
            tt = NPT + m
            if kind == "new":
                S.op("pe", lambda e: e.matmul(SA[u][:, 0:128], lhsT=KT[0:64, tt * 128:(tt + 1) * 128], rhs=QTS[0:64, m * 128:(m + 1) * 128], start=True, stop=False), [sa_free[u]], inc=False)
                S.op("pe", lambda e: e.matmul(SA[u][:, 0:128], lhsT=idb[:], rhs=maskA[:, 4 * 512:4 * 512 + 128], start=False, stop=True), inc=False)
                S.op("pe", lambda e: e.matmul(SB_[u][:, 0:128], lhsT=KT[64:128, tt * 128:(tt + 1) * 128], rhs=QTS[64:128, m * 128:(m + 1) * 128], start=True, stop=False), inc=False)
                S.op("pe", lambda e: e.matmul(SB_[u][:, 0:128], lhsT=jb[:], rhs=mbhi[:, 5 * 512:5 * 512 + 128], start=False, stop=False), inc=False)
                return S.op("pe", lambda e: e.matmul(SB_[u][:, 0:128], lhsT=jb[:], rhs=mblo[:, 5 * 512:5 * 512 + 128], start=False, stop=True))
            n = 2 * m + hh
            cb = n % 2
            if g == 0:
                lda, ldb_, ldc_ = ldtok.pop(n)
                tva = S.op("dve", lambda e: e.tensor_tensor(out=VAc[cb][:, :, 0:64], in0=VAf[cb][:], in1=Wc[:, n * 32:(n + 1) * 32].unsqueeze(2).to_broadcast([128, 32, 64]), op=ALU.mult), [ldc_, t_wc])
                tva = S.op("dve", lambda e: e.tensor_copy(out=VAc[cb][:, :, 64:65], in_=Wc[:, n * 32:(n + 1) * 32].unsqueeze(2)))
                sm["seq"][n] = (lda, ldb_, tva)
            lda, ldb_, tva = sm["seq"][n]
            qcols = slice(m * 128 + hh * 64, m * 128 + hh * 64 + 64)
            tqk = None
            for jj in range(8):
                j = 8 * g + jj
                cs = slice(jj * 64, (jj + 1) * 64)
                S.op("pe", lambda e: e.matmul(SA[u][:, cs], lhsT=KTc[cb][0:64, j * 128:(j + 1) * 128], rhs=QTS[0:64, qcols], start=True, stop=True),
                     [lda, sa_free[u]] if jj == 0 else (), inc=False)
            for jj in range(8):
                j = 8 * g + jj
                cs = slice(jj * 64, (jj + 1) * 64)
                lastb = j != 31
                tqk = S.op("pe", lambda e: e.matmul(SB_[u][:, cs], lhsT=KTc[cb][64:128, j * 128:(j + 1) * 128], rhs=QTS[64:128, qcols], start=True, stop=lastb), inc=(jj == 7 and lastb))
                if j == 31:
                    S.op("pe", lambda e: e.matmul(SB_[u][:, cs], lhsT=jb[:], rhs=mbhi[:, 0:64], start=False, stop=False), inc=False)
                    tqk = S.op("pe", lambda e: e.matmul(SB_[u][:, cs], lhsT=jb[:], rhs=mblo[:, 0:64], start=False, stop=True))
            return tqk

        def s_ex(it, u, tqk):
            ncol = 128 if it[0] == "new" else 512
            S.op("act", lambda e: e.activation(out=PA[u][:, 0:ncol], in_=SA[u][:, 0:ncol], func=AF.Exp), [tqk] + pa_free[u], inc=False)
            tb = S.op("act", lambda e: e.activation(out=PB[u][:, 0:ncol], in_=SB_[u][:, 0:ncol], func=AF.Exp))
            sa_free[u] = tb
            return tb

        def s_pv(it, u, tex):
            kind, m, hh, g = it
            tt = NPT + m
            if kind == "new":
                S.op("pe", lambda e: e.matmul(OA[0:65, 0:128], lhsT=VA[:, tt, :], rhs=PA[u][:, 0:128], start=True, stop=False), [tex, sm["acc_free"]], inc=False)
                S.op("pe", lambda e: e.matmul(OB[:, 0:128], lhsT=VB[:, tt, :], rhs=PB[u][:, 0:128], start=True, stop=False), inc=False)
                tpv = S.op("pe", lambda e: e.matmul(LB[0:1, 0:128], lhsT=oneb[:, 0:1], rhs=PB[u][:, 0:128], start=True, stop=False))
                pa_free[u] = [tpv]
                return
            n = 2 * m + hh
            cb = n % 2
            lp = n % 2
            lda, ldb_, tva = sm["seq"][n]
            ocols = slice(hh * 64, hh * 64 + 64)
            tpv = None
            for jj in range(8):
                j = 8 * g + jj
                cs = slice(jj * 64, (jj + 1) * 64)
                lastj = (j == 31)
                S.op("pe", lambda e: e.matmul(OA[0:65, ocols], lhsT=VAc[cb][:, j, :], rhs=PA[u][:, cs], start=False, stop=lastj, skip_group_check=True),
                     [tex, tva, ldb_] if jj == 0 else (), inc=False)
                tpv = S.op("pe", lambda e: e.matmul(OB[:, ocols], lhsT=VBc[cb][:, j, :], rhs=PB[u][:, cs], start=False, stop=lastj, skip_group_check=True), inc=(jj == 7))
            if g == 0:
                if n + 1 < NSEQ:
                    load_seq(n + 1)
                td = S.op("dve", lambda e: e.tensor_copy(out=lacc[lp][:], in_=PB[u][:]), [tex, sm["lacc_free"][lp]])
            else:
                td = S.op("dve", lambda e: e.tensor_tensor(out=lacc[lp][:], in0=lacc[lp][:], in1=PB[u][:], op=ALU.add), [tex])
            pa_free[u] = [tpv, td]
            if g == 3:
                tr = S.op("dve", lambda e: e.tensor_reduce(out=red[lp], in_=lacc[lp][:].rearrange("p (j q) -> p q j", q=64), axis=mybir.AxisListType.X, op=ALU.add), [td, sm["red_free"][lp]])
                sm["lacc_free"][lp] = tr
                tpv = S.op("pe", lambda e: e.matmul(LB[0:1, ocols], lhsT=one_f[:, 0:1], rhs=red[lp], start=False, stop=True, skip_group_check=True), [tr])
                sm["red_free"][lp] = tpv
                cst_free[cb] = tpv
                del sm["seq"][n]
                if hh == 1:
                    sm["acc_free"] = finalize(128, TP + m * 128, tpv)

        tq = s_qk(items[0], 0)
        for idx, it in enumerate(items):
            u = idx % 2
            tq_next = s_qk(items[idx + 1], (idx + 1) % 2) if idx + 1 < len(items) else None
            tex = s_ex(it, u, tq)
            s_pv(it, u, tex)
            tq = tq_next

        for c0, n in ((0, 128), (128, 16)):
            tmm = S.op("pe", lambda e: e.matmul(MISC[0:n, 0:128], lhsT=LF[:, c0:c0 + n], rhs=ident_f, start=True, stop=True), [S.last("dve"), S.last("act")])
            tcp = S.op("dve", lambda e: e.tensor_copy(out=rl[0:n, 0:128], in_=MISC[0:n, 0:128]), [tmm, S.last("st")])
            S.dma("sp", "st", o_lf[c0:c0 + n, :], rl[0:n, 0:128], [tcp])
        S.barrier(ALLST)
    return nc


def build_C():
    nc = bass.Bass("TRN2", target_bir_lowering=False)

    def din(name, shape):
        return nc.dram_tensor(name, shape, F32, kind="ExternalInput").ap()

    AT = din("AT", [1536, TC])
    xTc = din("xTc", [D, TC])
    xtok = din("xtok", [TC, D])
    wg = din("wg", [D, 1024])
    wo = din("wo", [1024, D])
    gpre = din("gpre", [128, 8])
    gpost = din("gpost", [128, D])
    sgin = din("sg", [128, 1])
    lamv = din("lamv", [128, 256])
    consts = din("consts", [128, NCONST])
    yc = nc.dram_tensor("yc", [TC, D], F32, kind="ExternalOutput").ap()

    es = ExitStack()
    with es:
        def sb(name, shape, dt):
            return es.enter_context(nc.sbuf_tensor(name, shape, dt))

        sem_names = ["act", "dve", "pool", "pe", "sp", "ldc", "ldx", "ldx0", "ldx1", "lda0", "lda1", "ldt0", "ldt1", "st"]
        ALLST = [k for k in sem_names if k not in ("act", "dve", "pool", "pe", "sp")]
        sems = {k: es.enter_context(nc.semaphore("s_" + k)) for k in sem_names}
        S = Sched(nc, sems)
        PS = [es.enter_context(nc.psum_tensor(f"ps{i}", [128, 512], F32)) for i in range(8)]

        onef = sb("onef", [128, 128], F32)
        oneb = sb("oneb", [128, 128], BF16)
        wgs = sb("wgs", [128, 8, 1024], F32)
        wgb = sb("wgb", [128, 8, 1024], BF16)
        wob = sb("wob", [128, 8, 1024], BF16)
        gp = sb("gp", [128, 8], F32)
        gpo = sb("gpo", [128, D], F32)
        sgl = sb("sgl", [128, 1], F32)
        lv = sb("lv", [128, 256], F32)
        ltmp = sb("ltmp", [128, 128], F32)
        lsum = sb("lsum", [128, 2], F32)
        neglam = sb("neglam", [128, 1], F32)
        epsT = sb("epsT", [128, 1], F32)
        xb = [sb(f"xb{i}", [128, 8, CB], BF16) for i in range(2)]
        xsq = sb("xsq", [128, 8, CB], BF16)
        rbc = sb("rbc", [128, CB], F32)
        at = [sb(f"at{i}", [128, 12, CB], F32) for i in range(2)]
        gsb = [sb(f"gsb{i}", [128, CB], F32) for i in range(2)]
        esb = [sb(f"esb{i}", [128, CB], F32) for i in range(2)]
        sil = sb("sil", [128, 8, CB], F32)
        dsb = sb("dsb", [128, CB], F32)
        dsq = sb("dsq", [128, CB], F32)
        rs = sb("rs", [128, CB], F32)
        mixT = sb("mixT", [128, 8, CB], BF16)
        xt = [sb(f"xt{i}", [128, D], F32) for i in range(2)]
        sq = [sb(f"sq{i}", [128, D], F32) for i in range(2)]
        ysb = [sb(f"ysb{i}", [128, D], F32) for i in range(2)]
        ssq = [sb(f"ssq{i}", [128, 1], F32) for i in range(2)]
        rstd = [sb(f"rstd{i}", [128, 1], F32) for i in range(2)]

        S.dma("sp", "ldc", onef[:], consts[:, C_ONE:C_ONE + 128])
        S.dma("pool", "ldx", oneb[:], consts[:, C_ONE:C_ONE + 128])
        S.dma("sp", "ldc", wgs[:], wg.rearrange("(k p) c -> p k c", p=128))
        S.dma("pool", "ldx", wob[:], wo.rearrange("(k p) c -> p k c", p=128))
        S.dma("sp", "ldc", gp[:], gpre[:, :])
        S.dma("sp", "ldc", gpo[:], gpost[:, :])
        S.dma("sp", "ldc", sgl[:], sgin[:, :])
        S.dma("sp", "ldc", lv[:], lamv[:, :])
        S.op("pool", lambda e: e.memset(epsT[:], EPS))
        S.barrier(ALLST)
        for k in range(8):
            S.op("dve", lambda e: e.tensor_scalar(out=wgb[:, k, :], in0=wgs[:, k, :], scalar1=gp[:, k:k + 1], scalar2=None, op0=ALU.mult))
        S.op("dve", lambda e: e.tensor_scalar(out=sgl[:], in0=sgl[:], scalar1=0.5 * (1.0 - LAM_INIT), scalar2=None, op0=ALU.mult), [S.last("dve")])
        t = S.op("dve", lambda e: e.tensor_tensor(out=ltmp[:, 0:64], in0=lv[:, 0:64], in1=lv[:, 64:128], op=ALU.mult))
        t = S.op("dve", lambda e: e.tensor_tensor(out=ltmp[:, 64:128], in0=lv[:, 128:192], in1=lv[:, 192:256], op=ALU.mult))
        t = S.op("dve", lambda e: e.reduce_sum(out=lsum[:, 0:1], in_=ltmp[:, 0:64], axis=mybir.AxisListType.X), [t])
        t = S.op("dve", lambda e: e.reduce_sum(out=lsum[:, 1:2], in_=ltmp[:, 64:128], axis=mybir.AxisListType.X), [t])
        t = S.op("act", lambda e: e.activation(out=lsum[:], in_=lsum[:], func=AF.Exp), [t])
        t = S.op("dve", lambda e: e.tensor_tensor(out=neglam[:], in0=lsum[:, 1:2], in1=lsum[:, 0:1], op=ALU.subtract), [t])
        t = S.op("dve", lambda e: e.tensor_scalar(out=neglam[:], in0=neglam[:], scalar1=-LAM_INIT, scalar2=None, op0=ALU.add), [t])
        S.barrier(ALLST)

        xT_v = xTc.rearrange("(k p) t -> p k t", p=128)
        AT_v = AT.rearrange("(c p) t -> p c t", p=128)
        OP = [[PS[4], PS[5]], [PS[6], PS[7]]]
        fr = {"xb": [None, None], "at": [None, None], "xsq": None, "ps0": None, "rbc": None, "gps": [None, None], "gsb": [None, None],
              "esb": [None, None], "sil": None, "mix": None, "ps3": None, "rs": None, "dsb": None, "op": [None, None], "sq": [None, None],
              "ysb": [None, None], "xt": [None, None]}
        ld = {}

        def issue_loads(blk):
            pb = blk % 2
            cs = slice(blk * CB, (blk + 1) * CB)
            ld[blk] = (S.dma("pool", f"ldx{pb}", xb[pb][:], xT_v[:, :, cs], [fr["xb"][pb]]),
                       S.dma("sp", f"lda{pb}", at[pb][:], AT_v[:, :, cs], [fr["at"][pb]]))

        issue_loads(0)
        tile_idx = 0
        for blk in range(NCB):
            pb = blk % 2
            if blk + 1 < NCB:
                issue_loads(blk + 1)
            tlx, tla = ld.pop(blk)
            tq = S.op("pool", lambda e: e.tensor_tensor(out=xsq[:], in0=xb[pb][:], in1=xb[pb][:], op=ALU.mult), [tlx, fr["xsq"]])
            for k in range(8):
                tss = S.op("pe", lambda e: e.matmul(PS[0][:, 0:CB], lhsT=oneb[:], rhs=xsq[:, k, :], start=(k == 0), stop=(k == 7)), [tq, fr["ps0"]] if k == 0 else (), inc=(k == 7))
            fr["xsq"] = tss
            t = S.op("act", lambda e: e.activation(out=rbc[:], in_=PS[0][:, 0:CB], func=AF.Ln, bias=epsT[:, 0:1], scale=1.0 / D), [tss, fr["rbc"]])
            fr["ps0"] = t
            t_r = S.op("act", lambda e: e.activation(out=rbc[:], in_=rbc[:], func=AF.Exp, scale=-0.5), [t])
            t_mix = []
            for c8 in range(8):
                par = c8 % 2
                gps = PS[1 + par]
                for k in range(8):
                    tg = S.op("pe", lambda e: e.matmul(gps[:, 0:CB], lhsT=wgb[:, k, c8 * 128:(c8 + 1) * 128], rhs=xb[pb][:, k, :], start=(k == 0), stop=(k == 7)),
                              [tlx, fr["gps"][par]] if k == 0 else (), inc=(k == 7))
                t_g = S.op("dve", lambda e: e.tensor_tensor(out=gsb[par][:], in0=gps[:, 0:CB], in1=rbc[:], op=ALU.mult), [tg, t_r, fr["gsb"][par]])
                fr["gps"][par] = t_g
                t_th = S.op("act", lambda e: e.activation(out=esb[par][:], in_=gsb[par][:], func=AF.Tanh, scale=0.5), [t_g, fr["esb"][par]])
                t_s2 = S.op("dve", lambda e: e.scalar_tensor_tensor(out=sil[:, c8, :], in0=esb[par][:], scalar=1.0, in1=gsb[par][:], op0=ALU.add, op1=ALU.mult), [t_th, fr["sil"]])
                fr["gsb"][par] = t_s2
                fr["esb"][par] = t_s2
                if c8 < 4:
                    t_mix.append(S.op("dve", lambda e: e.scalar_tensor_tensor(out=mixT[:, c8, :], in0=sil[:, c8, :], scalar=0.5, in1=at[pb][:, c8, :], op0=ALU.mult, op1=ALU.mult), [t_s2, tla, fr["mix"]]))
            fr["xb"][pb] = tg
            fr["rbc"] = t_s2
            for h in range(4):
                c8 = 4 + h
                t = S.op("dve", lambda e: e.scalar_tensor_tensor(out=dsb[:], in0=at[pb][:, 4 + 2 * h + 1, :], scalar=neglam[:, 0:1], in1=at[pb][:, 4 + 2 * h, :], op0=ALU.mult, op1=ALU.add), [tla, fr["dsb"]])
                t = S.op("dve", lambda e: e.tensor_tensor(out=dsq[:], in0=dsb[:], in1=dsb[:], op=ALU.mult), [t, fr["ps3"]])
                tm = S.op("pe", lambda e: e.matmul(PS[3][:, 0:CB], lhsT=onef[:], rhs=dsq[:], start=True, stop=True), [t, fr["rs"]])
                fr["ps3"] = tm
                t = S.op("act", lambda e: e.activation(out=rs[:], in_=PS[3][:, 0:CB], func=AF.Ln, bias=epsT[:, 0:1], scale=1.0 / 128.0), [tm, fr["dsb"]])
                t = S.op("act", lambda e: e.activation(out=rs[:], in_=rs[:], func=AF.Exp, scale=-0.5), [t])
                fr["rs"] = t
                t = S.op("dve", lambda e: e.scalar_tensor_tensor(out=dsb[:], in0=dsb[:], scalar=sgl[:, 0:1], in1=rs[:], op0=ALU.mult, op1=ALU.mult), [t])
                t = S.op("dve", lambda e: e.tensor_tensor(out=mixT[:, c8, :], in0=dsb[:], in1=sil[:, c8, :], op=ALU.mult), [t, fr["mix"]])
                fr["dsb"] = t
                fr["rs"] = t
                t_mix.append(t)
            fr["at"][pb] = t
            fr["sil"] = t
            t_mixall = t
            for tt in range(CB // 128):
                q = tile_idx % 2
                tile_idx += 1
                r0 = blk * CB + tt * 128
                tlt = S.dma("sp", f"ldt{q}", xt[q][:], xtok[r0:r0 + 128, :], [fr["xt"][q]])
                for half in range(2):
                    for c8 in range(8):
                        to = S.op("pe", lambda e: e.matmul(OP[q][half][:], lhsT=mixT[:, c8, tt * 128:(tt + 1) * 128], rhs=wob[:, c8, half * 512:(half + 1) * 512], start=(c8 == 0), stop=(c8 == 7)),
                                  [t_mixall, fr["op"][q]] if (c8 == 0 and half == 0) else (), inc=(c8 == 7))
                for half in range(2):
                    t = S.op("act", lambda e: e.activation(out=sq[q][:, half * 512:(half + 1) * 512], in_=OP[q][half][:], func=AF.Square), [to, fr["sq"][q]])
                t = S.op("dve", lambda e: e.reduce_sum(out=ssq[q][:], in_=sq[q][:], axis=mybir.AxisListType.X), [t])
                fr["sq"][q] = t
                t = S.op("act", lambda e: e.activation(out=rstd[q][:], in_=ssq[q][:], func=AF.Ln, bias=epsT[:, 0:1], scale=1.0 / D), [t])
                t = S.op("act", lambda e: e.activation(out=rstd[q][:], in_=rstd[q][:], func=AF.Exp, scale=-0.5), [t])
                for half in range(2):
                    t = S.op("dve", lambda e: e.scalar_tensor_tensor(out=ysb[q][:, half * 512:(half + 1) * 512], in0=OP[q][half][:], scalar=rstd[q][:, 0:1], in1=gpo[:, half * 512:(half + 1) * 512], op0=ALU.mult, op1=ALU.mult),
                             [t, fr["ysb"][q]])
                fr["op"][q] = t
                t = S.op("pool", lambda e: e.tensor_tensor(out=ysb[q][:], in0=ysb[q][:], in1=xt[q][:], op=ALU.add), [t, tlt])
                fr["xt"][q] = t
                fr["ysb"][q] = S.dma("sp", "st", yc[r0:r0 + 128, :], ysb[q][:], [t])
            fr["mix"] = to
        S.barrier(ALLST)
    return nc


_CACHE = {}
O_FQ, O_FK, O_FV, O_FF, O_FG, O_DQ, O_DK, O_DV, O_DG = 0, 512, 1024, 1536, 1544, 2056, 2568, 3080, 3592


def _prep_A(inp):
    xp = np.asarray(inp["x_prompt"], np.float32)[0]
    xs = np.asarray(inp["x_sample"], np.float32).reshape(TS, D)
    xT = np.ascontiguousarray(np.concatenate([xp, xs], axis=0).T)
    w = np.asarray(inp["w_in"], np.float32)[0]
    consts, oh = make_consts()
    gpre = np.ascontiguousarray(np.asarray(inp["norm_pre_g"], np.float32)[0].reshape(8, 128).T)
    fb = np.asarray(inp["forget_bias"], np.float32)[0]
    rb = np.asarray(inp["rel_bias"], np.float32)
    cfk = np.asarray(inp["cache_fox_k"], np.float32)[0]
    cfv = np.asarray(inp["cache_fox_v"], np.float32)[0]
    clf = np.asarray(inp["cache_fox_logf"], np.float32)[0]
    cdk = np.asarray(inp["cache_diff_k"], np.float32)[0]
    cdv = np.asarray(inp["cache_diff_v"], np.float32)[0]
    maps = []
    for c in range(NCORES):
        h, part = c // 2, c % 2
        fkc = np.arange(O_FK + 64 * c, O_FK + 64 * c + 64)
        dkc = np.arange(O_DK + 128 * h + 64 * part, O_DK + 128 * h + 64 * part + 64)
        cols = np.concatenate([
            np.arange(O_FQ + 64 * c, O_FQ + 64 * c + 64),
            np.arange(O_DQ + 128 * h + 64 * part, O_DQ + 128 * h + 64 * part + 64),
            fkc, dkc, fkc, dkc,
            np.arange(O_FV + 64 * c, O_FV + 64 * c + 64),
            np.arange(O_DV + 128 * h, O_DV + 128 * h + 128),
            np.arange(O_FF + c, O_FF + c + 1)])
        maps.append({
            "xT": xT,
            "wA": np.ascontiguousarray(w[:, cols]),
            "gpre": gpre,
            "bfc": np.full((128, 1), fb[c], np.float32),
            "rbh": np.ascontiguousarray(rb[:, h:h + 1]),
            "consts": consts,
            "oh": oh,
            "cfkT": np.ascontiguousarray(cfk[:, :, c, :].transpose(0, 2, 1)),
            "cdkT": np.ascontiguousarray(cdk[:, :, h, 64 * part:64 * part + 64].transpose(0, 2, 1)),
            "cfv": np.ascontiguousarray(cfv[:, :, c, :]),
            "cdv": np.ascontiguousarray(cdv[:, :, h, :]),
            "clogf": np.ascontiguousarray(clf[:, :, c].reshape(NSEQ, 32, 128).transpose(2, 0, 1).reshape(128, NSEQ * 32)),
        })
    return maps, xT


def run_A(inp):
    if "A" not in _CACHE:
        _CACHE["A"] = build_A()
    maps, xT = _prep_A(inp)
    res = run_bass_kernel_spmd(_CACHE["A"], maps, core_ids=list(range(NCORES)))
    return res.results, xT


def _tok_cols(c):
    return np.concatenate([np.arange(2048 * c, 2048 * c + 2048), np.arange(TP + 256 * c, TP + 256 * c + 256)])


def run_C(inp, resA, xT):
    if "C" not in _CACHE:
        _CACHE["C"] = build_C()
    w = np.asarray(inp["w_in"], np.float32)[0]
    wg = np.ascontiguousarray(np.concatenate([w[:, O_FG:O_FG + 512], w[:, O_DG:O_DG + 512]], axis=1))
    wo = np.ascontiguousarray(np.asarray(inp["w_out"], np.float32)[0])
    consts, _ = make_consts()
    gpre = np.ascontiguousarray(np.asarray(inp["norm_pre_g"], np.float32)[0].reshape(8, 128).T)
    gpost = np.ascontiguousarray(np.broadcast_to(np.asarray(inp["norm_post_g"], np.float32)[0][None, :], (128, D)))
    sg = np.ascontiguousarray(np.asarray(inp["subln_g"], np.float32)[0].reshape(128, 1))
    lamv = np.concatenate([np.asarray(inp[k], np.float32)[0] for k in ("lambda_q1", "lambda_k1", "lambda_q2", "lambda_k2")])
    lamv = np.ascontiguousarray(np.broadcast_to(lamv[None, :], (128, 256)))
    ATfull = np.concatenate([resA[c]["o_fo"] for c in range(NCORES)] + [resA[c]["o_dp"] for c in range(NCORES)], axis=0)
    maps = []
    for c in range(NCORES):
        tc = _tok_cols(c)
        maps.append({
            "AT": np.ascontiguousarray(ATfull[:, tc]),
            "xTc": np.ascontiguousarray(xT[:, tc]),
            "xtok": np.ascontiguousarray(xT[:, tc].T),
            "wg": wg, "wo": wo, "gpre": gpre, "gpost": gpost, "sg": sg, "lamv": lamv, "consts": consts,
        })
    res = run_bass_kernel_spmd(_CACHE["C"], maps, core_ids=list(range(NCORES)))
    return res.results


def kernel(**inp):
    resA, xT = run_A(inp)
    resC = run_C(inp, resA, xT)
    y_p = np.concatenate([resC[c]["yc"][:2048] for c in range(NCORES)], axis=0).reshape(1, TP, D)
    y_s = np.concatenate([resC[c]["yc"][2048:] for c in range(NCORES)], axis=0).reshape(NSEQ, SQ, D)

    def heads(key, cores, width):
        a = np.stack([resA[c][key] for c in cores], axis=1)
        return a

    fk = heads("o_fk", range(8), 64)
    fv = heads("o_fv", range(8), 64)
    lf = np.stack([resA[c]["o_lf"].reshape(T) for c in range(8)], axis=1)
    dk = heads("o_dk", range(8), 64).reshape(T, 4, 128)
    dv = heads("o_dv", range(0, 8, 2), 128)
    f32 = np.float32
    outs = (y_p.astype(f32), y_s.astype(f32),
            fk[:TP].reshape(1, 1, TP, 8, 64), fv[:TP].reshape(1, 1, TP, 8, 64), lf[:TP].reshape(1, 1, TP, 8),
            dk[:TP].reshape(1, 1, TP, 4, 128), dv[:TP].reshape(1, 1, TP, 4, 128),
            fk[TP:].reshape(1, NSEQ, SQ, 8, 64), fv[TP:].reshape(1, NSEQ, SQ, 8, 64), lf[TP:].reshape(1, NSEQ, SQ, 8),
            dk[TP:].reshape(1, NSEQ, SQ, 4, 128), dv[TP:].reshape(1, NSEQ, SQ, 4, 128))
    return tuple(np.ascontiguousarray(o, dtype=f32) for o in outs)
```
